# Optimizing a Trainium2 kernel written in Bass

```python
import math
import jax, jax.numpy as jnp
from jax import lax
import numpy as np

D_MODEL = 1024
BATCH = 16
SEQ = 256
DEPTH = 1
DEC_BATCH = 4
DEC_SEQ = 2048
PAST_LEN = 256

GRID_W = 64
N_HEADS_A = 8
DK_A = 64
DV_A = 2 * DK_A
E_A = N_HEADS_A * DV_A
N_HEADS_R = 8
DK_R = 64
DV_R = 2 * DK_R
E_R = N_HEADS_R * DV_R
CHUNK = 128
Q_BLOCK = 128
ROPE_BASE = 10000.0
QA_W = N_HEADS_A * 2 * DK_A
KA_W = N_HEADS_A * 2 * DK_A
VA_W = E_A
ZA_W = E_A
QR_W = N_HEADS_R * DK_R
KR_W = N_HEADS_R * DK_R
VR_W = E_R
ZR_W = E_R
IN_W = QA_W + KA_W + VA_W + ZA_W + QR_W + KR_W + VR_W + ZR_W
IN_SPLITS = (QA_W, QA_W + KA_W, QA_W + KA_W + VA_W, QA_W + KA_W + VA_W + ZA_W,
             QA_W + KA_W + VA_W + ZA_W + QR_W, QA_W + KA_W + VA_W + ZA_W + QR_W + KR_W,
             QA_W + KA_W + VA_W + ZA_W + QR_W + KR_W + VR_W)
DEEPNORM_ALPHA = (2.0 * DEPTH) ** 0.25
DEEPNORM_BETA = (8.0 * DEPTH) ** -0.25
MOD_EPS = 1e-6
LN_EPS = 1e-5

kernel_name = "diff_retention_adaln_prefix_step"

F32 = jnp.float32


def _layernorm(x, eps):
    xf = x.astype(F32)
    mu = jnp.mean(xf, axis=-1, keepdims=True)
    var = jnp.mean(jnp.square(xf - mu), axis=-1, keepdims=True)
    return (xf - mu) * lax.rsqrt(var + eps)


def _axial_rope(n_tokens):
    rows = n_tokens // GRID_W
    r = jnp.broadcast_to(jnp.arange(rows, dtype=F32)[:, None], (rows, GRID_W)).reshape(-1)
    col = jnp.broadcast_to(jnp.arange(GRID_W, dtype=F32)[None, :], (rows, GRID_W)).reshape(-1)
    n_freq = DK_A // 4
    inv = ROPE_BASE ** (-jnp.arange(n_freq, dtype=F32) / n_freq)
    ang = jnp.concatenate([r[:, None] * inv, col[:, None] * inv], axis=-1)
    return jnp.cos(ang), jnp.sin(ang)


def _apply_rope(x, cos, sin):
    xf = x.astype(F32)
    x1 = xf[..., 0::2]
    x2 = xf[..., 1::2]
    c = cos[None, :, None, None, :]
    s = sin[None, :, None, None, :]
    out = jnp.stack([x1 * c - x2 * s, x1 * s + x2 * c], axis=-1).reshape(x.shape)
    return out.astype(x.dtype)


def _diff_attention(q, k, v, lam):
    b, nq = q.shape[0], q.shape[1]
    nb = nq // Q_BLOCK
    qb = jnp.moveaxis(q.reshape(b, nb, Q_BLOCK, N_HEADS_A, 2, DK_A), 1, 0)
    kf = k.astype(F32)
    vf = v.astype(F32)
    scale = DK_A ** -0.5

    def block(qblk):
        s = jnp.einsum('bqhmd,bkhmd->bmhqk', qblk.astype(F32), kf) * scale
        p = jax.nn.softmax(s, axis=-1)
        w = p[:, 0] - lam * p[:, 1]
        return jnp.einsum('bhqk,bkhd->bqhd', w, vf)

    o = lax.map(block, qb)
    return jnp.moveaxis(o, 0, 1).reshape(b, nq, N_HEADS_A, DV_A)


def _retention_scan(q, k, v, log_gamma, s0):
    b, n, h, dk = q.shape
    dv = v.shape[-1]
    nc = n // CHUNK
    idx = jnp.arange(CHUNK, dtype=F32)
    lg = log_gamma.astype(F32)
    rel = idx[:, None] - idx[None, :]
    decay_in = jnp.where(rel[None] >= 0, jnp.exp(lg[:, None, None] * jnp.maximum(rel, 0.0)[None]), 0.0)
    q_dec = jnp.exp(lg[:, None] * (idx + 1.0))[..., None]
    k_dec = jnp.exp(lg[:, None] * (CHUNK - 1.0 - idx))[..., None]
    chunk_dec = jnp.exp(lg * CHUNK)[:, None, None]

    def to_chunks(a):
        return a.astype(F32).reshape(b, nc, CHUNK, h, a.shape[-1]).transpose(1, 0, 3, 2, 4)

    def step(s, inp):
        qb, kb, vb = inp
        inner = jnp.einsum('bhik,bhjk->bhij', qb, kb) * decay_in
        o = jnp.einsum('bhij,bhjv->bhiv', inner, vb) + jnp.einsum('bhik,bhkv->bhiv', qb, s) * q_dec
        s = s * chunk_dec + jnp.einsum('bhjk,bhjv->bhkv', kb * k_dec, vb)
        return s, o

    s_fin, o = lax.scan(step, s0.astype(F32), (to_chunks(q), to_chunks(k), to_chunks(v)))
    o = o.transpose(1, 0, 3, 2, 4).reshape(b, n, h, dv)
    return o, s_fin


def _mixer_layer(x, cond, p, layer_idx, ctx):
    b, n, _ = x.shape
    mod = jax.nn.silu(cond.astype(F32)) @ p['w_mod'].astype(F32) + p['b_mod'].astype(F32)
    shift, scale, gate = jnp.split(mod[:, None, :], 3, axis=-1)
    h = (_layernorm(x, MOD_EPS) * (1.0 + scale) + shift).astype(x.dtype)

    proj = h @ p['w_in']
    qa, ka, va, za, qr, kr, vr, zr = jnp.split(proj, IN_SPLITS, axis=-1)
    qa = qa.reshape(b, n, N_HEADS_A, 2, DK_A)
    ka = ka.reshape(b, n, N_HEADS_A, 2, DK_A)
    va = va.reshape(b, n, N_HEADS_A, DV_A)
    qr = qr.reshape(b, n, N_HEADS_R, DK_R)
    kr = kr.reshape(b, n, N_HEADS_R, DK_R) * (DK_R ** -0.5)
    vr = vr.reshape(b, n, N_HEADS_R, DV_R)

    if ctx is None:
        k_all, v_all = ka, va
    else:
        cos, sin = _axial_rope(n)
        qa = _apply_rope(qa, cos, sin)
        ka_rot = _apply_rope(ka, cos, sin)
        ctx_k = ctx[0].reshape(b, ctx[0].shape[1], N_HEADS_A, 2, DK_A).astype(ka.dtype)
        k_all = jnp.concatenate([ctx_k, ka_rot], axis=1)
        v_all = jnp.concatenate([ctx[1].astype(va.dtype), va], axis=1)
    lam_init = 0.8 - 0.6 * math.exp(-0.3 * layer_idx)
    lp = p['lam'].astype(F32)
    lam = jnp.exp(jnp.sum(lp[0] * lp[1])) - jnp.exp(jnp.sum(lp[2] * lp[3])) + lam_init
    oa = _diff_attention(qa, k_all, v_all, lam)
    oa = oa * lax.rsqrt(jnp.mean(jnp.square(oa), axis=-1, keepdims=True) + LN_EPS)
    oa = oa * p['subln_g'].astype(F32) * (1.0 - lam_init)
    oa = (oa.reshape(b, n, E_A).astype(x.dtype) * jax.nn.silu(za)) @ p['w_pa']

    log_g = jnp.log1p(-jnp.exp2(p['ret_decay'].astype(F32)))
    if ctx is None:
        s0f = jnp.zeros((b, N_HEADS_R, DK_R, DV_R), F32)
        s0b = s0f
    else:
        s0f, s0b = ctx[2], ctx[3]
    of, sf = _retention_scan(qr, kr, vr, log_g[0], s0f)
    ob, sb = _retention_scan(qr[:, ::-1], kr[:, ::-1], vr[:, ::-1], log_g[1], s0b)
    orr = of + ob[:, ::-1]
    orr = _layernorm(orr, LN_EPS).reshape(b, n, E_R) * p['ret_gn_g'].astype(F32)
    orr = (orr.astype(x.dtype) * jax.nn.silu(zr)) @ p['w_pr']

    g = jax.nn.sigmoid((h @ p['w_gate'] + p['b_gate']).astype(F32))
    ga, gr = jnp.split(g, 2, axis=-1)
    m = (ga * oa.astype(F32) + gr * orr.astype(F32)).astype(x.dtype)
    out = (m @ p['w_out']).astype(F32)
    y = _layernorm(DEEPNORM_ALPHA * x.astype(F32) + gate * out, LN_EPS)
    y = (y * p['ln_g'].astype(F32) + p['ln_b'].astype(F32)).astype(x.dtype)
    if ctx is None:
        ctx_out = (ka.reshape(b, n, N_HEADS_A, 2 * DK_A), va, sf.astype(x.dtype), sb.astype(x.dtype))
        return y, ctx_out
    return y, None


def setup_inputs(seed: int = 0) -> dict:
    key = jax.random.key(seed)
    ks = jax.random.split(key, 24)

    def nrm(k, shape, s):
        return jax.random.normal(k, shape, F32) * s

    return {
        "x_prompt": nrm(ks[0], (BATCH, SEQ, D_MODEL), 1.0),
        "x_sample": nrm(ks[1], (DEC_BATCH, DEC_SEQ, D_MODEL), 1.0),
        "cache_attn_k": nrm(ks[2], (DEC_BATCH, DEPTH, PAST_LEN, N_HEADS_A, 2 * DK_A), 1.0),
        "cache_attn_v": nrm(ks[3], (DEC_BATCH, DEPTH, PAST_LEN, N_HEADS_A, DV_A), 1.0),
        "state_ret_fwd": nrm(ks[4], (DEC_BATCH, DEPTH, N_HEADS_R, DK_R, DV_R), 0.5),
        "state_ret_bwd": nrm(ks[5], (DEC_BATCH, DEPTH, N_HEADS_R, DK_R, DV_R), 0.5),
        "c": nrm(ks[6], (DEC_BATCH, D_MODEL), 1.0),
        "c_ctx": nrm(ks[7], (D_MODEL,), 1.0),
        "w_mod": nrm(ks[8], (DEPTH, D_MODEL, 3 * D_MODEL), 0.5 * D_MODEL ** -0.5),
        "b_mod": nrm(ks[9], (DEPTH, 3 * D_MODEL), 0.02),
        "w_in": nrm(ks[10], (DEPTH, D_MODEL, IN_W), D_MODEL ** -0.5),
        "lam_params": nrm(ks[11], (DEPTH, 4, DK_A), 0.1),
        "subln_g": 1.0 + nrm(ks[12], (DEPTH, DV_A), 0.02),
        "ret_decay": (-5.0 - jnp.arange(N_HEADS_R, dtype=F32))[None, None, :] + nrm(ks[13], (DEPTH, 2, N_HEADS_R), 0.1),
        "ret_gn_g": 1.0 + nrm(ks[14], (DEPTH, E_R), 0.02),
        "w_pa": nrm(ks[15], (DEPTH, E_A, D_MODEL), DEEPNORM_BETA * E_A ** -0.5),
        "w_pr": nrm(ks[16], (DEPTH, E_R, D_MODEL), DEEPNORM_BETA * E_R ** -0.5),
        "w_gate": nrm(ks[17], (DEPTH, D_MODEL, 2 * D_MODEL), D_MODEL ** -0.5),
        "b_gate": nrm(ks[18], (DEPTH, 2 * D_MODEL), 0.02),
        "w_out": nrm(ks[19], (DEPTH, D_MODEL, D_MODEL), DEEPNORM_BETA * D_MODEL ** -0.5),
        "ln_g": 1.0 + nrm(ks[20], (DEPTH, D_MODEL), 0.02),
        "ln_b": nrm(ks[21], (DEPTH, D_MODEL), 0.02),
    }


def reference(x_prompt, x_sample, cache_attn_k, cache_attn_v, state_ret_fwd, state_ret_bwd,
              c, c_ctx, w_mod, b_mod, w_in, lam_params, subln_g, ret_decay, ret_gn_g,
              w_pa, w_pr, w_gate, b_gate, w_out, ln_g, ln_b):
    y_prompt = x_prompt
    y_sample = x_sample
    cond_ctx = jnp.broadcast_to(c_ctx[None, :], (x_prompt.shape[0], D_MODEL))
    k_list, v_list, sf_list, sb_list = [], [], [], []
    for l in range(DEPTH):
        p = {"w_mod": w_mod[l], "b_mod": b_mod[l], "w_in": w_in[l], "lam": lam_params[l],
             "subln_g": subln_g[l], "ret_decay": ret_decay[l], "ret_gn_g": ret_gn_g[l],
             "w_pa": w_pa[l], "w_pr": w_pr[l], "w_gate": w_gate[l], "b_gate": b_gate[l],
             "w_out": w_out[l], "ln_g": ln_g[l], "ln_b": ln_b[l]}
        y_prompt, (k_l, v_l, sf_l, sb_l) = _mixer_layer(y_prompt, cond_ctx, p, l, None)
        k_list.append(k_l)
        v_list.append(v_l)
        sf_list.append(sf_l)
        sb_list.append(sb_l)
        ctx = (cache_attn_k[:, l], cache_attn_v[:, l], state_ret_fwd[:, l], state_ret_bwd[:, l])
        y_sample, _ = _mixer_layer(y_sample, c, p, l, ctx)
    new_attn_k = jnp.stack(k_list, axis=1)
    new_attn_v = jnp.stack(v_list, axis=1)
    new_ret_fwd = jnp.stack(sf_list, axis=1)
    new_ret_bwd = jnp.stack(sb_list, axis=1)
    return (y_prompt, y_sample, new_attn_k, new_attn_v, new_ret_fwd, new_ret_bwd)
```

```python
import math
from contextlib import ExitStack

import numpy as np
import concourse.bass as bass
import concourse.mybir as mybir
from concourse.bass_utils import run_bass_kernel_spmd

F32 = mybir.dt.float32
BF16 = mybir.dt.bfloat16
AF = mybir.ActivationFunctionType
ALU = mybir.AluOpType
AX = mybir.AxisListType

NCORES = 8
D = 1024
NP_TOK = 512
NM_TOK = 1024
NO_TOK = 1024
NTOK = NP_TOK + NM_TOK + NO_TOK
NQ_TOK = NP_TOK + NM_TOK
WA_COLS = 768
WR_COLS = 576
LAM_INIT = 0.8 - 0.6 * math.exp(-0.3 * 0)
ALPHA = (2.0 * 1) ** 0.25
MOD_EPS = 1e-6
LN_EPS = 1e-5


class _Sem:
    _n = 0

    def __init__(self, nc, name):
        _Sem._n += 1
        self.uid = _Sem._n
        self.h = nc.alloc_semaphore(f"{name}_{self.uid}")


class _Eng:
    def __init__(self, nc, name, handle):
        self.name = name
        self.h = handle
        self.sem = _Sem(nc, "e" + name)
        self.n = 0
        self.seen = {}


class _Buf:
    __slots__ = ("name", "w", "rs", "dsem", "dcnt", "excl")

    def __init__(self, name):
        self.name = name
        self.excl = False
        self.w = None
        self.rs = {}
        self.dsem = None
        self.dcnt = 0


class Tracker:
    def __init__(self, nc):
        self.nc = nc
        self.pe = _Eng(nc, "pe", nc.tensor)
        self.act = _Eng(nc, "act", nc.scalar)
        self.dve = _Eng(nc, "dve", nc.vector)
        self.pool = _Eng(nc, "pool", nc.gpsimd)
        self.sp = _Eng(nc, "sp", nc.sync)
        self.engs = [self.pe, self.act, self.dve, self.pool, self.sp]
        self.bufs = {}
        self.dsems = []

    def b(self, *key):
        bb = self.bufs.get(key)
        if bb is None:
            bb = _Buf(key)
            self.bufs[key] = bb
        return bb

    def _wait(self, eng, tok):
        if tok is None:
            return
        sem, val = tok
        if eng is self.pe and sem is self.pe.sem:
            return
        if eng.seen.get(sem.uid, 0) >= val:
            return
        eng.h.wait_ge(sem.h, val)
        eng.seen[sem.uid] = val

    def _deps(self, eng, reads, writes):
        for r in reads:
            self._wait(eng, r.w)
            if r.excl:
                for t in r.rs.values():
                    if t[0] is not eng.sem:
                        self._wait(eng, t)
        for w in writes:
            self._wait(eng, w.w)
            for t in w.rs.values():
                if t[0] is eng.sem and eng is not self.pool:
                    continue
                self._wait(eng, t)

    def _mark(self, tok, reads, writes):
        for w in writes:
            w.w = tok
            w.rs = {}
        for r in reads:
            r.rs[tok[0].uid] = tok

    def group(self, eng, fns, reads=(), writes=()):
        self._deps(eng, reads, writes)
        ins = None
        for f in fns:
            ins = f()
        eng.n += 1
        ins.then_inc(eng.sem.h, 1)
        self._mark((eng.sem, eng.n), reads, writes)

    def op(self, eng, fn, reads=(), writes=()):
        self.group(eng, [fn], reads, writes)

    def dma(self, eng, pairs, reads=(), writes=()):
        owner = (list(writes) + list(reads))[0]
        kind = "sw" if eng is self.pool else "hw"
        if owner.dsem is None:
            owner.dsem = {}
        if kind not in owner.dsem:
            owner.dsem[kind] = [_Sem(self.nc, "d" + kind), 0]
            self.dsems.append(owner.dsem[kind])
        ent = owner.dsem[kind]
        self._deps(eng, reads, writes)
        for (o, i) in pairs:
            eng.h.dma_start(out=o, in_=i).then_inc(ent[0].h, 16)
            ent[1] += 16
        self._mark((ent[0], ent[1]), reads, writes)

    def barrier(self, skip=()):
        skip_ids = set()
        for b_ in skip:
            for ent in (b_.dsem or {}).values():
                skip_ids.add(ent[0].uid)
        for e in self.engs:
            for f in self.engs:
                if f is not e and f.n > 0:
                    self._wait(e, (f.sem, f.n))
            for d in self.dsems:
                if d[0].uid not in skip_ids:
                    self._wait(e, (d[0], d[1]))

    def finish(self):
        for d in self.dsems:
            self._wait(self.sp, (d[0], d[1]))
        for f in self.engs:
            if f is not self.sp and f.n > 0:
                self._wait(self.sp, (f.sem, f.n))


class _StopBuild(Exception):
    pass


def build_program(STOP=99):
    import os as _os
    tag_stop = _os.environ.get("MK_TAG", "")
    holder = {}

    def chk(tag):
        if tag_stop and tag == tag_stop:
            raise _StopBuild()

    try:
        return _build_program(STOP, chk, holder)
    except _StopBuild:
        holder["T"].finish()
        return holder["nc"]


def _build_program(STOP, chk, holder):
    nc = bass.Bass("TRN2", target_bir_lowering=False)
    T = Tracker(nc)
    holder["T"] = T
    holder["nc"] = nc
    pe, act, dve, pool, sp = T.pe, T.act, T.dve, T.pool, T.sp

    def din(name, shape):
        return nc.dram_tensor(name, list(shape), F32, kind="ExternalInput").ap()

    def dout(name, shape):
        return nc.dram_tensor(name, list(shape), F32, kind="ExternalOutput").ap()

    xs = din("xs", [NTOK, D])
    ck = din("ck", [256, D])
    cv = din("cv", [256, D])
    s0_d = din("s0", [128, 8, 128])
    condT_d = din("condT", [128, 8, 2])
    wmod_d = din("w_mod", [D, 3 * D])
    bmod_d = din("bmod2", [2, 3 * D])
    wa_d = din("wa", [8, D, WA_COLS])
    wr_d = din("wr", [8, D, WR_COLS])
    wg_d = din("wg", [8, D, 512])
    wout_d = din("w_out", [D, D])
    bgT_d = din("bgT", [128, 16])
    lnrow_d = din("lnrow", [128, 2, D])
    lamp_d = din("lamp", [128, 256])
    sublnT_d = din("sublnT", [128, 1])
    gngT_d = din("gngT", [128, 8])
    rdsel_d = din("rdsel", [128, 8])
    rdrow_d = din("rdrow", [128, 16])
    ident_d = din("ident", [128, 128])
    cmat_d = din("cmat", [128, 128])
    iq_d = din("iq", [128, 128])
    jp_d = din("jp", [128, 2])
    diff_d = din("diff", [128, 128])
    mx8_d = din("mx8", [128, 128])
    my8_d = din("my8", [128, 128])
    ropeC_d = din("ropeC", [128, 2048])
    ropeS_d = din("ropeS", [128, 2048])
    sel2_d = din("sel2", [2, 2, 128])

    y_d = dout("y", [NQ_TOK, D])
    nk_d = dout("nk", [NP_TOK, D])
    nv_d = dout("nv", [NP_TOK, D])
    nst_d = dout("nst", [2, 8, 128, 128])

    es = ExitStack()
    KB = 1024
    BASE = 16640
    TOP = 228992
    BIG = BASE + 30 * KB
    _esz = {F32: 4, BF16: 2}

    class Arena:
        def __init__(self, start, end):
            self.start, self.end, self.cur = start, end, start

        def alloc(self, name, shape, dt):
            n = 1
            for d_ in shape[1:]:
                n *= d_
            nbytes = (n * _esz[dt] + 63) // 64 * 64
            off = self.cur
            assert off + nbytes <= self.end, (name, off, nbytes, self.end)
            self.cur = off + nbytes
            Arena.uid += 1
            return nc.alloc_sbuf_tensor_at(f"s_{name}_{Arena.uid}", list(shape), dt, offset=off).ap()

    Arena.uid = 0
    A_const = Arena(BASE, BIG)
    A_pers = Arena(BIG, BIG + 112 * KB)

    def sb(name, shape, dt=F32, arena=None):
        return (arena or A_const).alloc(name, shape, dt)

    PS = es.enter_context(nc.psum_tensor("PS", [128, 8, 512], F32)).ap()
    psb = [T.b("ps", k) for k in range(8)]
    for _b in psb:
        _b.excl = True
    bank_rr = [0]

    def next_bank():
        k = bank_rr[0]
        bank_rr[0] = (k + 1) % 8
        return k

    hT = sb("hT", [128, 8, NTOK], BF16, A_pers)
    oagT = sb("oagT", [128, 8, NQ_TOK], BF16, A_pers)
    orgT = sb("orgT", [128, 8, NQ_TOK], BF16, A_pers)
    ident = sb("ident", [128, 128])
    cmat_f = sb("cmat_f", [128, 128])
    cmat = sb("cmat", [128, 128], BF16)
    jmat = sb("jmat", [128, 128], BF16)
    ones_b = sb("ones_b", [128, 128], BF16)
    neghalf = sb("neghalf", [128, 512])
    one_col = sb("one_col", [128, 1])
    QD = sb("QD", [128, 8, 128])
    kdcol = sb("kdcol", [128, 16])
    DC = sb("DC", [128, 8, 128])
    CDv = sb("CDv", [128, 8])
    s0 = sb("s0", [128, 8, 128])
    shiftT = sb("shiftT", [128, 8, 2])
    scale1T = sb("scale1T", [128, 8, 2])
    gate_bc = sb("gate_bc", [128, 2, D])
    bgT = sb("bgT", [128, 16])
    bgTh = sb("bgTh", [128, 16])
    neglam = sb("neglam", [128, 1])
    gsc = sb("gsc", [128, 1])
    gngT = sb("gngT", [128, 8])
    gngTh = sb("gngTh", [128, 8])
    sublnT = sb("sublnT", [128, 1])

    B_const = T.b("const")
    B_hT = [T.b("hT", st) for st in range(5)]

    if True:
        ph = Arena(TOP - 32 * KB, TOP)
        condT = sb("condT", [128, 8, 2], F32, ph)
        silT = sb("silT", [128, 8, 2], BF16, ph)
        ctmp = sb("ctmp", [128, 8, 2], F32, ph)
        lamp = sb("lamp", [128, 256], F32, ph)
        ltmp = sb("ltmp", [128, 128], F32, ph)
        lsum = sb("lsum", [128, 2], F32, ph)
        rdsel = sb("rdsel", [128, 8], F32)
        rdrow = sb("rdrow", [128, 16], F32)
        lgsel = sb("lgsel", [128, 8], F32)
        lgrow = sb("lgrow", [128, 16], F32)
        nlgrow = sb("nlgrow", [128, 16], F32)
        lg128 = sb("lg128", [128, 8], F32, ph)
        iq = sb("iq", [128, 128], F32)
        jp = sb("jp", [128, 2], F32)
        diff = sb("diff", [128, 128], F32)
        mx8 = sb("mx8", [128, 1, 128], F32)
        my8 = sb("my8", [128, 1, 128], F32)
        sel2 = sb("sel2", [2, 2, 128], F32, ph)
        modrow = sb("modrow", [2, 3 * D], F32, ph)
        wm = [sb(f"wm{i}", [128, 8, 512], BF16, ph) for i in range(2)]
        B_wm = [T.b("wm", i) for i in range(2)]

        T.dma(sp, [(ident, ident_d), (cmat_f, cmat_d), (s0, s0_d), (condT, condT_d), (bgT, bgT_d),
                   (lamp, lamp_d), (sublnT, sublnT_d), (gngT, gngT_d),
                   (rdsel, rdsel_d), (rdrow, rdrow_d), (iq, iq_d), (jp, jp_d), (diff, diff_d),
                   (mx8, mx8_d.rearrange("p (o i) -> p o i", o=1)), (my8, my8_d.rearrange("p (o i) -> p o i", o=1)), (sel2, sel2_d), (modrow, bmod_d)],
              writes=[B_const])
        B_c2 = T.b("const2")
        T.op(pool, lambda: nc.gpsimd.memset(jmat, 1.0 / 128.0), writes=[B_c2])
        T.op(pool, lambda: nc.gpsimd.memset(ones_b, 1.0), writes=[B_c2])
        T.op(pool, lambda: nc.gpsimd.memset(neghalf, -0.5), writes=[B_c2])
        T.op(pool, lambda: nc.gpsimd.memset(one_col, 1.0), writes=[B_c2])
        T.op(dve, lambda: nc.vector.tensor_copy(out=cmat, in_=cmat_f), reads=[B_const], writes=[B_c2])

        chk('a')
        B_sil = T.b("sil")
        T.op(act, lambda: nc.scalar.activation(out=ctmp, in_=condT, func=AF.Tanh, scale=0.5),
             reads=[B_const], writes=[B_sil])
        T.op(dve, lambda: nc.vector.scalar_tensor_tensor(out=ctmp, in0=ctmp, scalar=1.0, in1=condT,
                                                         op0=ALU.add, op1=ALU.mult),
             reads=[B_sil, B_const], writes=[B_sil])
        T.op(dve, lambda: nc.vector.tensor_scalar(out=silT, in0=ctmp, scalar1=0.5, scalar2=None, op0=ALU.mult),
             reads=[B_sil], writes=[B_sil])

        chk('b')
        B_mod = T.b("modrow")
        wmod_v = wmod_d.rearrange("(kt p) c -> p kt c", p=128)
        for blk in range(6):
            slot = blk % 2
            T.dma(pool, [(wm[slot], wmod_v[:, :, blk * 512:(blk + 1) * 512])], writes=[B_wm[slot]])
            k = next_bank()
            T.group(pe, [(lambda kt=kt, k=k, slot=slot: nc.tensor.matmul(
                PS[0:2, k, :], lhsT=silT[:, kt, :], rhs=wm[slot][:, kt, :], start=(kt == 0), stop=(kt == 7)))
                for kt in range(8)], reads=[B_sil, B_wm[slot]], writes=[psb[k]])
            T.op(dve, lambda k=k, blk=blk: nc.vector.tensor_tensor(
                out=modrow[:, blk * 512:(blk + 1) * 512], in0=PS[0:2, k, :],
                in1=modrow[:, blk * 512:(blk + 1) * 512], op=ALU.add),
                reads=[psb[k], B_const], writes=[B_mod])
        chk('c')
        k = next_bank()
        T.group(pe, [(lambda j=j, k=k: nc.tensor.transpose(
            PS[:, k, 2 * j:2 * j + 2], modrow[0:2, j * 128:(j + 1) * 128], ident[0:2, 0:2]))
            for j in range(16)], reads=[B_mod, B_const], writes=[psb[k]])
        B_modT = T.b("modT")
        T.op(dve, lambda k=k: nc.vector.tensor_copy(
            out=shiftT, in_=PS[:, k, 0:16].rearrange("p (j c) -> p j c", c=2)),
            reads=[psb[k]], writes=[B_modT])
        T.op(dve, lambda k=k: nc.vector.tensor_scalar(
            out=scale1T, in0=PS[:, k, 16:32].rearrange("p (j c) -> p j c", c=2), scalar1=1.0, scalar2=None,
            op0=ALU.add), reads=[psb[k]], writes=[B_modT])
        chk('d')
        B_gate = T.b("gate_bc")
        for ci in range(2):
            for hf in range(2):
                k = next_bank()
                T.op(pe, lambda k=k, ci=ci, hf=hf: nc.tensor.matmul(
                    PS[:, k, :], lhsT=sel2[:, ci, :], rhs=modrow[0:2, 2048 + hf * 512:2048 + (hf + 1) * 512],
                    start=True, stop=True), reads=[B_mod, B_const], writes=[psb[k]])
                T.op(act, lambda k=k, ci=ci, hf=hf: nc.scalar.mul(
                    out=gate_bc[:, ci, hf * 512:(hf + 1) * 512], in_=PS[:, k, :], mul=0.5),
                    reads=[psb[k]], writes=[B_gate])

        chk('e')
        B_lam = T.b("lam")
        T.op(dve, lambda: nc.vector.tensor_tensor(
            out=ltmp.rearrange("p (a b) -> p a b", a=2),
            in0=lamp.rearrange("p (a r b) -> p a r b", a=2, r=2)[:, :, 0, :],
            in1=lamp.rearrange("p (a r b) -> p a r b", a=2, r=2)[:, :, 1, :], op=ALU.mult),
            reads=[B_const], writes=[B_lam])
        T.op(dve, lambda: nc.vector.reduce_sum(out=lsum, in_=ltmp.rearrange("p (a b) -> p a b", a=2), axis=AX.X),
             reads=[B_lam], writes=[B_lam])
        T.op(act, lambda: nc.scalar.activation(out=lsum, in_=lsum, func=AF.Exp), reads=[B_lam], writes=[B_lam])
        T.op(dve, lambda: nc.vector.tensor_tensor(out=neglam, in0=lsum[:, 1:2], in1=lsum[:, 0:1], op=ALU.subtract),
             reads=[B_lam], writes=[B_lam])
        T.op(dve, lambda: nc.vector.tensor_scalar(out=neglam, in0=neglam, scalar1=-LAM_INIT, scalar2=None,
                                                  op0=ALU.add), reads=[B_lam], writes=[B_lam])
        T.op(dve, lambda: nc.vector.tensor_scalar(out=gsc, in0=sublnT, scalar1=(1.0 - LAM_INIT),
                                                  scalar2=None, op0=ALU.mult), reads=[B_const], writes=[B_lam])
        T.op(dve, lambda: nc.vector.tensor_scalar(out=gngTh, in0=gngT, scalar1=0.5, scalar2=None, op0=ALU.mult),
             reads=[B_const], writes=[B_lam])
        T.op(dve, lambda: nc.vector.tensor_scalar(out=bgTh, in0=bgT, scalar1=0.5, scalar2=None, op0=ALU.mult),
             reads=[B_const], writes=[B_lam])

        B_dec = T.b("dec")
        ph0 = ph

        def emit_decay_tables():
            chk('f')
            B_dec = T.b("dec")

            def log1p_neg(dst, src, n):
                T.op(act, lambda: nc.scalar.activation(out=src, in_=src, func=AF.Exp, scale=math.log(2.0)),
                     reads=[B_const, B_dec], writes=[B_dec])
                T.op(dve, lambda: nc.vector.tensor_scalar(out=dst, in0=src, scalar1=1.0 / 8.0, scalar2=None,
                                                          op0=ALU.mult), reads=[B_dec], writes=[B_dec])
                for c in (1.0 / 7.0, 1.0 / 6.0, 1.0 / 5.0, 0.25, 1.0 / 3.0, 0.5, 1.0):
                    T.op(dve, lambda c=c: nc.vector.scalar_tensor_tensor(
                        out=dst, in0=dst, scalar=c, in1=src, op0=ALU.add, op1=ALU.mult),
                        reads=[B_dec], writes=[B_dec])
                T.op(dve, lambda: nc.vector.tensor_scalar(out=dst, in0=dst, scalar1=-1.0, scalar2=None, op0=ALU.mult),
                     reads=[B_dec], writes=[B_dec])

            chk('g')
            log1p_neg(lgsel, rdsel, 8)
            log1p_neg(lgrow, rdrow, 16)
            T.op(dve, lambda: nc.vector.tensor_scalar(out=nlgrow, in0=lgrow, scalar1=-1.0, scalar2=None, op0=ALU.mult),
                 reads=[B_dec], writes=[B_dec])
            T.op(act, lambda: nc.scalar.activation(out=CDv, in_=lgsel, func=AF.Exp, scale=128.0),
                 reads=[B_dec], writes=[B_dec])
            chk('h')
            ln8 = math.log(0.125)
            lnb = sb("lnb", [128, 1], F32)
            T.op(pool, lambda: nc.gpsimd.memset(lnb, ln8), writes=[B_dec])
            T.op(dve, lambda: nc.vector.tensor_scalar(out=kdcol[:, 0:8], in0=lgrow[:, 0:8], scalar1=jp[:, 0:1], scalar2=None,
                                                      op0=ALU.mult), reads=[B_dec, B_const], writes=[B_dec])
            T.op(dve, lambda: nc.vector.tensor_scalar(out=kdcol[:, 8:16], in0=lgrow[:, 8:16], scalar1=jp[:, 1:2],
                                                      scalar2=None, op0=ALU.mult), reads=[B_dec, B_const], writes=[B_dec])
            T.op(act, lambda: nc.scalar.activation(out=kdcol, in_=kdcol, func=AF.Exp, bias=lnb),
                 reads=[B_dec], writes=[B_dec])
            chk('i')
            for h in range(8):
                T.op(act, lambda h=h: nc.scalar.activation(out=DC[:, h, :], in_=diff, func=AF.Exp,
                                                           scale=lgrow[:, h:h + 1]),
                     reads=[B_dec, B_const], writes=[B_dec])
                T.op(act, lambda h=h: nc.scalar.activation(out=QD[:, h, :], in_=diff, func=AF.Exp,
                                                           scale=nlgrow[:, 8 + h:9 + h]),
                     reads=[B_dec, B_const], writes=[B_dec])
            T.op(dve, lambda: nc.vector.tensor_tensor(out=DC, in0=DC, in1=mx8.to_broadcast([128, 8, 128]),
                                                      op=ALU.mult), reads=[B_dec, B_const], writes=[B_dec])
            T.op(dve, lambda: nc.vector.tensor_tensor(out=QD, in0=QD, in1=my8.to_broadcast([128, 8, 128]),
                                                      op=ALU.mult), reads=[B_dec, B_const], writes=[B_dec])
            T.op(dve, lambda: nc.vector.tensor_tensor(out=DC, in0=DC, in1=QD, op=ALU.add),
                 reads=[B_dec], writes=[B_dec])
            for h in range(8):
                T.op(act, lambda h=h: nc.scalar.activation(out=QD[:, h, :], in_=iq, func=AF.Exp,
                                                           scale=lgsel[:, h:h + 1]),
                     reads=[B_dec, B_const], writes=[B_dec])

        if STOP == 0:
            T.finish()
            return nc

    if True:
        ph2 = Arena(BIG + 64 * KB, TOP - 65 * KB)
        ropeC = sb("ropeC", [128, 2048], F32, ph2)
        ropeS = sb("ropeS", [128, 2048], F32, ph2)
        ckT = sb("ckT", [128, 8, 256], BF16, ph2)
        cvb = sb("cvb", [128, 2, D], BF16, ph2)
        WA0 = sb("WA0", [128, 8, WA_COLS], BF16, ph2)
        _wa1_off = ph2.cur
        cst = sb("cst", [128, D], F32, ph2)
        ph2.cur = _wa1_off
        WA1 = sb("WA1", [128, 8, WA_COLS], BF16, ph2)
        WA = [WA0, WA1]
        B_rope = T.b("rope")
        B_WA = [T.b("WA", i) for i in range(2)]
        B_ctx = T.b("ctx")

        def load_WA(h):
            v = wa_d[h].rearrange("(kt p) c -> p kt c", p=128)
            T.dma(pool, [(WA[h % 2], v)], writes=[B_WA[h % 2]])

        load_WA(0)

    if True:
        ph = Arena(TOP - 65 * KB, TOP - 32 * KB)
        NB1 = 4
        XT = [sb(f"XT{i}", [128, D], F32, ph) for i in range(NB1)]
        XN = [sb(f"XN{i}", [128, D], F32, ph) for i in range(NB1)]
        st6 = [sb(f"st6_{i}", [128, 2, 6], F32, ph) for i in range(NB1)]
        mv = [sb(f"mv{i}", [128, 2], F32, ph) for i in range(NB1)]
        rs = [sb(f"rs{i}", [128, 1], F32, ph) for i in range(NB1)]
        B_xt = [T.b("XT", i) for i in range(NB1)]
        B_xn = [T.b("XN", i) for i in range(NB1)]
        B_stt = [T.b("stt", i) for i in range(NB1)]

        def p1_A(g):
            s = g % NB1
            T.dma(sp, [(XT[s], xs[g * 128:(g + 1) * 128, :])], writes=[B_xt[s]])
            for c2 in range(2):
                T.op(dve, lambda c2=c2: nc.vector.bn_stats(out=st6[s][:, c2, :],
                                                           in_=XT[s][:, c2 * 512:(c2 + 1) * 512]),
                     reads=[B_xt[s]], writes=[B_stt[s]])
            T.op(dve, lambda: nc.vector.bn_aggr(out=mv[s], in_=st6[s]), reads=[B_stt[s]], writes=[B_stt[s]])
            T.op(dve, lambda: nc.vector.tensor_scalar(out=rs[s], in0=mv[s][:, 1:2], scalar1=MOD_EPS,
                                                      scalar2=None, op0=ALU.add),
                 reads=[B_stt[s]], writes=[B_stt[s]])
            T.op(act, lambda: nc.scalar.activation(out=rs[s], in_=rs[s], func=AF.Sqrt),
                 reads=[B_stt[s]], writes=[B_stt[s]])

        def p1_B(g):
            s = g % NB1
            T.op(dve, lambda: nc.vector.reciprocal(out=rs[s], in_=rs[s]), reads=[B_stt[s]], writes=[B_stt[s]])
            T.op(dve, lambda: nc.vector.scalar_tensor_tensor(
                out=mv[s][:, 1:2], in0=mv[s][:, 0:1], scalar=-1.0, in1=rs[s], op0=ALU.mult, op1=ALU.mult),
                reads=[B_stt[s]], writes=[B_stt[s]])
            T.op(act, lambda: nc.scalar.activation(out=XN[s], in_=XT[s], func=AF.Identity,
                                                   scale=rs[s], bias=mv[s][:, 1:2]),
                 reads=[B_xt[s], B_stt[s]], writes=[B_xn[s]])

        def p1_C(g):
            s = g % NB1
            st, tt = g // 4, g % 4
            ci = 0 if st == 0 else 1
            for kt in range(8):
                T.op(pe, lambda kt=kt: nc.tensor.transpose(
                    PS[:, kt, tt * 128:(tt + 1) * 128], XN[s][:, kt * 128:(kt + 1) * 128], ident),
                    reads=[B_xn[s], B_const], writes=[psb[kt]])
            if tt == 3:
                for kt in range(8):
                    if kt % 2 == 0:
                        T.op(dve, lambda kt=kt: nc.vector.tensor_scalar(
                            out=hT[:, kt, st * 512:(st + 1) * 512], in0=PS[:, kt, :],
                            scalar1=scale1T[:, kt, ci:ci + 1], scalar2=shiftT[:, kt, ci:ci + 1],
                            op0=ALU.mult, op1=ALU.add), reads=[psb[kt], B_modT], writes=[B_hT[st]])
                    else:
                        T.op(act, lambda kt=kt: nc.scalar.activation(
                            out=hT[:, kt, st * 512:(st + 1) * 512], in_=PS[:, kt, :], func=AF.Identity,
                            scale=scale1T[:, kt, ci:ci + 1], bias=shiftT[:, kt, ci:ci + 1]),
                            reads=[psb[kt], B_modT], writes=[B_hT[st]])

        for i_ in range(20 + 2):
            if i_ < 20:
                p1_A(i_)
            if 0 <= i_ - 1 < 20:
                p1_B(i_ - 1)
            if 0 <= i_ - 2 < 20:
                p1_C(i_ - 2)
        T.dma(sp, [(ropeC, ropeC_d), (ropeS, ropeS_d)], writes=[B_rope])
        for t in range(2):
            T.dma(sp, [(cst, ck[t * 128:(t + 1) * 128, :])], writes=[B_WA[1]])
            for hq in range(2):
                k = next_bank()
                T.group(pe, [(lambda h=h, k=k: nc.tensor.transpose(
                    PS[:, k, (h % 4) * 128:(h % 4 + 1) * 128], cst[:, h * 128:(h + 1) * 128], ident))
                    for h in range(hq * 4, hq * 4 + 4)], reads=[B_WA[1], B_const], writes=[psb[k]])
                T.op(act, lambda k=k, hq=hq, t=t: nc.scalar.activation(
                    out=ckT[:, hq * 4:(hq + 1) * 4, t * 128:(t + 1) * 128],
                    in_=PS[:, k, :].rearrange("p (h j) -> p h j", h=4), func=AF.Copy),
                    reads=[psb[k]], writes=[B_ctx])
        for t in range(2):
            T.dma(sp, [(cst, cv[t * 128:(t + 1) * 128, :])], writes=[B_WA[1]])
            T.op(dve, lambda t=t: nc.vector.tensor_copy(out=cvb[:, t, :], in_=cst), reads=[B_WA[1]], writes=[B_ctx])
        load_WA(1)

        T.barrier(skip=[B_rope, B_WA[1]])
        if STOP == 1:
            T.finish()
            return nc

    if True:
        ph = ph2
        ph.end = TOP
        qT = [sb(f"qT{i}", [128, NQ_TOK], BF16, ph) for i in range(2)]
        kT = [sb(f"kT{i}", [128, NTOK], BF16, ph) for i in range(2)]
        vS = [sb(f"vS{i}", [128, 20, 128], BF16, ph) for i in range(2)]
        tza = [sb(f"tza{i}", [128, NQ_TOK], BF16, ph) for i in range(2)]
        E = [sb(f"E{i}", [128, 2, 256], BF16, ph) for i in range(3)]
        r1 = [sb(f"r1_{i}", [128, 512], F32, ph) for i in range(2)]
        r2 = [sb(f"r2_{i}", [128, 512], F32, ph) for i in range(2)]
        tmpz = [sb(f"tmpz{i}", [128, 512], F32, ph) for i in range(2)]
        stage = [sb(f"stage{i}", [128, 256], F32, ph) for i in range(2)]
        dn = sb("dn", [128, 2, 256], F32, ph)
        o2 = sb("o2", [128, 256], F32, ph)
        osq_ = [sb(f"osq{i}", [128, 256], BF16, ph) for i in range(2)]
        dd_ = [sb(f"dd{i}", [128, 256], F32, ph) for i in range(2)]
        o_all_ = [sb(f"o_all{i}", [128, 512], F32, ph) for i in range(2)]
        v_all_ = [sb(f"v_all{i}", [128, 512], F32, ph) for i in range(2)]

        B_qT = [[T.b("qT", i, g) for g in range(3)] for i in range(2)]
        B_kT = [[T.b("kT", i, g) for g in range(5)] for i in range(2)]
        B_vS = [[T.b("vS", i, g) for g in range(5)] for i in range(2)]
        B_tza = [[T.b("tza", i, g) for g in range(3)] for i in range(2)]
        B_E = [T.b("E", i) for i in range(3)]
        B_r1 = [T.b("r1", i) for i in range(2)]
        B_r2 = [T.b("r2", i) for i in range(2)]
        B_tz = [T.b("tmpz", i) for i in range(2)]
        B_stage = [T.b("stage", i) for i in range(2)]
        B_dn = T.b("dn")
        B_o2 = T.b("o2")
        B_osq_ = [T.b("osq", i) for i in range(2)]
        B_dd_ = [T.b("dd", i) for i in range(2)]
        B_oall_ = [[T.b("o_all", bp, i) for i in range(2)] for bp in range(2)]
        B_vall_ = [[T.b("v_all", bp, i) for i in range(2)] for bp in range(2)]
        gcount = [0]
        bcount = [0]
        B_oag = [[T.b("oag", h, g) for g in range(3)] for h in range(8)]

        chk('p2a')
        rr = [0]

        def rot2():
            rr[0] ^= 1
            return rr[0]

        pbank = [0]
        PB = [5, 6, 7]

        def pool_bank():
            pbank[0] = (pbank[0] + 1) % 3
            return PB[pbank[0]]

        Q2 = [sb(f"Q2_{i}", [128, 512], BF16, ph) for i in range(2)]
        B_Q2 = [T.b("Q2", i) for i in range(2)]
        for q_ in range(2):
            T.op(pool, lambda q_=q_: nc.gpsimd.memset(Q2[q_], 0.0), writes=[B_Q2[q_]])
        q2c = [0]
        q2carry = {}

        def build_Q2(i, qc0, g):
            q2c[0] ^= 1
            u = q2c[0]
            T.op(pool, lambda: nc.gpsimd.tensor_copy(out=Q2[u][0:64, 0:256], in_=qT[i][0:64, qc0:qc0 + 256]),
                 reads=[B_qT[i][g]], writes=[B_Q2[u]])
            T.op(pool, lambda: nc.gpsimd.tensor_copy(out=Q2[u][64:128, 256:512], in_=qT[i][64:128, qc0:qc0 + 256]),
                 reads=[B_qT[i][g]], writes=[B_Q2[u]])
            return u

        def mk_chunks(h):
            i = h % 2
            W = WA[i]
            ch = []

            def fgroup(c0, st, ncols=128):
                k = pool_bank()
                T.group(pe, [(lambda kt=kt, k=k: nc.tensor.matmul(
                    PS[0:ncols, k, :], lhsT=W[:, kt, c0:c0 + ncols], rhs=hT[:, kt, st * 512:(st + 1) * 512],
                    start=(kt == 0), stop=(kt == 7))) for kt in range(8)],
                    reads=[B_WA[i], B_hT[st]], writes=[psb[k]])
                return k

            def plain(c0, dst, dbuf):
                k = fgroup(c0, 0)
                T.op(dve, lambda k=k: nc.vector.tensor_copy(out=dst, in_=PS[:, k, :]),
                     reads=[psb[k]], writes=[dbuf])

            def rope_evac(c_plain, c_sw, st, dst, dbuf):
                ka_ = fgroup(c_plain, st)
                kb_ = fgroup(c_sw, st)
                j = rot2()
                rc = (st - 1) * 512
                T.op(dve, lambda: nc.vector.tensor_tensor(out=r1[j], in0=PS[:, ka_, :],
                                                          in1=ropeC[:, rc:rc + 512], op=ALU.mult),
                     reads=[psb[ka_], B_rope], writes=[B_r1[j]])
                T.op(dve, lambda: nc.vector.tensor_tensor(out=r2[j], in0=PS[:, kb_, :],
                                                          in1=ropeS[:, rc:rc + 512], op=ALU.mult),
                     reads=[psb[kb_], B_rope], writes=[B_r2[j]])
                T.op(pool, lambda: nc.gpsimd.tensor_tensor(out=dst, in0=r1[j], in1=r2[j], op=ALU.add),
                     reads=[B_r1[j], B_r2[j]], writes=[dbuf])

            def zgate(st):
                k = fgroup(384, st)
                j = rot2()
                T.op(act, lambda: nc.scalar.activation(out=tmpz[j], in_=PS[:, k, :], func=AF.Exp, scale=-1.0),
                     reads=[psb[k]], writes=[B_tz[j]])
                T.op(dve, lambda: nc.vector.tensor_copy(out=r2[j], in_=PS[:, k, :]),
                     reads=[psb[k]], writes=[B_r2[j]])
                T.op(act, lambda: nc.scalar.activation(out=tmpz[j], in_=tmpz[j], func=AF.Ln, bias=one_col),
                     reads=[B_tz[j], B_c2], writes=[B_tz[j]])
                T.op(act, lambda: nc.scalar.activation(out=tmpz[j], in_=tmpz[j], func=AF.Exp, scale=-1.0),
                     reads=[B_tz[j]], writes=[B_tz[j]])
                T.op(pool, lambda: nc.gpsimd.tensor_tensor(
                    out=tza[i][:, st * 512:(st + 1) * 512], in0=r2[j], in1=tmpz[j], op=ALU.mult),
                    reads=[B_tz[j], B_r2[j]], writes=[B_tza[i][st]])

            def ptile(tt):
                k = pool_bank()
                T.group(pe, [(lambda kt=kt: nc.tensor.matmul(
                    PS[:, k, 0:256], lhsT=hT[:, kt, tt * 128:(tt + 1) * 128], rhs=W[:, kt, 512:768],
                    start=(kt == 0), stop=(kt == 7))) for kt in range(8)],
                    reads=[B_WA[i], B_hT[0]], writes=[psb[k]])
                j = rot2()
                T.op(dve, lambda: nc.vector.tensor_copy(out=stage[j], in_=PS[:, k, 0:256]),
                     reads=[psb[k]], writes=[B_stage[j]])
                T.op(dve, lambda: nc.vector.tensor_copy(out=vS[i][:, tt, :], in_=PS[:, k, 128:256]),
                     reads=[psb[k]], writes=[B_vS[i][0]])
                T.dma(sp, [(nk_d[tt * 128:(tt + 1) * 128, h * 128:(h + 1) * 128], stage[j][:, 0:128]),
                           (nv_d[tt * 128:(tt + 1) * 128, h * 128:(h + 1) * 128], stage[j][:, 128:256])],
                      reads=[B_stage[j]])

            def vtiles(st):
                k = pool_bank()
                fns = []
                for t4 in range(4):
                    tt = st * 4 + t4
                    for kt in range(8):
                        fns.append(lambda kt=kt, tt=tt, t4=t4: nc.tensor.matmul(
                            PS[:, k, t4 * 128:(t4 + 1) * 128], lhsT=hT[:, kt, tt * 128:(tt + 1) * 128],
                            rhs=W[:, kt, 640:768], start=(kt == 0), stop=(kt == 7)))
                T.group(pe, fns, reads=[B_WA[i], B_hT[st]], writes=[psb[k]])
                T.op(dve, lambda: nc.vector.tensor_copy(
                    out=vS[i][:, st * 4:(st + 1) * 4, :], in_=PS[:, k, :].rearrange("p (t d) -> p t d", t=4)),
                    reads=[psb[k]], writes=[B_vS[i][st]])

            def seq(*fs):
                return lambda: [f() for f in fs]

            qrope = lambda st: (lambda: rope_evac(0, 128, st, qT[i][:, st * 512:(st + 1) * 512], B_qT[i][st]))
            krope = lambda st: (lambda: rope_evac(512, 256, st, kT[i][:, st * 512:(st + 1) * 512], B_kT[i][st]))
            ch.append(seq(lambda: plain(0, qT[i][:, 0:512], B_qT[i][0]), lambda: plain(512, kT[i][:, 0:512], B_kT[i][0])))
            ch.append(seq(lambda: ptile(0), lambda: ptile(1)))
            ch.append(seq(lambda: ptile(2), lambda: ptile(3)))
            ch.append(seq(lambda: zgate(0), qrope(1)))
            ch.append(qrope(2))
            ch.append(seq(lambda: zgate(1), krope(1)))
            ch.append(krope(2))
            ch.append(seq(lambda: zgate(2), krope(3)))
            ch.append(krope(4))
            ch.append(seq(lambda: vtiles(1), lambda: vtiles(2)))
            ch.append(seq(lambda: vtiles(3), lambda: vtiles(4)))
            return ch

        deferred = []
        act_heavy = []
        stepc = [0]

        def run_deferred(force=False):
            while deferred and (force or deferred[0][0] <= stepc[0]):
                deferred.pop(0)[1]()

        def attention_head(h, pend):
            i = h % 2
            groups = []
            for pb in range(2):
                keys = [(kT[i][:, pb * 256 + t * 128:pb * 256 + (t + 1) * 128], vS[i][:, pb * 2 + t, :],
                         (B_kT[i][0], B_vS[i][0])) for t in range(2)]
                groups.append((pb * 256, keys, 0, pb))
            skeys = [(ckT[:, h, t * 128:(t + 1) * 128], cvb[:, t, h * 128:(h + 1) * 128], (B_ctx,))
                     for t in range(2)]
            for t in range(16):
                st = 1 + t // 4
                skeys.append((kT[i][:, 512 + t * 128:512 + (t + 1) * 128], vS[i][:, 4 + t, :],
                              (B_kT[i][st], B_vS[i][st])))
            for qt in range(4):
                groups.append((512 + qt * 256, skeys, 1 + qt // 2, qt % 2))
            nsteps = sum(len(g_[1]) for g_ in groups)
            done = 0
            emitted = 0
            nq = 256
            nextq = q2carry.pop('q') if 'q' in q2carry else build_Q2(i, groups[0][0], groups[0][2])
            for gix, (qc0, keys, g, half) in enumerate(groups):
                n = len(keys)
                qu = nextq

                def S(kt):
                    kap, vap, kb = keys[kt]
                    T.op(pe, lambda: nc.tensor.matmul(PS[:, kt % 3, :], lhsT=kap, rhs=Q2[qu], start=True, stop=True),
                         reads=list(kb) + [B_Q2[qu]], writes=[psb[kt % 3]])

                S(0)
                if n > 1:
                    S(1)
                if gix + 1 < len(groups):
                    nextq = build_Q2(i, groups[gix + 1][0], groups[gix + 1][2])
                elif h + 1 < 8:
                    q2carry['q'] = build_Q2((h + 1) % 2, 0, 0)
                for kt in range(n):
                    if kt + 2 < n:
                        S(kt + 2)
                    a = kt % 3
                    e = kt % 3
                    T.op(act, lambda: nc.scalar.activation(
                        out=E[e], in_=PS[:, a, :].rearrange("p (m q) -> p m q", m=2), func=AF.Exp, scale=0.125),
                        reads=[psb[a]], writes=[B_E[e]])
                    kap, vap, kb = keys[kt]
                    fns = []
                    for m in range(2):
                        fns.append(lambda m=m: nc.tensor.matmul(
                            PS[:, 3, m * nq:(m + 1) * nq], lhsT=vap, rhs=E[e][:, m, :], start=(kt == 0 and m == 0),
                            stop=(kt == n - 1), skip_group_check=True))
                        fns.append(lambda m=m: nc.tensor.matmul(
                            PS[:, 4, m * nq:(m + 1) * nq], lhsT=ones_b, rhs=E[e][:, m, :], start=(kt == 0 and m == 0),
                            stop=(kt == n - 1), skip_group_check=True))
                    T.group(pe, fns, reads=[B_E[e]] + list(kb) + [B_c2], writes=[psb[3], psb[4]])
                    done += 1
                    stepc[0] += 1
                    run_deferred()
                    while emitted < len(pend) and emitted * nsteps < done * len(pend):
                        while act_heavy and act_heavy[0][0] <= stepc[0] and not any(d_[2] <= act_heavy[0][2] for d_ in deferred):
                            act_heavy.pop(0)[1]()
                        pend[emitted]()
                        emitted += 1
                    if not pend:
                        while act_heavy and act_heavy[0][0] <= stepc[0] and not any(d_[2] <= act_heavy[0][2] for d_ in deferred):
                            act_heavy.pop(0)[1]()
                gi = gcount[0]
                gcount[0] += 1
                gp = gi % 2
                bp = bcount[0] % 2
                if half == 1:
                    bcount[0] += 1
                while deferred and deferred[0][2] <= gi - 2:
                    deferred.pop(0)[1]()
                while act_heavy and act_heavy[0][2] <= gi - 2:
                    act_heavy.pop(0)[1]()
                while deferred and deferred[0][2] <= gi - 2:
                    deferred.pop(0)[1]()
                for d_ in [d_ for d_ in deferred if d_[2] == gi - 1 and len(d_) > 3 and d_[3] == "st1"]:
                    deferred.remove(d_)
                    d_[1]()
                o_all, v_all = o_all_[bp], v_all_[bp]
                B_oall, B_vall = B_oall_[bp], B_vall_[bp]
                osq, dd, B_osq, B_dd = osq_[gp], dd_[gp], B_osq_[gp], B_dd_[gp]
                oa = o_all[:, half * nq:(half + 1) * nq]
                va = v_all[:, half * nq:(half + 1) * nq]
                T.op(dve, lambda: nc.vector.tensor_scalar(
                    out=dn, in0=PS[:, 4, :].rearrange("p (m q) -> p m q", m=2), scalar1=2.0 ** -12, scalar2=None,
                    op0=ALU.mult), reads=[psb[4]], writes=[B_dn])
                T.op(dve, lambda: nc.vector.tensor_tensor(out=oa, in0=PS[:, 3, 0:nq], in1=dn[:, 1, :], op=ALU.mult),
                     reads=[psb[3], B_dn], writes=[B_oall[half]])
                T.op(dve, lambda: nc.vector.tensor_tensor(out=o2, in0=PS[:, 3, nq:2 * nq], in1=dn[:, 0, :],
                                                          op=ALU.mult), reads=[psb[3], B_dn], writes=[B_o2])
                T.op(pool, lambda: nc.gpsimd.tensor_tensor(out=dd, in0=dn[:, 0, :], in1=dn[:, 1, :], op=ALU.mult),
                     reads=[B_dn], writes=[B_dd])
                if emitted < len(pend):
                    pend[emitted]()
                    emitted += 1

                def st1(oa=oa, half=half, B_oall=B_oall, dd=dd, B_dd=B_dd):
                    T.op(dve, lambda: nc.vector.scalar_tensor_tensor(
                        out=oa, in0=o2, scalar=neglam, in1=oa, op0=ALU.mult, op1=ALU.add),
                        reads=[B_o2, B_oall[half], B_lam], writes=[B_oall[half]])
                    T.op(pool, lambda: nc.gpsimd.tensor_tensor(out=dd, in0=dd, in1=dd, op=ALU.mult),
                         reads=[B_dd], writes=[B_dd])

                def st2(oa=oa, half=half, B_oall=B_oall, osq=osq, B_osq=B_osq):
                    T.op(dve, lambda: nc.vector.tensor_tensor(out=osq, in0=oa, in1=oa, op=ALU.mult),
                         reads=[B_oall[half]], writes=[B_osq])

                hold = {}

                def st3(osq=osq, B_osq=B_osq, hold=hold):
                    k = pool_bank()
                    hold["k"] = k
                    T.op(pe, lambda: nc.tensor.matmul(PS[:, k, 0:nq], lhsT=jmat, rhs=osq, start=True, stop=True),
                         reads=[B_osq, B_c2], writes=[psb[k]])

                def st4(va=va, half=half, B_vall=B_vall, dd=dd, B_dd=B_dd, hold=hold):
                    k = hold["k"]
                    T.op(dve, lambda: nc.vector.scalar_tensor_tensor(
                        out=va, in0=dd, scalar=LN_EPS * (2.0 ** 24), in1=PS[:, k, 0:nq], op0=ALU.mult, op1=ALU.add),
                        reads=[B_dd, psb[k]], writes=[B_vall[half]])

                deferred.append((stepc[0] + 1, st1, gi, "st1"))
                deferred.append((stepc[0] + 2, st2, gi))
                deferred.append((stepc[0] + 3, st3, gi))
                deferred.append((stepc[0] + 5, st4, gi))
                if half == 1:
                    def b1(v_all=v_all, B_vall=B_vall):
                        T.op(act, lambda: nc.scalar.activation(out=v_all, in_=v_all, func=AF.Ln),
                             reads=B_vall, writes=B_vall)
                        T.op(act, lambda: nc.scalar.activation(out=v_all, in_=v_all, func=AF.Exp, scale=-0.5),
                             reads=B_vall, writes=B_vall)

                    def b2(v_all=v_all, B_vall=B_vall, g=g, i=i):
                        c0 = g * 512
                        T.op(pool, lambda: nc.gpsimd.tensor_tensor(out=v_all, in0=v_all, in1=tza[i][:, c0:c0 + 512],
                                                                   op=ALU.mult),
                             reads=B_vall + [B_tza[i][g]], writes=B_vall)

                    def b3(v_all=v_all, B_vall=B_vall, o_all=o_all, B_oall=B_oall, g=g, h=h):
                        c0 = g * 512
                        T.op(dve, lambda: nc.vector.scalar_tensor_tensor(
                            out=oagT[:, h, c0:c0 + 512], in0=o_all, scalar=gsc, in1=v_all, op0=ALU.mult,
                            op1=ALU.mult), reads=B_oall + B_vall + [B_lam], writes=[B_oag[h][g]])

                    def b1x(b1=b1, b2=b2, b3=b3, gi=gi):
                        b1()
                        deferred.append((stepc[0] + 3, b2, gi))
                        deferred.append((stepc[0] + 5, b3, gi))
                    act_heavy.append((stepc[0] + 7, b1x, gi))
            while emitted < len(pend):
                pend[emitted]()
                emitted += 1

        ph3pre = Arena(BIG + 88 * KB, BIG + 88 * KB + 2 * 9216)
        WR = [sb(f"WR{i}", [128, 8, WR_COLS], BF16, ph3pre) for i in range(2)]
        B_WR = [T.b("WR", i) for i in range(2)]

        def load_WR(h, extra=()):
            v = wr_d[h].rearrange("(kt p) c -> p kt c", p=128)
            T.dma(pool, [(WR[h % 2], v)], writes=[B_WR[h % 2]] + list(extra))

        for c_ in mk_chunks(0):
            c_()
        emit_decay_tables()
        for h in range(8):
            if h == 7:
                load_WR(0, extra=[B_WA[0], B_WA[1]])
                load_WR(1, extra=[B_WA[0], B_WA[1]])
            pend = mk_chunks(h + 1) if h + 1 < 8 else []
            if h + 2 < 8:
                load_WA(h + 2)
            attention_head(h, pend)
        run_deferred(force=True)
        while act_heavy:
            act_heavy.pop(0)[1]()
        run_deferred(force=True)
        T.barrier()
        if STOP == 2:
            T.finish()
            return nc

    if True:
        ph = Arena(BIG + 88 * KB + 2 * 9216, TOP)
        qpl = [sb(f"qpl{i}", [64, NQ_TOK], BF16, ph) for i in range(2)]
        qdT = [sb(f"qdT{i}", [128, NQ_TOK], BF16, ph) for i in range(2)]
        krT = [sb(f"krT{i}", [64, NQ_TOK], BF16, ph) for i in range(2)]
        kd = [sb(f"kd{i}", [128, 20, 128], BF16, ph) for i in range(2)]
        vr = [sb(f"vr{i}", [128, 20, 128], BF16, ph) for i in range(2)]
        tzr = [sb(f"tzr{i}", [128, NQ_TOK], BF16, ph) for i in range(2)]
        tmpz = sb("tmpzr", [128, 512], F32, ph)
        KDh = [sb(f"KDh{i}", [128, 1, 128], F32, ph) for i in range(2)]
        B_KDh = [T.b("KDh", i) for i in range(2)]
        zc = sb("zcr", [128, 512], F32, ph)
        S_ = sb("stS", [128, 12, 128], F32, ph)
        stB = sb("stB", [128, 12, 128], BF16, ph)
        yc = sb("ycar", [128, 2, 128], F32, ph)
        fin = [sb(f"fin{i}", [128, 2, 128], F32, ph) for i in range(2)]
        innT = [sb(f"innT{i}", [128, 4, 128], BF16, ph) for i in range(3)]
        ob = [t_.rearrange("p c j -> p (c j)") for t_ in innT]
        sqb = ob
        rstd = [sb(f"rstd3_{i}", [128, 512], F32, ph) for i in range(3)]

        B_qpl = [[T.b("qpl", i, g) for g in range(3)] for i in range(2)]
        B_qd = [[T.b("qd", i, g) for g in range(3)] for i in range(2)]
        B_krT = [[T.b("krT", i, g) for g in range(3)] for i in range(2)]
        B_kd = [[T.b("kd", i, g) for g in range(5)] for i in range(2)]
        B_vr = [[T.b("vr", i, g) for g in range(5)] for i in range(2)]
        B_tzr = [[T.b("tzr", i, g) for g in range(3)] for i in range(2)]
        B_tz = T.b("tmpzr")
        B_zc = T.b("zcr")
        B_st = T.b("stS")
        B_stb = T.b("stB")
        B_fin = [T.b("fin", i) for i in range(2)]
        B_inn = [T.b("innT", i) for i in range(3)]
        B_ob = B_inn
        B_sqb = B_inn
        B_rstd = [T.b("rstd3", i) for i in range(3)]
        B_org = [[T.b("org", h, g) for g in range(3)] for h in range(8)]

        pb3 = [0]
        PB3 = [5, 6, 7]

        def pool_bank3():
            pb3[0] = (pb3[0] + 1) % 3
            return PB3[pb3[0]]

        def mk_chunks3(h):
            i = h % 2
            W = WR[i]

            def fgroup(c0, st, ncols=128):
                k = pool_bank3()
                T.group(pe, [(lambda kt=kt: nc.tensor.matmul(
                    PS[0:ncols, k, :], lhsT=W[:, kt, c0:c0 + ncols], rhs=hT[:, kt, st * 512:(st + 1) * 512],
                    start=(kt == 0), stop=(kt == 7))) for kt in range(8)],
                    reads=[B_WR[i], B_hT[st]], writes=[psb[k]])
                return k

            def qq(st):
                k = fgroup(0, st)
                T.op(dve, lambda: nc.vector.tensor_copy(out=qpl[i][:, st * 512:(st + 1) * 512], in_=PS[0:64, k, :]),
                     reads=[psb[k]], writes=[B_qpl[i][st]])
                T.op(dve, lambda: nc.vector.tensor_tensor(
                    out=qdT[i][:, st * 512:(st + 1) * 512].rearrange("p (c j) -> p c j", c=4),
                    in0=PS[:, k, :].rearrange("p (c j) -> p c j", c=4),
                    in1=QD[:, h:h + 1, :].to_broadcast([128, 4, 128]), op=ALU.mult),
                    reads=[psb[k], B_dec], writes=[B_qd[i][st]])

            def kr(st):
                k = fgroup(256, st, 64)
                T.op(act, lambda: nc.scalar.activation(out=krT[i][:, st * 512:(st + 1) * 512], in_=PS[0:64, k, :],
                                                       func=AF.Copy), reads=[psb[k]], writes=[B_krT[i][st]])

            def zr(st):
                k = fgroup(128, st)
                T.op(act, lambda: nc.scalar.activation(out=tmpz, in_=PS[:, k, :], func=AF.Exp, scale=-1.0),
                     reads=[psb[k]], writes=[B_tz])
                T.op(dve, lambda: nc.vector.tensor_copy(out=zc, in_=PS[:, k, :]), reads=[psb[k]], writes=[B_zc])
                T.op(act, lambda: nc.scalar.activation(out=tmpz, in_=tmpz, func=AF.Ln, bias=one_col),
                     reads=[B_tz, B_c2], writes=[B_tz])
                T.op(act, lambda: nc.scalar.activation(out=tmpz, in_=tmpz, func=AF.Exp, scale=-1.0),
                     reads=[B_tz], writes=[B_tz])
                T.op(pool, lambda: nc.gpsimd.tensor_tensor(out=tzr[i][:, st * 512:(st + 1) * 512], in0=zc, in1=tmpz,
                                                           op=ALU.mult),
                     reads=[B_tz, B_zc], writes=[B_tzr[i][st]])

            def tpair(tp):
                k = pool_bank3()
                st = tp // 2
                fns = []
                for t2 in range(2):
                    tt = tp * 2 + t2
                    for kt in range(8):
                        fns.append(lambda kt=kt, tt=tt, t2=t2: nc.tensor.matmul(
                            PS[:, k, t2 * 256:(t2 + 1) * 256], lhsT=hT[:, kt, tt * 128:(tt + 1) * 128],
                            rhs=W[:, kt, 320:576], start=(kt == 0), stop=(kt == 7)))
                T.group(pe, fns, reads=[B_WR[i], B_hT[st]], writes=[psb[k]])
                pv = PS[:, k, :].rearrange("p (t c) -> p t c", t=2)
                T.op(dve, lambda: nc.vector.tensor_tensor(
                    out=kd[i][:, tp * 2:tp * 2 + 2, :], in0=pv[:, :, 0:128],
                    in1=KDh[i].to_broadcast([128, 2, 128]), op=ALU.mult),
                    reads=[psb[k], B_KDh[i]], writes=[B_kd[i][st]])
                T.op(act, lambda: nc.scalar.activation(out=vr[i][:, tp * 2:tp * 2 + 2, :], in_=pv[:, :, 128:256],
                                                       func=AF.Copy), reads=[psb[k]], writes=[B_vr[i][st]])

            def seq(*fs):
                return lambda: [f() for f in fs]

            def mk_kdh():
                T.op(pool, lambda: nc.gpsimd.tensor_copy(out=KDh[i][:, 0, 0:64],
                                                         in_=kdcol[:, h:h + 1].to_broadcast([128, 64])),
                     reads=[B_dec], writes=[B_KDh[i]])
                T.op(pool, lambda: nc.gpsimd.tensor_copy(out=KDh[i][:, 0, 64:128],
                                                         in_=kdcol[:, 8 + h:9 + h].to_broadcast([128, 64])),
                     reads=[B_dec], writes=[B_KDh[i]])

            mk_kdh()
            ch = []
            fch = [seq(lambda st=st: zr(st), lambda st=st: qq(st), lambda st=st: kr(st)) for st in range(3)]
            ch.append(seq(lambda: tpair(9), lambda: tpair(8)))
            ch.append(fch[0])
            ch.append(seq(lambda: tpair(7), lambda: tpair(6)))
            ch.append(seq(lambda: tpair(0), lambda: tpair(1)))
            ch.append(fch[1])
            ch.append(seq(lambda: tpair(2), lambda: tpair(3)))
            ch.append(fch[2])
            ch.append(seq(lambda: tpair(4), lambda: tpair(5)))
            return ch

        def mk_ret(h):
            i = h % 2
            ops = []
            cdX = CDv[0:64, h:h + 1]
            cdY = CDv[64:128, h:h + 1]
            ubank = {}

            def Ugrp(g4, bank):
                def f():
                    T.group(pe, [(lambda t4=t4: nc.tensor.matmul(
                        PS[:, bank, t4 * 128:(t4 + 1) * 128], lhsT=kd[i][:, g4 * 4 + t4, :],
                        rhs=vr[i][:, g4 * 4 + t4, :], start=True, stop=True)) for t4 in range(4)],
                        reads=[B_kd[i][g4], B_vr[i][g4]], writes=[psb[bank]])
                for t4 in range(4):
                    ubank[g4 * 4 + t4] = (bank, t4)
                return f

            def U(tile, lo, hi):
                k, t4 = ubank[tile]
                return PS[lo:hi, k, t4 * 128:(t4 + 1) * 128], psb[k]

            def chain(dst, src, cd, tile, lo, hi, wbuf):
                def f():
                    u, ub = U(tile, lo, hi)
                    T.op(dve, lambda: nc.vector.scalar_tensor_tensor(out=dst, in0=src, scalar=cd, in1=u,
                                                                     op0=ALU.mult, op1=ALU.add),
                         reads=[ub, B_st, B_dec, B_const], writes=[wbuf])
                return f

            def ucopy(dst, tile, lo, hi):
                def f():
                    u, ub = U(tile, lo, hi)
                    T.op(dve, lambda: nc.vector.tensor_copy(out=dst, in_=u), reads=[ub], writes=[B_st])
                return f

            ops.append(Ugrp(4, 0))
            ops.append(Ugrp(3, 1))
            prev = s0[64:128, h, :]
            for n_, tile in enumerate(range(19, 11, -1)):
                dst = yc[64:128, n_ % 2, :] if tile > 12 else S_[64:128, 11, :]
                ops.append(chain(dst, prev, cdY, tile, 64, 128, B_st))
                prev = dst
            ops.append(Ugrp(0, 2))
            ops.append(Ugrp(1, 3))
            ops.append(Ugrp(2, 4))
            ops.append(lambda: T.op(pool, lambda: nc.gpsimd.memset(S_[0:64, 0:4:2, :], 0.0), writes=[B_st]))
            ops.append(lambda: T.op(pool, lambda: nc.gpsimd.memset(S_[64:128, 1:4:2, :], 0.0), writes=[B_st]))
            for pb in range(2):
                c0 = pb * 2
                ops.append(ucopy(S_[0:64, c0 + 1, :], c0, 0, 64))
                ops.append(ucopy(S_[64:128, c0, :], c0 + 1, 64, 128))
                ops.append(chain(fin[i][0:64, pb, :], S_[0:64, c0 + 1, :], cdX, c0 + 1, 0, 64, B_fin[i]))
                ops.append(chain(fin[i][64:128, pb, :], S_[64:128, c0, :], cdY, c0, 64, 128, B_fin[i]))
            ops.append(lambda: T.dma(sp, [(nst_d[pb, h], fin[i][:, pb, :]) for pb in range(2)], reads=[B_fin[i]]))
            ops.append(lambda: T.op(dve, lambda: nc.vector.tensor_copy(out=S_[0:64, 4, :], in_=s0[0:64, h, :]),
                                    reads=[B_const], writes=[B_st]))
            for c in range(7):
                ops.append(chain(S_[0:64, 5 + c, :], S_[0:64, 4 + c, :], cdX, 4 + c, 0, 64, B_st))
                cy = 7 - c
                ops.append(chain(S_[64:128, 4 + cy - 1, :], S_[64:128, 4 + cy, :], cdY, 4 + cy, 64, 128, B_st))
            ops.append(lambda: T.op(act, lambda: nc.scalar.activation(out=stB, in_=S_, func=AF.Copy),
                                    reads=[B_st], writes=[B_stb]))
            n_early = len(ops)
            A_b = [0, 1, 2]
            O_b = [3, 4, 0]
            C_b = [1, 2, 3]
            V_b = [4, 0, 4]
            for g in range(3):
                ops.append(lambda g=g: T.group(pe, [(lambda c4=c4: nc.tensor.matmul(
                    PS[:, A_b[g], c4 * 128:(c4 + 1) * 128],
                    lhsT=krT[i][:, (g * 4 + c4) * 128:(g * 4 + c4 + 1) * 128],
                    rhs=qpl[i][:, (g * 4 + c4) * 128:(g * 4 + c4 + 1) * 128], start=True, stop=True))
                    for c4 in range(4)], reads=[B_krT[i][g], B_qpl[i][g]], writes=[psb[A_b[g]]]))
            for g in range(3):
                ops.append(lambda g=g: T.op(dve, lambda: nc.vector.tensor_tensor(
                    out=innT[g], in0=PS[:, A_b[g], :].rearrange("p (c j) -> p c j", c=4),
                    in1=DC[:, h:h + 1, :].to_broadcast([128, 4, 128]), op=ALU.mult),
                    reads=[psb[A_b[g]], B_dec], writes=[B_inn[g]]))
            for g in range(3):
                def og(g=g):
                    fns = []
                    for c4 in range(4):
                        s_ = g * 4 + c4
                        fns.append(lambda c4=c4, s_=s_: nc.tensor.matmul(
                            PS[:, O_b[g], c4 * 128:(c4 + 1) * 128], lhsT=vr[i][:, s_, :], rhs=innT[g][:, c4, :],
                            start=True, stop=False))
                        fns.append(lambda c4=c4, s_=s_: nc.tensor.matmul(
                            PS[:, O_b[g], c4 * 128:(c4 + 1) * 128], lhsT=stB[:, s_, :],
                            rhs=qdT[i][:, s_ * 128:(s_ + 1) * 128], start=False, stop=True))
                    T.group(pe, fns, reads=[B_vr[i][g], B_inn[g], B_stb, B_qd[i][g]], writes=[psb[O_b[g]]])
                ops.append(og)
            for g in range(3):
                ops.append(lambda g=g: T.op(act, lambda: nc.scalar.activation(out=ob[g], in_=PS[:, O_b[g], :],
                                                                              func=AF.Copy),
                                            reads=[psb[O_b[g]]], writes=[B_ob[g]]))
            for g in range(3):
                ops.append(lambda g=g: T.op(pe, lambda: nc.tensor.matmul(PS[:, C_b[g], :], lhsT=cmat, rhs=ob[g],
                                                                         start=True, stop=True),
                                            reads=[B_ob[g], B_c2], writes=[psb[C_b[g]]]))
            for g in range(3):
                ops.append(lambda g=g: T.op(act, lambda: nc.scalar.activation(out=sqb[g], in_=PS[:, C_b[g], :],
                                                                              func=AF.Square),
                                            reads=[psb[C_b[g]]], writes=[B_sqb[g]]))
            for g in range(3):
                ops.append(lambda g=g: T.op(pe, lambda: nc.tensor.matmul(PS[:, V_b[g], :], lhsT=jmat, rhs=sqb[g],
                                                                         start=True, stop=True),
                                            reads=[B_sqb[g], B_c2], writes=[psb[V_b[g]]]))
                ops.append(lambda g=g: T.op(dve, lambda: nc.vector.tensor_scalar(
                    out=rstd[g], in0=PS[:, V_b[g], :], scalar1=LN_EPS, scalar2=None, op0=ALU.add),
                    reads=[psb[V_b[g]]], writes=[B_rstd[g]]))
            for g in range(3):
                ops.append(lambda g=g: T.op(act, lambda: nc.scalar.activation(out=rstd[g], in_=rstd[g], func=AF.Ln),
                                            reads=[B_rstd[g]], writes=[B_rstd[g]]))
            for g in range(3):
                ops.append(lambda g=g: T.op(act, lambda: nc.scalar.activation(out=rstd[g], in_=rstd[g], func=AF.Exp,
                                                                              scale=-0.5),
                                            reads=[B_rstd[g]], writes=[B_rstd[g]]))
            for g in range(3):
                ops.append(lambda g=g: T.op(pool, lambda: nc.gpsimd.tensor_tensor(
                    out=rstd[g], in0=rstd[g], in1=tzr[i][:, g * 512:(g + 1) * 512], op=ALU.mult),
                    reads=[B_rstd[g], B_tzr[i][g]], writes=[B_rstd[g]]))
            for g in range(3):
                ops.append(lambda g=g: T.op(dve, lambda: nc.vector.scalar_tensor_tensor(
                    out=orgT[:, h, g * 512:(g + 1) * 512], in0=PS[:, C_b[g], :], scalar=gngT[:, h:h + 1],
                    in1=rstd[g], op0=ALU.mult, op1=ALU.mult),
                    reads=[psb[C_b[g]], B_rstd[g], B_const], writes=[B_org[h][g]]))
            return ops, n_early

        ph4pre = Arena(BIG + 88 * KB, BIG + 88 * KB + 2 * 9216)
        WG = [sb(f"WG{i}", [128, 8, 512], BF16, ph4pre) for i in range(2)]
        B_WG = [T.b("WG", i) for i in range(2)]

        def load_WG(j, extra=()):
            v = wg_d[j].rearrange("(kt p) c -> p kt c", p=128)
            T.dma(pool, [(WG[j % 2], v)], writes=[B_WG[j % 2]] + list(extra))

        for c_ in mk_chunks3(0):
            c_()
        for h in range(8):
            ret, n_early = mk_ret(h)
            if h + 2 < 8:
                load_WR(h + 2)
            if h == 7:
                load_WG(0, extra=[B_WR[0], B_WR[1]])
                load_WG(1, extra=[B_WR[0], B_WR[1]])
            chs = mk_chunks3(h + 1) if h + 1 < 8 else []
            nr, ncs = len(ret), len(chs)
            ri = 0
            n_e_ch = 1
            for ci, c_ in enumerate(chs):
                c_()
                if ci < n_e_ch:
                    tgt = (ci + 1) * n_early // n_e_ch
                else:
                    tgt = n_early + (ci + 1 - n_e_ch) * (nr - n_early) // (ncs - n_e_ch)
                while ri < tgt:
                    ret[ri]()
                    ri += 1
            while ri < nr:
                ret[ri]()
                ri += 1
        T.barrier()
        if STOP == 3:
            T.finish()
            return nc

    if True:
        ph = Arena(BIG + 112 * KB, TOP)
        ph4b = Arena(BIG, BIG + 88 * KB)
        WO = sb("WO", [128, 8, D], BF16, ph)
        mT = sb("mT", [128, 8, NQ_TOK], BF16, ph)
        lnrow = sb("lnrow", [128, 2, D], F32, ph)
        ta = [sb(f"ta{i}", [128, 512], F32, ph) for i in range(2)]
        tr_ = [sb(f"tr{i}", [128, 512], F32, ph) for i in range(2)]
        XT = [sb(f"XT4_{i}", [128, D], F32, ph4b) for i in range(4)]
        Z = [sb(f"Z{i}", [128, D], F32, ph4b) for i in range(4)]
        Y = [sb(f"Y{i}", [128, D], F32, ph4b) for i in range(4)]
        st6 = [sb(f"st6b_{i}", [128, 2, 6], F32, ph) for i in range(4)]
        mv = [sb(f"mvb{i}", [128, 2], F32, ph) for i in range(4)]
        rs = [sb(f"rsb{i}", [128, 1], F32, ph) for i in range(4)]
        B_WO = T.b("WO")
        B_mT = [[T.b("mT", j, t) for t in range(3)] for j in range(8)]
        B_ta = [T.b("ta", i) for i in range(2)]
        B_tr = [T.b("tr", i) for i in range(2)]
        B_x4 = [T.b("XT4", i) for i in range(4)]
        B_z = [T.b("Z", i) for i in range(4)]
        B_y = [T.b("Y", i) for i in range(4)]
        B_s4 = [T.b("s4", i) for i in range(4)]

        wov = wout_d.rearrange("(kt p) c -> p kt c", p=128)
        T.dma(pool, [(WO, wov)], writes=[B_WO])
        B_ln = T.b("lnrow")
        T.dma(sp, [(lnrow, lnrow_d)], writes=[B_ln])
        rr = [0]
        for j in range(8):
            i = j % 2
            W = WG[i]
            if j >= 1 and j + 1 < 8:
                load_WG(j + 1)
            for t in range(3):
                def grp(c0, rhs_fn, rbufs):
                    k = next_bank()
                    T.group(pe, [(lambda kt=kt, k=k: nc.tensor.matmul(
                        PS[:, k, :], lhsT=W[:, kt, c0:c0 + 128], rhs=rhs_fn(kt), start=(kt == 0), stop=(kt == 7)))
                        for kt in range(8)], reads=[B_WG[i]] + rbufs, writes=[psb[k]])
                    return k
                hsl = lambda kt, t=t: hT[:, kt, t * 512:(t + 1) * 512]
                kga = grp(0, hsl, [B_hT[t]])
                kgr = grp(128, hsl, [B_hT[t]])
                kpa = grp(256, lambda kt, t=t: oagT[:, kt, t * 512:(t + 1) * 512], [B_oag[hh][t] for hh in range(8)])
                kpr = grp(384, lambda kt, t=t: orgT[:, kt, t * 512:(t + 1) * 512], [B_org[hh][t] for hh in range(8)])
                rr[0] ^= 1
                u = rr[0]
                T.op(act, lambda u=u, kga=kga, j=j: nc.scalar.activation(
                    out=ta[u], in_=PS[:, kga, :], func=AF.Tanh, scale=0.5, bias=bgTh[:, j:j + 1]),
                    reads=[psb[kga], B_lam], writes=[B_ta[u]])
                T.op(act, lambda u=u, kgr=kgr, j=j: nc.scalar.activation(
                    out=tr_[u], in_=PS[:, kgr, :], func=AF.Tanh, scale=0.5, bias=bgTh[:, 8 + j:9 + j]),
                    reads=[psb[kgr], B_lam], writes=[B_tr[u]])
                T.op(dve, lambda u=u, kpa=kpa: nc.vector.scalar_tensor_tensor(
                    out=ta[u], in0=ta[u], scalar=1.0, in1=PS[:, kpa, :], op0=ALU.add, op1=ALU.mult),
                    reads=[B_ta[u], psb[kpa]], writes=[B_ta[u]])
                T.op(dve, lambda u=u, kpr=kpr: nc.vector.scalar_tensor_tensor(
                    out=tr_[u], in0=tr_[u], scalar=1.0, in1=PS[:, kpr, :], op0=ALU.add, op1=ALU.mult),
                    reads=[B_tr[u], psb[kpr]], writes=[B_tr[u]])
                T.op(pool, lambda u=u, j=j, t=t: nc.gpsimd.tensor_tensor(
                    out=mT[:, j, t * 512:(t + 1) * 512], in0=ta[u], in1=tr_[u], op=ALU.add),
                    reads=[B_ta[u], B_tr[u]], writes=[B_mT[j][t]])
        T.barrier()
        def f_A(tt):
            u = tt % 4
            ci = 0 if tt < 4 else 1
            t = tt // 4
            kk = []
            for hf in range(2):
                k = next_bank()
                kk.append(k)
                T.group(pe, [(lambda kt=kt, k=k, hf=hf: nc.tensor.matmul(
                    PS[:, k, :], lhsT=mT[:, kt, tt * 128:(tt + 1) * 128], rhs=WO[:, kt, hf * 512:(hf + 1) * 512],
                    start=(kt == 0), stop=(kt == 7))) for kt in range(8)],
                    reads=[B_WO] + [B_mT[jj][t] for jj in range(8)], writes=[psb[k]])
            for hf in range(2):
                T.op(dve, lambda hf=hf, k=kk[hf]: nc.vector.tensor_tensor(
                    out=Z[u][:, hf * 512:(hf + 1) * 512], in0=PS[:, k, :],
                    in1=gate_bc[:, ci, hf * 512:(hf + 1) * 512], op=ALU.mult),
                    reads=[psb[kk[hf]], B_gate], writes=[B_z[u]])
            T.op(dve, lambda: nc.vector.scalar_tensor_tensor(
                out=Z[u], in0=XT[u], scalar=ALPHA, in1=Z[u], op0=ALU.mult, op1=ALU.add),
                reads=[B_x4[u], B_z[u]], writes=[B_z[u]])
            for c2 in range(2):
                T.op(dve, lambda c2=c2: nc.vector.bn_stats(out=st6[u][:, c2, :],
                                                           in_=Z[u][:, c2 * 512:(c2 + 1) * 512]),
                     reads=[B_z[u]], writes=[B_s4[u]])
            T.op(dve, lambda: nc.vector.bn_aggr(out=mv[u], in_=st6[u]), reads=[B_s4[u]], writes=[B_s4[u]])
            T.op(dve, lambda: nc.vector.tensor_scalar(out=rs[u], in0=mv[u][:, 1:2], scalar1=LN_EPS,
                                                      scalar2=None, op0=ALU.add),
                 reads=[B_s4[u]], writes=[B_s4[u]])
            T.op(act, lambda: nc.scalar.activation(out=rs[u], in_=rs[u], func=AF.Sqrt),
                 reads=[B_s4[u]], writes=[B_s4[u]])

        def f_B(tt):
            u = tt % 4
            T.op(dve, lambda: nc.vector.reciprocal(out=rs[u], in_=rs[u]), reads=[B_s4[u]], writes=[B_s4[u]])
            T.op(dve, lambda: nc.vector.scalar_tensor_tensor(
                out=mv[u][:, 1:2], in0=mv[u][:, 0:1], scalar=-1.0, in1=rs[u], op0=ALU.mult, op1=ALU.mult),
                reads=[B_s4[u]], writes=[B_s4[u]])
            T.op(act, lambda: nc.scalar.activation(out=Y[u], in_=Z[u], func=AF.Identity, scale=rs[u],
                                                   bias=mv[u][:, 1:2]),
                 reads=[B_z[u], B_s4[u]], writes=[B_y[u]])
            T.op(pool, lambda: nc.gpsimd.tensor_tensor(out=Y[u], in0=Y[u], in1=lnrow[:, 0, :], op=ALU.mult),
                 reads=[B_y[u], B_ln], writes=[B_y[u]])

        def f_C(tt):
            u = tt % 4
            T.op(dve, lambda: nc.vector.tensor_tensor(out=Y[u], in0=Y[u], in1=lnrow[:, 1, :], op=ALU.add),
                 reads=[B_y[u], B_ln], writes=[B_y[u]])
            if tt + 4 < 12:
                T.dma(sp, [(XT[u], xs[(tt + 4) * 128:(tt + 5) * 128, :])], writes=[B_x4[u]])
            T.dma(sp, [(y_d[tt * 128:(tt + 1) * 128, :], Y[u])], reads=[B_y[u]])

        for t_ in range(4):
            T.dma(sp, [(XT[t_], xs[t_ * 128:(t_ + 1) * 128, :])], writes=[B_x4[t_]])
        for i_ in range(12 + 2):
            if i_ < 12:
                f_A(i_)
            if 0 <= i_ - 1 < 12:
                f_B(i_ - 1)
            if 0 <= i_ - 2 < 12:
                f_C(i_ - 2)
        T.finish()
    es.close()
    return nc


def _consts():
    p = np.arange(128)
    i = np.arange(128)
    c = {}
    c["ident"] = np.eye(128, dtype=np.float32)
    c["cmat"] = (np.eye(128) - 1.0 / 128.0).astype(np.float32)
    iq = np.zeros((128, 128), np.float32)
    iq[0:64, :] = (i + 1)[None, :]
    iq[64:128, :] = (128 - i)[None, :]
    c["iq"] = iq
    c["jp"] = np.stack([127 - p, p], axis=1).astype(np.float32)
    diff = (i[None, :] - p[:, None]).astype(np.float32)
    c["diff"] = diff
    c["mx8"] = (0.125 * (diff >= 0)).astype(np.float32)
    c["my8"] = (0.125 * (diff <= 0)).astype(np.float32)
    sel2 = np.zeros((2, 2, 128), np.float32)
    sel2[0, 0, :] = 1.0
    sel2[1, 1, :] = 1.0
    c["sel2"] = sel2
    return c


def _rope_tables(pos):
    pos = np.asarray(pos)
    r = (pos // 64).astype(np.float32)
    col = (pos % 64).astype(np.float32)
    inv = (np.float32(10000.0) ** (-np.arange(16, dtype=np.float32) / np.float32(16))).astype(np.float32)
    ang = np.concatenate([r[:, None] * inv[None, :], col[:, None] * inv[None, :]], axis=1).astype(np.float32)
    cos = np.cos(ang).astype(np.float32)
    sin = np.sin(ang).astype(np.float32)
    pidx = (np.arange(128) % 64) // 2
    sgn = np.where(np.arange(128) % 2 == 1, 1.0, -1.0).astype(np.float32)
    C = cos[:, pidx].T.copy()
    S = (sin[:, pidx] * sgn[None, :]).T.copy()
    return np.ascontiguousarray(C, np.float32), np.ascontiguousarray(S, np.float32)


_PROGRAM = None


def kernel(x_prompt, x_sample, cache_attn_k, cache_attn_v, state_ret_fwd, state_ret_bwd,
           c, c_ctx, w_mod, b_mod, w_in, lam_params, subln_g, ret_decay, ret_gn_g,
           w_pa, w_pr, w_gate, b_gate, w_out, ln_g, ln_b):
    global _PROGRAM
    f = lambda a: np.ascontiguousarray(np.asarray(a), dtype=np.float32)
    x_prompt, x_sample = f(x_prompt), f(x_sample)
    cache_attn_k, cache_attn_v = f(cache_attn_k), f(cache_attn_v)
    state_ret_fwd, state_ret_bwd = f(state_ret_fwd), f(state_ret_bwd)
    c, c_ctx = f(c), f(c_ctx)
    w_in0 = f(w_in)[0]
    w_gate0, w_pa0, w_pr0 = f(w_gate)[0], f(w_pa)[0], f(w_pr)[0]
    ret_decay0 = f(ret_decay)[0]

    sw = np.arange(128) ^ 1
    wa = np.empty((8, D, WA_COLS), np.float32)
    wr = np.empty((8, D, WR_COLS), np.float32)
    wg = np.empty((8, D, 512), np.float32)
    for h in range(8):
        qa = w_in0[:, h * 128:(h + 1) * 128]
        ka = w_in0[:, 1024 + h * 128:1024 + (h + 1) * 128]
        va = w_in0[:, 2048 + h * 128:2048 + (h + 1) * 128]
        za = w_in0[:, 3072 + h * 128:3072 + (h + 1) * 128]
        qr = w_in0[:, 4096 + h * 64:4096 + (h + 1) * 64]
        kr = w_in0[:, 4608 + h * 64:4608 + (h + 1) * 64]
        vr = w_in0[:, 5120 + h * 128:5120 + (h + 1) * 128]
        zr = w_in0[:, 6144 + h * 128:6144 + (h + 1) * 128]
        wa[h] = np.concatenate([qa, qa[:, sw], ka[:, sw], za, ka, va], axis=1)
        wr[h] = np.concatenate([qr, qr, zr, kr, kr, kr, vr], axis=1)
        wg[h] = np.concatenate([w_gate0[:, h * 128:(h + 1) * 128], w_gate0[:, 1024 + h * 128:1024 + (h + 1) * 128],
                                w_pa0[:, h * 128:(h + 1) * 128], w_pr0[:, h * 128:(h + 1) * 128]], axis=1)
    consts = _consts()
    shared = {
        "w_mod": f(w_mod)[0], "bmod2": np.ascontiguousarray(np.broadcast_to(f(b_mod)[0][None, :], (2, 3 * D))),
        "wa": wa, "wr": wr, "wg": wg, "w_out": f(w_out)[0],
        "bgT": np.ascontiguousarray(f(b_gate)[0].reshape(16, 128).T),
        "lnrow": np.ascontiguousarray(np.broadcast_to(np.stack([f(ln_g)[0], f(ln_b)[0]])[None], (128, 2, D))),
        "lamp": np.ascontiguousarray(np.broadcast_to(f(lam_params)[0].reshape(1, 256), (128, 256))),
        "sublnT": np.ascontiguousarray(f(subln_g)[0].reshape(128, 1)),
        "gngT": np.ascontiguousarray(f(ret_gn_g)[0].reshape(8, 128).T),
    }
    shared.update(consts)

    in_maps = []
    for core in range(NCORES):
        b, half = core // 2, core % 2
        rev = half == 1
        o = (lambda a: a[::-1]) if rev else (lambda a: a)
        p0, p1 = o(x_prompt[2 * core]), o(x_prompt[2 * core + 1])
        mine = o(x_sample[b, half * 1024:(half + 1) * 1024])
        other = o(x_sample[b, (1 - half) * 1024:(2 - half) * 1024])
        pos = np.concatenate([o(np.arange(half * 1024, (half + 1) * 1024)),
                              o(np.arange((1 - half) * 1024, (2 - half) * 1024))])
        rc, rs_ = _rope_tables(pos)
        sX = state_ret_fwd[b, 0] if not rev else state_ret_bwd[b, 0]
        sY = state_ret_bwd[b, 0] if not rev else state_ret_fwd[b, 0]
        dX = ret_decay0[0] if not rev else ret_decay0[1]
        dY = ret_decay0[1] if not rev else ret_decay0[0]
        m = dict(shared)
        m["xs"] = np.ascontiguousarray(np.concatenate([p0, p1, mine, other], axis=0))
        m["ck"] = np.ascontiguousarray(cache_attn_k[b, 0].reshape(256, D))
        m["cv"] = np.ascontiguousarray(cache_attn_v[b, 0].reshape(256, D))
        m["s0"] = np.ascontiguousarray(np.concatenate([sX.transpose(1, 0, 2), sY.transpose(1, 0, 2)], axis=0))
        m["condT"] = np.ascontiguousarray(np.stack([c_ctx.reshape(8, 128).T, c[b].reshape(8, 128).T], axis=2))
        m["rdsel"] = np.ascontiguousarray(np.concatenate([np.broadcast_to(dX[None], (64, 8)),
                                                           np.broadcast_to(dY[None], (64, 8))], axis=0))
        m["rdrow"] = np.ascontiguousarray(np.broadcast_to(np.concatenate([dX, dY])[None], (128, 16)))
        m["ropeC"] = rc
        m["ropeS"] = rs_
        in_maps.append(m)

    if _PROGRAM is None:
        import os as _os
        _PROGRAM = build_program(int(_os.environ.get('MK_STOP', '99')))
    res = run_bass_kernel_spmd(_PROGRAM, in_maps, core_ids=list(range(NCORES)))

    y_prompt = np.empty((16, 256, D), np.float32)
    y_sample = np.empty((4, 2048, D), np.float32)
    new_k = np.empty((16, 1, 256, 8, 128), np.float32)
    new_v = np.empty((16, 1, 256, 8, 128), np.float32)
    new_f = np.empty((16, 1, 8, 64, 128), np.float32)
    new_b = np.empty((16, 1, 8, 64, 128), np.float32)
    for core in range(NCORES):
        r = res.results[core]
        b, half = core // 2, core % 2
        rev = half == 1
        o = (lambda a: a[::-1]) if rev else (lambda a: a)
        y = r["y"]
        for pb in range(2):
            y_prompt[2 * core + pb] = o(y[pb * 256:(pb + 1) * 256])
            new_k[2 * core + pb, 0] = o(r["nk"][pb * 256:(pb + 1) * 256]).reshape(256, 8, 128)
            new_v[2 * core + pb, 0] = o(r["nv"][pb * 256:(pb + 1) * 256]).reshape(256, 8, 128)
            X = r["nst"][pb][:, 0:64, :]
            Yv = r["nst"][pb][:, 64:128, :]
            new_f[2 * core + pb, 0] = Yv if rev else X
            new_b[2 * core + pb, 0] = X if rev else Yv
        y_sample[b, half * 1024:(half + 1) * 1024] = o(y[512:1536])
    return (y_prompt, y_sample, new_k, new_v, new_f, new_b)
```

```python
import math
from contextlib import ExitStack

import numpy as np
import concourse.bass as bass
import concourse.mybir as mybir
from concourse.bass_utils import run_bass_kernel_spmd

F32 = mybir.dt.float32
BF16 = mybir.dt.bfloat16
AF = mybir.ActivationFunctionType
ALU = mybir.AluOpType
AX = mybir.AxisListType

NCORES = 8
D = 1024
NP_TOK = 512
NM_TOK = 1024
NO_TOK = 1024
NTOK = NP_TOK + NM_TOK + NO_TOK
NQ_TOK = NP_TOK + NM_TOK
WA_COLS = 768
WR_COLS = 576
LAM_INIT = 0.8 - 0.6 * math.exp(-0.3 * 0)
ALPHA = (2.0 * 1) ** 0.25
MOD_EPS = 1e-6
LN_EPS = 1e-5


class _Sem:
    _n = 0

    def __init__(self, nc, name):
        _Sem._n += 1
        self.uid = _Sem._n
        self.h = nc.alloc_semaphore(f"{name}_{self.uid}")


class _Eng:
    def __init__(self, nc, name, handle):
        self.name = name
        self.h = handle
        self.sem = _Sem(nc, "e" + name)
        self.n = 0
        self.seen = {}


class _Buf:
    __slots__ = ("name", "w", "rs", "dsem", "dcnt", "excl")

    def __init__(self, name):
        self.name = name
        self.excl = False
        self.w = None
        self.rs = {}
        self.dsem = None
        self.dcnt = 0


class Tracker:
    def __init__(self, nc):
        self.nc = nc
        self.pe = _Eng(nc, "pe", nc.tensor)
        self.act = _Eng(nc, "act", nc.scalar)
        self.dve = _Eng(nc, "dve", nc.vector)
        self.pool = _Eng(nc, "pool", nc.gpsimd)
        self.sp = _Eng(nc, "sp", nc.sync)
        self.engs = [self.pe, self.act, self.dve, self.pool, self.sp]
        self.bufs = {}
        self.dsems = []

    def b(self, *key):
        bb = self.bufs.get(key)
        if bb is None:
            bb = _Buf(key)
            self.bufs[key] = bb
        return bb

    def _wait(self, eng, tok):
        if tok is None:
            return
        sem, val = tok
        if eng is self.pe and sem is self.pe.sem:
            return
        if eng.seen.get(sem.uid, 0) >= val:
            return
        eng.h.wait_ge(sem.h, val)
        eng.seen[sem.uid] = val

    def _deps(self, eng, reads, writes):
        for r in reads:
            self._wait(eng, r.w)
            if r.excl:
                for t in r.rs.values():
                    if t[0] is not eng.sem:
                        self._wait(eng, t)
        for w in writes:
            self._wait(eng, w.w)
            for t in w.rs.values():
                if t[0] is eng.sem and eng is not self.pool:
                    continue
                self._wait(eng, t)

    def _mark(self, tok, reads, writes):
        for w in writes:
            w.w = tok
            w.rs = {}
        for r in reads:
            r.rs[tok[0].uid] = tok

    def group(self, eng, fns, reads=(), writes=()):
        self._deps(eng, reads, writes)
        ins = None
        for f in fns:
            ins = f()
        eng.n += 1
        ins.then_inc(eng.sem.h, 1)
        self._mark((eng.sem, eng.n), reads, writes)

    def op(self, eng, fn, reads=(), writes=()):
        self.group(eng, [fn], reads, writes)

    def dma(self, eng, pairs, reads=(), writes=()):
        owner = (list(writes) + list(reads))[0]
        kind = "sw" if eng is self.pool else "hw"
        if owner.dsem is None:
            owner.dsem = {}
        if kind not in owner.dsem:
            owner.dsem[kind] = [_Sem(self.nc, "d" + kind), 0]
            self.dsems.append(owner.dsem[kind])
        ent = owner.dsem[kind]
        self._deps(eng, reads, writes)
        for (o, i) in pairs:
            eng.h.dma_start(out=o, in_=i).then_inc(ent[0].h, 16)
            ent[1] += 16
        self._mark((ent[0], ent[1]), reads, writes)

    def barrier(self, skip=()):
        skip_ids = set()
        for b_ in skip:
            for ent in (b_.dsem or {}).values():
                skip_ids.add(ent[0].uid)
        for e in self.engs:
            for f in self.engs:
                if f is not e and f.n > 0:
                    self._wait(e, (f.sem, f.n))
            for d in self.dsems:
                if d[0].uid not in skip_ids:
                    self._wait(e, (d[0], d[1]))

    def finish(self):
        for d in self.dsems:
            self._wait(self.sp, (d[0], d[1]))
        for f in self.engs:
            if f is not self.sp and f.n > 0:
                self._wait(self.sp, (f.sem, f.n))


class _StopBuild(Exception):
    pass


def build_program(STOP=99):
    import os as _os
    tag_stop = _os.environ.get("MK_TAG", "")
    holder = {}

    def chk(tag):
        if tag_stop and tag == tag_stop:
            raise _StopBuild()

    try:
        return _build_program(STOP, chk, holder)
    except _StopBuild:
        holder["T"].finish()
        return holder["nc"]


def _build_program(STOP, chk, holder):
    nc = bass.Bass("TRN2", target_bir_lowering=False)
    T = Tracker(nc)
    holder["T"] = T
    holder["nc"] = nc
    pe, act, dve, pool, sp = T.pe, T.act, T.dve, T.pool, T.sp

    def din(name, shape):
        return nc.dram_tensor(name, list(shape), F32, kind="ExternalInput").ap()

    def dout(name, shape):
        return nc.dram_tensor(name, list(shape), F32, kind="ExternalOutput").ap()

    xs = din("xs", [NTOK, D])
    ck = din("ck", [256, D])
    cv = din("cv", [256, D])
    s0_d = din("s0", [128, 8, 128])
    condT_d = din("condT", [128, 8, 2])
    wmod_d = din("w_mod", [D, 3 * D])
    bmod_d = din("bmod2", [2, 3 * D])
    wa_d = din("wa", [8, D, WA_COLS])
    wr_d = din("wr", [8, D, WR_COLS])
    wg_d = din("wg", [8, D, 512])
    wout_d = din("w_out", [D, D])
    bgT_d = din("bgT", [128, 16])
    lnrow_d = din("lnrow", [128, 2, D])
    lamp_d = din("lamp", [128, 256])
    sublnT_d = din("sublnT", [128, 1])
    gngT_d = din("gngT", [128, 8])
    rdsel_d = din("rdsel", [128, 8])
    rdrow_d = din("rdrow", [128, 16])
    ident_d = din("ident", [128, 128])
    cmat_d = din("cmat", [128, 128])
    iq_d = din("iq", [128, 128])
    jp_d = din("jp", [128, 2])
    diff_d = din("diff", [128, 128])
    mx8_d = din("mx8", [128, 128])
    my8_d = din("my8", [128, 128])
    ropeC_d = din("ropeC", [128, 2048])
    ropeS_d = din("ropeS", [128, 2048])
    sel2_d = din("sel2", [2, 2, 128])

    y_d = dout("y", [NQ_TOK, D])
    nk_d = dout("nk", [NP_TOK, D])
    nv_d = dout("nv", [NP_TOK, D])
    nst_d = dout("nst", [2, 8, 128, 128])

    es = ExitStack()
    KB = 1024
    BASE = 16640
    TOP = 228992
    BIG = BASE + 30 * KB
    _esz = {F32: 4, BF16: 2}

    class Arena:
        def __init__(self, start, end):
            self.start, self.end, self.cur = start, end, start

        def alloc(self, name, shape, dt):
            n = 1
            for d_ in shape[1:]:
                n *= d_
            nbytes = (n * _esz[dt] + 63) // 64 * 64
            off = self.cur
            assert off + nbytes <= self.end, (name, off, nbytes, self.end)
            self.cur = off + nbytes
            Arena.uid += 1
            return nc.alloc_sbuf_tensor_at(f"s_{name}_{Arena.uid}", list(shape), dt, offset=off).ap()

    Arena.uid = 0
    A_const = Arena(BASE, BIG)
    A_pers = Arena(BIG, BIG + 112 * KB)

    def sb(name, shape, dt=F32, arena=None):
        return (arena or A_const).alloc(name, shape, dt)

    PS = es.enter_context(nc.psum_tensor("PS", [128, 8, 512], F32)).ap()
    psb = [T.b("ps", k) for k in range(8)]
    for _b in psb:
        _b.excl = True
    bank_rr = [0]

    def next_bank():
        k = bank_rr[0]
        bank_rr[0] = (k + 1) % 8
        return k

    hT = sb("hT", [128, 8, NTOK], BF16, A_pers)
    oagT = sb("oagT", [128, 8, NQ_TOK], BF16, A_pers)
    orgT = sb("orgT", [128, 8, NQ_TOK], BF16, A_pers)
    ident = sb("ident", [128, 128])
    cmat_f = sb("cmat_f", [128, 128])
    cmat = sb("cmat", [128, 128], BF16)
    jmat = sb("jmat", [128, 128], BF16)
    ones_b = sb("ones_b", [128, 128], BF16)
    neghalf = sb("neghalf", [128, 512])
    one_col = sb("one_col", [128, 1])
    QD = sb("QD", [128, 8, 128])
    kdcol = sb("kdcol", [128, 16])
    DC = sb("DC", [128, 8, 128])
    CDv = sb("CDv", [128, 8])
    s0 = sb("s0", [128, 8, 128])
    shiftT = sb("shiftT", [128, 8, 2])
    scale1T = sb("scale1T", [128, 8, 2])
    gate_bc = sb("gate_bc", [128, 2, D])
    bgT = sb("bgT", [128, 16])
    bgTh = sb("bgTh", [128, 16])
    neglam = sb("neglam", [128, 1])
    gsc = sb("gsc", [128, 1])
    gngT = sb("gngT", [128, 8])
    gngTh = sb("gngTh", [128, 8])
    sublnT = sb("sublnT", [128, 1])

    B_const = T.b("const")
    B_hT = [T.b("hT", st) for st in range(5)]

    if True:
        ph = Arena(TOP - 32 * KB, TOP)
        condT = sb("condT", [128, 8, 2], F32, ph)
        silT = sb("silT", [128, 8, 2], BF16, ph)
        ctmp = sb("ctmp", [128, 8, 2], F32, ph)
        lamp = sb("lamp", [128, 256], F32, ph)
        ltmp = sb("ltmp", [128, 128], F32, ph)
        lsum = sb("lsum", [128, 2], F32, ph)
        rdsel = sb("rdsel", [128, 8], F32)
        rdrow = sb("rdrow", [128, 16], F32)
        lgsel = sb("lgsel", [128, 8], F32)
        lgrow = sb("lgrow", [128, 16], F32)
        nlgrow = sb("nlgrow", [128, 16], F32)
        lg128 = sb("lg128", [128, 8], F32, ph)
        iq = sb("iq", [128, 128], F32)
        jp = sb("jp", [128, 2], F32)
        diff = sb("diff", [128, 128], F32)
        mx8 = sb("mx8", [128, 1, 128], F32)
        my8 = sb("my8", [128, 1, 128], F32)
        sel2 = sb("sel2", [2, 2, 128], F32, ph)
        modrow = sb("modrow", [2, 3 * D], F32, ph)
        wm = [sb(f"wm{i}", [128, 8, 512], BF16, ph) for i in range(2)]
        B_wm = [T.b("wm", i) for i in range(2)]

        T.dma(sp, [(ident, ident_d), (cmat_f, cmat_d), (s0, s0_d), (condT, condT_d), (bgT, bgT_d),
                   (lamp, lamp_d), (sublnT, sublnT_d), (gngT, gngT_d),
                   (rdsel, rdsel_d), (rdrow, rdrow_d), (iq, iq_d), (jp, jp_d), (diff, diff_d),
                   (mx8, mx8_d.rearrange("p (o i) -> p o i", o=1)), (my8, my8_d.rearrange("p (o i) -> p o i", o=1)), (sel2, sel2_d), (modrow, bmod_d)],
              writes=[B_const])
        B_c2 = T.b("const2")
        T.op(pool, lambda: nc.gpsimd.memset(jmat, 1.0 / 128.0), writes=[B_c2])
        T.op(pool, lambda: nc.gpsimd.memset(ones_b, 1.0), writes=[B_c2])
        T.op(pool, lambda: nc.gpsimd.memset(neghalf, -0.5), writes=[B_c2])
        T.op(pool, lambda: nc.gpsimd.memset(one_col, 1.0), writes=[B_c2])
        T.op(dve, lambda: nc.vector.tensor_copy(out=cmat, in_=cmat_f), reads=[B_const], writes=[B_c2])

        chk('a')
        B_sil = T.b("sil")
        T.op(act, lambda: nc.scalar.activation(out=ctmp, in_=condT, func=AF.Tanh, scale=0.5),
             reads=[B_const], writes=[B_sil])
        T.op(dve, lambda: nc.vector.scalar_tensor_tensor(out=ctmp, in0=ctmp, scalar=1.0, in1=condT,
                                                         op0=ALU.add, op1=ALU.mult),
             reads=[B_sil, B_const], writes=[B_sil])
        T.op(dve, lambda: nc.vector.tensor_scalar(out=silT, in0=ctmp, scalar1=0.5, scalar2=None, op0=ALU.mult),
             reads=[B_sil], writes=[B_sil])

        chk('b')
        B_mod = T.b("modrow")
        wmod_v = wmod_d.rearrange("(kt p) c -> p kt c", p=128)
        for blk in range(6):
            slot = blk % 2
            T.dma(pool, [(wm[slot], wmod_v[:, :, blk * 512:(blk + 1) * 512])], writes=[B_wm[slot]])
            k = next_bank()
            T.group(pe, [(lambda kt=kt, k=k, slot=slot: nc.tensor.matmul(
                PS[0:2, k, :], lhsT=silT[:, kt, :], rhs=wm[slot][:, kt, :], start=(kt == 0), stop=(kt == 7)))
                for kt in range(8)], reads=[B_sil, B_wm[slot]], writes=[psb[k]])
            T.op(dve, lambda k=k, blk=blk: nc.vector.tensor_tensor(
                out=modrow[:, blk * 512:(blk + 1) * 512], in0=PS[0:2, k, :],
                in1=modrow[:, blk * 512:(blk + 1) * 512], op=ALU.add),
                reads=[psb[k], B_const], writes=[B_mod])
        chk('c')
        k = next_bank()
        T.group(pe, [(lambda j=j, k=k: nc.tensor.transpose(
            PS[:, k, 2 * j:2 * j + 2], modrow[0:2, j * 128:(j + 1) * 128], ident[0:2, 0:2]))
            for j in range(16)], reads=[B_mod, B_const], writes=[psb[k]])
        B_modT = T.b("modT")
        T.op(dve, lambda k=k: nc.vector.tensor_copy(
            out=shiftT, in_=PS[:, k, 0:16].rearrange("p (j c) -> p j c", c=2)),
            reads=[psb[k]], writes=[B_modT])
        T.op(dve, lambda k=k: nc.vector.tensor_scalar(
            out=scale1T, in0=PS[:, k, 16:32].rearrange("p (j c) -> p j c", c=2), scalar1=1.0, scalar2=None,
            op0=ALU.add), reads=[psb[k]], writes=[B_modT])
        chk('d')
        B_gate = T.b("gate_bc")
        for ci in range(2):
            for hf in range(2):
                k = next_bank()
                T.op(pe, lambda k=k, ci=ci, hf=hf: nc.tensor.matmul(
                    PS[:, k, :], lhsT=sel2[:, ci, :], rhs=modrow[0:2, 2048 + hf * 512:2048 + (hf + 1) * 512],
                    start=True, stop=True), reads=[B_mod, B_const], writes=[psb[k]])
                T.op(act, lambda k=k, ci=ci, hf=hf: nc.scalar.mul(
                    out=gate_bc[:, ci, hf * 512:(hf + 1) * 512], in_=PS[:, k, :], mul=0.5),
                    reads=[psb[k]], writes=[B_gate])

        chk('e')
        B_lam = T.b("lam")
        T.op(dve, lambda: nc.vector.tensor_tensor(
            out=ltmp.rearrange("p (a b) -> p a b", a=2),
            in0=lamp.rearrange("p (a r b) -> p a r b", a=2, r=2)[:, :, 0, :],
            in1=lamp.rearrange("p (a r b) -> p a r b", a=2, r=2)[:, :, 1, :], op=ALU.mult),
            reads=[B_const], writes=[B_lam])
        T.op(dve, lambda: nc.vector.reduce_sum(out=lsum, in_=ltmp.rearrange("p (a b) -> p a b", a=2), axis=AX.X),
             reads=[B_lam], writes=[B_lam])
        T.op(act, lambda: nc.scalar.activation(out=lsum, in_=lsum, func=AF.Exp), reads=[B_lam], writes=[B_lam])
        T.op(dve, lambda: nc.vector.tensor_tensor(out=neglam, in0=lsum[:, 1:2], in1=lsum[:, 0:1], op=ALU.subtract),
             reads=[B_lam], writes=[B_lam])
        T.op(dve, lambda: nc.vector.tensor_scalar(out=neglam, in0=neglam, scalar1=-LAM_INIT, scalar2=None,
                                                  op0=ALU.add), reads=[B_lam], writes=[B_lam])
        T.op(dve, lambda: nc.vector.tensor_scalar(out=gsc, in0=sublnT, scalar1=(1.0 - LAM_INIT),
                                                  scalar2=None, op0=ALU.mult), reads=[B_const], writes=[B_lam])
        T.op(dve, lambda: nc.vector.tensor_scalar(out=gngTh, in0=gngT, scalar1=0.5, scalar2=None, op0=ALU.mult),
             reads=[B_const], writes=[B_lam])
        T.op(dve, lambda: nc.vector.tensor_scalar(out=bgTh, in0=bgT, scalar1=0.5, scalar2=None, op0=ALU.mult),
             reads=[B_const], writes=[B_lam])

        B_dec = T.b("dec")
        ph0 = ph

        def emit_decay_tables():
            chk('f')
            B_dec = T.b("dec")

            def log1p_neg(dst, src, n):
                T.op(act, lambda: nc.scalar.activation(out=src, in_=src, func=AF.Exp, scale=math.log(2.0)),
                     reads=[B_const, B_dec], writes=[B_dec])
                T.op(dve, lambda: nc.vector.tensor_scalar(out=dst, in0=src, scalar1=1.0 / 8.0, scalar2=None,
                                                          op0=ALU.mult), reads=[B_dec], writes=[B_dec])
                for c in (1.0 / 7.0, 1.0 / 6.0, 1.0 / 5.0, 0.25, 1.0 / 3.0, 0.5, 1.0):
                    T.op(dve, lambda c=c: nc.vector.scalar_tensor_tensor(
                        out=dst, in0=dst, scalar=c, in1=src, op0=ALU.add, op1=ALU.mult),
                        reads=[B_dec], writes=[B_dec])
                T.op(dve, lambda: nc.vector.tensor_scalar(out=dst, in0=dst, scalar1=-1.0, scalar2=None, op0=ALU.mult),
                     reads=[B_dec], writes=[B_dec])

            chk('g')
            log1p_neg(lgsel, rdsel, 8)
            log1p_neg(lgrow, rdrow, 16)
            T.op(dve, lambda: nc.vector.tensor_scalar(out=nlgrow, in0=lgrow, scalar1=-1.0, scalar2=None, op0=ALU.mult),
                 reads=[B_dec], writes=[B_dec])
            T.op(act, lambda: nc.scalar.activation(out=CDv, in_=lgsel, func=AF.Exp, scale=128.0),
                 reads=[B_dec], writes=[B_dec])
            chk('h')
            ln8 = math.log(0.125)
            lnb = sb("lnb", [128, 1], F32)
            T.op(pool, lambda: nc.gpsimd.memset(lnb, ln8), writes=[B_dec])
            T.op(dve, lambda: nc.vector.tensor_scalar(out=kdcol[:, 0:8], in0=lgrow[:, 0:8], scalar1=jp[:, 0:1], scalar2=None,
                                                      op0=ALU.mult), reads=[B_dec, B_const], writes=[B_dec])
            T.op(dve, lambda: nc.vector.tensor_scalar(out=kdcol[:, 8:16], in0=lgrow[:, 8:16], scalar1=jp[:, 1:2],
                                                      scalar2=None, op0=ALU.mult), reads=[B_dec, B_const], writes=[B_dec])
            T.op(act, lambda: nc.scalar.activation(out=kdcol, in_=kdcol, func=AF.Exp, bias=lnb),
                 reads=[B_dec], writes=[B_dec])
            chk('i')
            for h in range(8):
                T.op(act, lambda h=h: nc.scalar.activation(out=DC[:, h, :], in_=diff, func=AF.Exp,
                                                           scale=lgrow[:, h:h + 1]),
                     reads=[B_dec, B_const], writes=[B_dec])
                T.op(act, lambda h=h: nc.scalar.activation(out=QD[:, h, :], in_=diff, func=AF.Exp,
                                                           scale=nlgrow[:, 8 + h:9 + h]),
                     reads=[B_dec, B_const], writes=[B_dec])
            T.op(dve, lambda: nc.vector.tensor_tensor(out=DC, in0=DC, in1=mx8.to_broadcast([128, 8, 128]),
                                                      op=ALU.mult), reads=[B_dec, B_const], writes=[B_dec])
            T.op(dve, lambda: nc.vector.tensor_tensor(out=QD, in0=QD, in1=my8.to_broadcast([128, 8, 128]),
                                                      op=ALU.mult), reads=[B_dec, B_const], writes=[B_dec])
            T.op(dve, lambda: nc.vector.tensor_tensor(out=DC, in0=DC, in1=QD, op=ALU.add),
                 reads=[B_dec], writes=[B_dec])
            for h in range(8):
                T.op(act, lambda h=h: nc.scalar.activation(out=QD[:, h, :], in_=iq, func=AF.Exp,
                                                           scale=lgsel[:, h:h + 1]),
                     reads=[B_dec, B_const], writes=[B_dec])

        if STOP == 0:
            T.finish()
            return nc

    if True:
        ph2 = Arena(BIG + 64 * KB, TOP - 65 * KB)
        ropeC = sb("ropeC", [128, 2048], F32, ph2)
        ropeS = sb("ropeS", [128, 2048], F32, ph2)
        ckT = sb("ckT", [128, 8, 256], BF16, ph2)
        cvb = sb("cvb", [128, 2, D], BF16, ph2)
        WA0 = sb("WA0", [128, 8, WA_COLS], BF16, ph2)
        _wa1_off = ph2.cur
        cst = sb("cst", [128, D], F32, ph2)
        ph2.cur = _wa1_off
        WA1 = sb("WA1", [128, 8, WA_COLS], BF16, ph2)
        WA = [WA0, WA1]
        B_rope = T.b("rope")
        B_WA = [T.b("WA", i) for i in range(2)]
        B_ctx = T.b("ctx")

        def load_WA(h):
            v = wa_d[h].rearrange("(kt p) c -> p kt c", p=128)
            T.dma(pool, [(WA[h % 2], v)], writes=[B_WA[h % 2]])

        load_WA(0)

    if True:
        ph = Arena(TOP - 65 * KB, TOP - 32 * KB)
        NB1 = 4
        XT = [sb(f"XT{i}", [128, D], F32, ph) for i in range(NB1)]
        XN = [sb(f"XN{i}", [128, D], F32, ph) for i in range(NB1)]
        st6 = [sb(f"st6_{i}", [128, 2, 6], F32, ph) for i in range(NB1)]
        mv = [sb(f"mv{i}", [128, 2], F32, ph) for i in range(NB1)]
        rs = [sb(f"rs{i}", [128, 1], F32, ph) for i in range(NB1)]
        B_xt = [T.b("XT", i) for i in range(NB1)]
        B_xn = [T.b("XN", i) for i in range(NB1)]
        B_stt = [T.b("stt", i) for i in range(NB1)]

        def p1_A(g):
            s = g % NB1
            T.dma(sp, [(XT[s], xs[g * 128:(g + 1) * 128, :])], writes=[B_xt[s]])
            for c2 in range(2):
                T.op(dve, lambda c2=c2: nc.vector.bn_stats(out=st6[s][:, c2, :],
                                                           in_=XT[s][:, c2 * 512:(c2 + 1) * 512]),
                     reads=[B_xt[s]], writes=[B_stt[s]])
            T.op(dve, lambda: nc.vector.bn_aggr(out=mv[s], in_=st6[s]), reads=[B_stt[s]], writes=[B_stt[s]])
            T.op(dve, lambda: nc.vector.tensor_scalar(out=rs[s], in0=mv[s][:, 1:2], scalar1=MOD_EPS,
                                                      scalar2=None, op0=ALU.add),
                 reads=[B_stt[s]], writes=[B_stt[s]])
            T.op(act, lambda: nc.scalar.activation(out=rs[s], in_=rs[s], func=AF.Sqrt),
                 reads=[B_stt[s]], writes=[B_stt[s]])

        def p1_B(g):
            s = g % NB1
            T.op(dve, lambda: nc.vector.reciprocal(out=rs[s], in_=rs[s]), reads=[B_stt[s]], writes=[B_stt[s]])
            T.op(dve, lambda: nc.vector.scalar_tensor_tensor(
                out=mv[s][:, 1:2], in0=mv[s][:, 0:1], scalar=-1.0, in1=rs[s], op0=ALU.mult, op1=ALU.mult),
                reads=[B_stt[s]], writes=[B_stt[s]])
            T.op(act, lambda: nc.scalar.activation(out=XN[s], in_=XT[s], func=AF.Identity,
                                                   scale=rs[s], bias=mv[s][:, 1:2]),
                 reads=[B_xt[s], B_stt[s]], writes=[B_xn[s]])

        def p1_C(g):
            s = g % NB1
            st, tt = g // 4, g % 4
            ci = 0 if st == 0 else 1
            for kt in range(8):
                T.op(pe, lambda kt=kt: nc.tensor.transpose(
                    PS[:, kt, tt * 128:(tt + 1) * 128], XN[s][:, kt * 128:(kt + 1) * 128], ident),
                    reads=[B_xn[s], B_const], writes=[psb[kt]])
            if tt == 3:
                for kt in range(8):
                    if kt % 2 == 0:
                        T.op(dve, lambda kt=kt: nc.vector.tensor_scalar(
                            out=hT[:, kt, st * 512:(st + 1) * 512], in0=PS[:, kt, :],
                            scalar1=scale1T[:, kt, ci:ci + 1], scalar2=shiftT[:, kt, ci:ci + 1],
                            op0=ALU.mult, op1=ALU.add), reads=[psb[kt], B_modT], writes=[B_hT[st]])
                    else:
                        T.op(act, lambda kt=kt: nc.scalar.activation(
                            out=hT[:, kt, st * 512:(st + 1) * 512], in_=PS[:, kt, :], func=AF.Identity,
                            scale=scale1T[:, kt, ci:ci + 1], bias=shiftT[:, kt, ci:ci + 1]),
                            reads=[psb[kt], B_modT], writes=[B_hT[st]])

        for i_ in range(20 + 2):
            if i_ < 20:
                p1_A(i_)
            if 0 <= i_ - 1 < 20:
                p1_B(i_ - 1)
            if 0 <= i_ - 2 < 20:
                p1_C(i_ - 2)
        T.dma(sp, [(ropeC, ropeC_d), (ropeS, ropeS_d)], writes=[B_rope])
        for t in range(2):
            T.dma(sp, [(cst, ck[t * 128:(t + 1) * 128, :])], writes=[B_WA[1]])
            for hq in range(2):
                k = next_bank()
                T.group(pe, [(lambda h=h, k=k: nc.tensor.transpose(
                    PS[:, k, (h % 4) * 128:(h % 4 + 1) * 128], cst[:, h * 128:(h + 1) * 128], ident))
                    for h in range(hq * 4, hq * 4 + 4)], reads=[B_WA[1], B_const], writes=[psb[k]])
                T.op(act, lambda k=k, hq=hq, t=t: nc.scalar.activation(
                    out=ckT[:, hq * 4:(hq + 1) * 4, t * 128:(t + 1) * 128],
                    in_=PS[:, k, :].rearrange("p (h j) -> p h j", h=4), func=AF.Copy),
                    reads=[psb[k]], writes=[B_ctx])
        for t in range(2):
            T.dma(sp, [(cst, cv[t * 128:(t + 1) * 128, :])], writes=[B_WA[1]])
            T.op(dve, lambda t=t: nc.vector.tensor_copy(out=cvb[:, t, :], in_=cst), reads=[B_WA[1]], writes=[B_ctx])
        load_WA(1)

        T.barrier(skip=[B_rope, B_WA[1]])
        if STOP == 1:
            T.finish()
            return nc

    if True:
        ph = ph2
        ph.end = TOP
        qT = [sb(f"qT{i}", [128, NQ_TOK], BF16, ph) for i in range(2)]
        kT = [sb(f"kT{i}", [128, NTOK], BF16, ph) for i in range(2)]
        vS = [sb(f"vS{i}", [128, 20, 128], BF16, ph) for i in range(2)]
        tza = [sb(f"tza{i}", [128, NQ_TOK], BF16, ph) for i in range(2)]
        E = [sb(f"E{i}", [128, 2, 256], BF16, ph) for i in range(3)]
        r1 = [sb(f"r1_{i}", [128, 512], F32, ph) for i in range(2)]
        r2 = [sb(f"r2_{i}", [128, 512], F32, ph) for i in range(2)]
        tmpz = [sb(f"tmpz{i}", [128, 512], F32, ph) for i in range(2)]
        stage = [sb(f"stage{i}", [128, 256], F32, ph) for i in range(2)]
        dn = sb("dn", [128, 2, 256], F32, ph)
        o2 = sb("o2", [128, 256], F32, ph)
        osq_ = [sb(f"osq{i}", [128, 256], BF16, ph) for i in range(2)]
        dd_ = [sb(f"dd{i}", [128, 256], F32, ph) for i in range(2)]
        o_all_ = [sb(f"o_all{i}", [128, 512], F32, ph) for i in range(2)]
        v_all_ = [sb(f"v_all{i}", [128, 512], F32, ph) for i in range(2)]

        B_qT = [[T.b("qT", i, g) for g in range(3)] for i in range(2)]
        B_kT = [[T.b("kT", i, g) for g in range(5)] for i in range(2)]
        B_vS = [[T.b("vS", i, g) for g in range(5)] for i in range(2)]
        B_tza = [[T.b("tza", i, g) for g in range(3)] for i in range(2)]
        B_E = [T.b("E", i) for i in range(3)]
        B_r1 = [T.b("r1", i) for i in range(2)]
        B_r2 = [T.b("r2", i) for i in range(2)]
        B_tz = [T.b("tmpz", i) for i in range(2)]
        B_stage = [T.b("stage", i) for i in range(2)]
        B_dn = T.b("dn")
        B_o2 = T.b("o2")
        B_osq_ = [T.b("osq", i) for i in range(2)]
        B_dd_ = [T.b("dd", i) for i in range(2)]
        B_oall_ = [[T.b("o_all", bp, i) for i in range(2)] for bp in range(2)]
        B_vall_ = [[T.b("v_all", bp, i) for i in range(2)] for bp in range(2)]
        gcount = [0]
        bcount = [0]
        B_oag = [[T.b("oag", h, g) for g in range(3)] for h in range(8)]

        chk('p2a')
        rr = [0]

        def rot2():
            rr[0] ^= 1
            return rr[0]

        pbank = [0]
        PB = [5, 6, 7]

        def pool_bank():
            pbank[0] = (pbank[0] + 1) % 3
            return PB[pbank[0]]

        Q2 = [sb(f"Q2_{i}", [128, 512], BF16, ph) for i in range(2)]
        B_Q2 = [T.b("Q2", i) for i in range(2)]
        for q_ in range(2):
            T.op(pool, lambda q_=q_: nc.gpsimd.memset(Q2[q_], 0.0), writes=[B_Q2[q_]])
        q2c = [0]
        q2carry = {}

        def build_Q2(i, qc0, g):
            q2c[0] ^= 1
            u = q2c[0]
            T.op(pool, lambda: nc.gpsimd.tensor_copy(out=Q2[u][0:64, 0:256], in_=qT[i][0:64, qc0:qc0 + 256]),
                 reads=[B_qT[i][g]], writes=[B_Q2[u]])
            T.op(pool, lambda: nc.gpsimd.tensor_copy(out=Q2[u][64:128, 256:512], in_=qT[i][64:128, qc0:qc0 + 256]),
                 reads=[B_qT[i][g]], writes=[B_Q2[u]])
            return u

        def mk_chunks(h):
            i = h % 2
            W = WA[i]
            ch = []

            def fgroup(c0, st, ncols=128):
                k = pool_bank()
                T.group(pe, [(lambda kt=kt, k=k: nc.tensor.matmul(
                    PS[0:ncols, k, :], lhsT=W[:, kt, c0:c0 + ncols], rhs=hT[:, kt, st * 512:(st + 1) * 512],
                    start=(kt == 0), stop=(kt == 7))) for kt in range(8)],
                    reads=[B_WA[i], B_hT[st]], writes=[psb[k]])
                return k

            def plain(c0, dst, dbuf):
                k = fgroup(c0, 0)
                T.op(dve, lambda k=k: nc.vector.tensor_copy(out=dst, in_=PS[:, k, :]),
                     reads=[psb[k]], writes=[dbuf])

            def rope_evac(c_plain, c_sw, st, dst, dbuf):
                ka_ = fgroup(c_plain, st)
                kb_ = fgroup(c_sw, st)
                j = rot2()
                rc = (st - 1) * 512
                T.op(dve, lambda: nc.vector.tensor_tensor(out=r1[j], in0=PS[:, ka_, :],
                                                          in1=ropeC[:, rc:rc + 512], op=ALU.mult),
                     reads=[psb[ka_], B_rope], writes=[B_r1[j]])
                T.op(dve, lambda: nc.vector.tensor_tensor(out=r2[j], in0=PS[:, kb_, :],
                                                          in1=ropeS[:, rc:rc + 512], op=ALU.mult),
                     reads=[psb[kb_], B_rope], writes=[B_r2[j]])
                T.op(pool, lambda: nc.gpsimd.tensor_tensor(out=dst, in0=r1[j], in1=r2[j], op=ALU.add),
                     reads=[B_r1[j], B_r2[j]], writes=[dbuf])

            def zgate(st):
                k = fgroup(384, st)
                j = rot2()
                T.op(act, lambda: nc.scalar.activation(out=tmpz[j], in_=PS[:, k, :], func=AF.Exp, scale=-1.0),
                     reads=[psb[k]], writes=[B_tz[j]])
                T.op(dve, lambda: nc.vector.tensor_copy(out=r2[j], in_=PS[:, k, :]),
                     reads=[psb[k]], writes=[B_r2[j]])
                T.op(act, lambda: nc.scalar.activation(out=tmpz[j], in_=tmpz[j], func=AF.Ln, bias=one_col),
                     reads=[B_tz[j], B_c2], writes=[B_tz[j]])
                T.op(act, lambda: nc.scalar.activation(out=tmpz[j], in_=tmpz[j], func=AF.Exp, scale=-1.0),
                     reads=[B_tz[j]], writes=[B_tz[j]])
                T.op(pool, lambda: nc.gpsimd.tensor_tensor(
                    out=tza[i][:, st * 512:(st + 1) * 512], in0=r2[j], in1=tmpz[j], op=ALU.mult),
                    reads=[B_tz[j], B_r2[j]], writes=[B_tza[i][st]])

            def ptile(tt):
                k = pool_bank()
                T.group(pe, [(lambda kt=kt: nc.tensor.matmul(
                    PS[:, k, 0:256], lhsT=hT[:, kt, tt * 128:(tt + 1) * 128], rhs=W[:, kt, 512:768],
                    start=(kt == 0), stop=(kt == 7))) for kt in range(8)],
                    reads=[B_WA[i], B_hT[0]], writes=[psb[k]])
                j = rot2()
                T.op(dve, lambda: nc.vector.tensor_copy(out=stage[j], in_=PS[:, k, 0:256]),
                     reads=[psb[k]], writes=[B_stage[j]])
                T.op(dve, lambda: nc.vector.tensor_copy(out=vS[i][:, tt, :], in_=PS[:, k, 128:256]),
                     reads=[psb[k]], writes=[B_vS[i][0]])
                T.dma(sp, [(nk_d[tt * 128:(tt + 1) * 128, h * 128:(h + 1) * 128], stage[j][:, 0:128]),
                           (nv_d[tt * 128:(tt + 1) * 128, h * 128:(h + 1) * 128], stage[j][:, 128:256])],
                      reads=[B_stage[j]])

            def vtiles(st):
                k = pool_bank()
                fns = []
                for t4 in range(4):
                    tt = st * 4 + t4
                    for kt in range(8):
                        fns.append(lambda kt=kt, tt=tt, t4=t4: nc.tensor.matmul(
                            PS[:, k, t4 * 128:(t4 + 1) * 128], lhsT=hT[:, kt, tt * 128:(tt + 1) * 128],
                            rhs=W[:, kt, 640:768], start=(kt == 0), stop=(kt == 7)))
                T.group(pe, fns, reads=[B_WA[i], B_hT[st]], writes=[psb[k]])
                T.op(dve, lambda: nc.vector.tensor_copy(
                    out=vS[i][:, st * 4:(st + 1) * 4, :], in_=PS[:, k, :].rearrange("p (t d) -> p t d", t=4)),
                    reads=[psb[k]], writes=[B_vS[i][st]])

            def seq(*fs):
                return lambda: [f() for f in fs]

            qrope = lambda st: (lambda: rope_evac(0, 128, st, qT[i][:, st * 512:(st + 1) * 512], B_qT[i][st]))
            krope = lambda st: (lambda: rope_evac(512, 256, st, kT[i][:, st * 512:(st + 1) * 512], B_kT[i][st]))
            ch.append(seq(lambda: plain(0, qT[i][:, 0:512], B_qT[i][0]), lambda: plain(512, kT[i][:, 0:512], B_kT[i][0])))
            ch.append(seq(lambda: ptile(0), lambda: ptile(1)))
            ch.append(seq(lambda: ptile(2), lambda: ptile(3)))
            ch.append(seq(lambda: zgate(0), qrope(1)))
            ch.append(qrope(2))
            ch.append(seq(lambda: zgate(1), krope(1)))
            ch.append(krope(2))
            ch.append(seq(lambda: zgate(2), krope(3)))
            ch.append(krope(4))
            ch.append(seq(lambda: vtiles(1), lambda: vtiles(2)))
            ch.append(seq(lambda: vtiles(3), lambda: vtiles(4)))
            return ch

        deferred = []
        act_heavy = []
        stepc = [0]

        def run_deferred(force=False):
            while deferred and (force or deferred[0][0] <= stepc[0]):
                deferred.pop(0)[1]()

        def attention_head(h, pend):
            i = h % 2
            groups = []
            for pb in range(2):
                keys = [(kT[i][:, pb * 256 + t * 128:pb * 256 + (t + 1) * 128], vS[i][:, pb * 2 + t, :],
                         (B_kT[i][0], B_vS[i][0])) for t in range(2)]
                groups.append((pb * 256, keys, 0, pb))
            skeys = [(ckT[:, h, t * 128:(t + 1) * 128], cvb[:, t, h * 128:(h + 1) * 128], (B_ctx,))
                     for t in range(2)]
            for t in range(16):
                st = 1 + t // 4
                skeys.append((kT[i][:, 512 + t * 128:512 + (t + 1) * 128], vS[i][:, 4 + t, :],
                              (B_kT[i][st], B_vS[i][st])))
            for qt in range(4):
                groups.append((512 + qt * 256, skeys, 1 + qt // 2, qt % 2))
            nsteps = sum(len(g_[1]) for g_ in groups)
            done = 0
            emitted = 0
            nq = 256
            nextq = q2carry.pop('q') if 'q' in q2carry else build_Q2(i, groups[0][0], groups[0][2])
            for gix, (qc0, keys, g, half) in enumerate(groups):
                n = len(keys)
                qu = nextq

                def S(kt):
                    kap, vap, kb = keys[kt]
                    T.op(pe, lambda: nc.tensor.matmul(PS[:, kt % 3, :], lhsT=kap, rhs=Q2[qu], start=True, stop=True),
                         reads=list(kb) + [B_Q2[qu]], writes=[psb[kt % 3]])

                S(0)
                if n > 1:
                    S(1)
                if gix + 1 < len(groups):
                    nextq = build_Q2(i, groups[gix + 1][0], groups[gix + 1][2])
                elif h + 1 < 8:
                    q2carry['q'] = build_Q2((h + 1) % 2, 0, 0)
                for kt in range(n):
                    if kt + 2 < n:
                        S(kt + 2)
                    a = kt % 3
                    e = kt % 3
                    T.op(act, lambda: nc.scalar.activation(
                        out=E[e], in_=PS[:, a, :].rearrange("p (m q) -> p m q", m=2), func=AF.Exp, scale=0.125),
                        reads=[psb[a]], writes=[B_E[e]])
                    kap, vap, kb = keys[kt]
                    fns = []
                    for m in range(2):
                        fns.append(lambda m=m: nc.tensor.matmul(
                            PS[:, 3, m * nq:(m + 1) * nq], lhsT=vap, rhs=E[e][:, m, :], start=(kt == 0 and m == 0),
                            stop=(kt == n - 1), skip_group_check=True))
                        fns.append(lambda m=m: nc.tensor.matmul(
                            PS[:, 4, m * nq:(m + 1) * nq], lhsT=ones_b, rhs=E[e][:, m, :], start=(kt == 0 and m == 0),
                            stop=(kt == n - 1), skip_group_check=True))
                    T.group(pe, fns, reads=[B_E[e]] + list(kb) + [B_c2], writes=[psb[3], psb[4]])
                    done += 1
                    stepc[0] += 1
                    run_deferred()
                    while emitted < len(pend) and emitted * nsteps < done * len(pend):
                        while act_heavy and act_heavy[0][0] <= stepc[0] and not any(d_[2] <= act_heavy[0][2] for d_ in deferred):
                            act_heavy.pop(0)[1]()
                        pend[emitted]()
                        emitted += 1
                    if not pend:
                        while act_heavy and act_heavy[0][0] <= stepc[0] and not any(d_[2] <= act_heavy[0][2] for d_ in deferred):
                            act_heavy.pop(0)[1]()
                gi = gcount[0]
                gcount[0] += 1
                gp = gi % 2
                bp = bcount[0] % 2
                if half == 1:
                    bcount[0] += 1
                while deferred and deferred[0][2] <= gi - 2:
                    deferred.pop(0)[1]()
                while act_heavy and act_heavy[0][2] <= gi - 2:
                    act_heavy.pop(0)[1]()
                while deferred and deferred[0][2] <= gi - 2:
                    deferred.pop(0)[1]()
                for d_ in [d_ for d_ in deferred if d_[2] == gi - 1 and len(d_) > 3 and d_[3] == "st1"]:
                    deferred.remove(d_)
                    d_[1]()
                o_all, v_all = o_all_[bp], v_all_[bp]
                B_oall, B_vall = B_oall_[bp], B_vall_[bp]
                osq, dd, B_osq, B_dd = osq_[gp], dd_[gp], B_osq_[gp], B_dd_[gp]
                oa = o_all[:, half * nq:(half + 1) * nq]
                va = v_all[:, half * nq:(half + 1) * nq]
                T.op(dve, lambda: nc.vector.tensor_scalar(
                    out=dn, in0=PS[:, 4, :].rearrange("p (m q) -> p m q", m=2), scalar1=2.0 ** -12, scalar2=None,
                    op0=ALU.mult), reads=[psb[4]], writes=[B_dn])
                T.op(dve, lambda: nc.vector.tensor_tensor(out=oa, in0=PS[:, 3, 0:nq], in1=dn[:, 1, :], op=ALU.mult),
                     reads=[psb[3], B_dn], writes=[B_oall[half]])
                T.op(dve, lambda: nc.vector.tensor_tensor(out=o2, in0=PS[:, 3, nq:2 * nq], in1=dn[:, 0, :],
                                                          op=ALU.mult), reads=[psb[3], B_dn], writes=[B_o2])
                T.op(pool, lambda: nc.gpsimd.tensor_tensor(out=dd, in0=dn[:, 0, :], in1=dn[:, 1, :], op=ALU.mult),
                     reads=[B_dn], writes=[B_dd])
                if emitted < len(pend):
                    pend[emitted]()
                    emitted += 1

                def st1(oa=oa, half=half, B_oall=B_oall, dd=dd, B_dd=B_dd):
                    T.op(dve, lambda: nc.vector.scalar_tensor_tensor(
                        out=oa, in0=o2, scalar=neglam, in1=oa, op0=ALU.mult, op1=ALU.add),
                        reads=[B_o2, B_oall[half], B_lam], writes=[B_oall[half]])
                    T.op(pool, lambda: nc.gpsimd.tensor_tensor(out=dd, in0=dd, in1=dd, op=ALU.mult),
                         reads=[B_dd], writes=[B_dd])

                def st2(oa=oa, half=half, B_oall=B_oall, osq=osq, B_osq=B_osq):
                    T.op(dve, lambda: nc.vector.tensor_tensor(out=osq, in0=oa, in1=oa, op=ALU.mult),
                         reads=[B_oall[half]], writes=[B_osq])

                hold = {}

                def st3(osq=osq, B_osq=B_osq, hold=hold):
                    k = pool_bank()
                    hold["k"] = k
                    T.op(pe, lambda: nc.tensor.matmul(PS[:, k, 0:nq], lhsT=jmat, rhs=osq, start=True, stop=True),
                         reads=[B_osq, B_c2], writes=[psb[k]])

                def st4(va=va, half=half, B_vall=B_vall, dd=dd, B_dd=B_dd, hold=hold):
                    k = hold["k"]
                    T.op(dve, lambda: nc.vector.scalar_tensor_tensor(
                        out=va, in0=dd, scalar=LN_EPS * (2.0 ** 24), in1=PS[:, k, 0:nq], op0=ALU.mult, op1=ALU.add),
                        reads=[B_dd, psb[k]], writes=[B_vall[half]])

                deferred.append((stepc[0] + 1, st1, gi, "st1"))
                deferred.append((stepc[0] + 2, st2, gi))
                deferred.append((stepc[0] + 3, st3, gi))
                deferred.append((stepc[0] + 5, st4, gi))
                if half == 1:
                    def b1(v_all=v_all, B_vall=B_vall):
                        T.op(act, lambda: nc.scalar.activation(out=v_all, in_=v_all, func=AF.Ln),
                             reads=B_vall, writes=B_vall)
                        T.op(act, lambda: nc.scalar.activation(out=v_all, in_=v_all, func=AF.Exp, scale=-0.5),
                             reads=B_vall, writes=B_vall)

                    def b2(v_all=v_all, B_vall=B_vall, g=g, i=i):
                        c0 = g * 512
                        T.op(pool, lambda: nc.gpsimd.tensor_tensor(out=v_all, in0=v_all, in1=tza[i][:, c0:c0 + 512],
                                                                   op=ALU.mult),
                             reads=B_vall + [B_tza[i][g]], writes=B_vall)

                    def b3(v_all=v_all, B_vall=B_vall, o_all=o_all, B_oall=B_oall, g=g, h=h):
                        c0 = g * 512
                        T.op(dve, lambda: nc.vector.scalar_tensor_tensor(
                            out=oagT[:, h, c0:c0 + 512], in0=o_all, scalar=gsc, in1=v_all, op0=ALU.mult,
                            op1=ALU.mult), reads=B_oall + B_vall + [B_lam], writes=[B_oag[h][g]])

                    def b1x(b1=b1, b2=b2, b3=b3, gi=gi):
                        b1()
                        deferred.append((stepc[0] + 3, b2, gi))
                        deferred.append((stepc[0] + 5, b3, gi))
                    act_heavy.append((stepc[0] + 7, b1x, gi))
            while emitted < len(pend):
                pend[emitted]()
                emitted += 1

        ph3pre = Arena(BIG + 88 * KB, BIG + 88 * KB + 2 * 9216)
        WR = [sb(f"WR{i}", [128, 8, WR_COLS], BF16, ph3pre) for i in range(2)]
        B_WR = [T.b("WR", i) for i in range(2)]

        def load_WR(h, extra=()):
            v = wr_d[h].rearrange("(kt p) c -> p kt c", p=128)
            T.dma(pool, [(WR[h % 2], v)], writes=[B_WR[h % 2]] + list(extra))

        for c_ in mk_chunks(0):
            c_()
        emit_decay_tables()
        for h in range(8):
            if h == 7:
                load_WR(0, extra=[B_WA[0], B_WA[1]])
                load_WR(1, extra=[B_WA[0], B_WA[1]])
            pend = mk_chunks(h + 1) if h + 1 < 8 else []
            if h + 2 < 8:
                load_WA(h + 2)
            attention_head(h, pend)
        run_deferred(force=True)
        while act_heavy:
            act_heavy.pop(0)[1]()
        run_deferred(force=True)
        T.barrier()
        if STOP == 2:
            T.finish()
            return nc

    if True:
        ph = Arena(BIG + 88 * KB + 2 * 9216, TOP)
        qpl = [sb(f"qpl{i}", [64, NQ_TOK], BF16, ph) for i in range(2)]
        qdT = [sb(f"qdT{i}", [128, NQ_TOK], BF16, ph) for i in range(2)]
        krT = [sb(f"krT{i}", [64, NQ_TOK], BF16, ph) for i in range(2)]
        kd = [sb(f"kd{i}", [128, 20, 128], BF16, ph) for i in range(2)]
        vr = [sb(f"vr{i}", [128, 20, 128], BF16, ph) for i in range(2)]
        tzr = [sb(f"tzr{i}", [128, NQ_TOK], BF16, ph) for i in range(2)]
        tmpz = sb("tmpzr", [128, 512], F32, ph)
        KDh = [sb(f"KDh{i}", [128, 1, 128], F32, ph) for i in range(2)]
        B_KDh = [T.b("KDh", i) for i in range(2)]
        zc = sb("zcr", [128, 512], F32, ph)
        S_ = sb("stS", [128, 12, 128], F32, ph)
        stB = sb("stB", [128, 12, 128], BF16, ph)
        yc = sb("ycar", [128, 2, 128], F32, ph)
        fin = [sb(f"fin{i}", [128, 2, 128], F32, ph) for i in range(2)]
        innT = [sb(f"innT{i}", [128, 4, 128], BF16, ph) for i in range(3)]
        ob = [t_.rearrange("p c j -> p (c j)") for t_ in innT]
        sqb = ob
        rstd = [sb(f"rstd3_{i}", [128, 512], F32, ph) for i in range(3)]

        B_qpl = [[T.b("qpl", i, g) for g in range(3)] for i in range(2)]
        B_qd = [[T.b("qd", i, g) for g in range(3)] for i in range(2)]
        B_krT = [[T.b("krT", i, g) for g in range(3)] for i in range(2)]
        B_kd = [[T.b("kd", i, g) for g in range(5)] for i in range(2)]
        B_vr = [[T.b("vr", i, g) for g in range(5)] for i in range(2)]
        B_tzr = [[T.b("tzr", i, g) for g in range(3)] for i in range(2)]
        B_tz = T.b("tmpzr")
        B_zc = T.b("zcr")
        B_st = T.b("stS")
        B_stb = T.b("stB")
        B_fin = [T.b("fin", i) for i in range(2)]
        B_inn = [T.b("innT", i) for i in range(3)]
        B_ob = B_inn
        B_sqb = B_inn
        B_rstd = [T.b("rstd3", i) for i in range(3)]
        B_org = [[T.b("org", h, g) for g in range(3)] for h in range(8)]

        pb3 = [0]
        PB3 = [5, 6, 7]

        def pool_bank3():
            pb3[0] = (pb3[0] + 1) % 3
            return PB3[pb3[0]]

        def mk_chunks3(h):
            i = h % 2
            W = WR[i]

            def fgroup(c0, st, ncols=128):
                k = pool_bank3()
                T.group(pe, [(lambda kt=kt: nc.tensor.matmul(
                    PS[0:ncols, k, :], lhsT=W[:, kt, c0:c0 + ncols], rhs=hT[:, kt, st * 512:(st + 1) * 512],
                    start=(kt == 0), stop=(kt == 7))) for kt in range(8)],
                    reads=[B_WR[i], B_hT[st]], writes=[psb[k]])
                return k

            def qq(st):
                k = fgroup(0, st)
                T.op(dve, lambda: nc.vector.tensor_copy(out=qpl[i][:, st * 512:(st + 1) * 512], in_=PS[0:64, k, :]),
                     reads=[psb[k]], writes=[B_qpl[i][st]])
                T.op(dve, lambda: nc.vector.tensor_tensor(
                    out=qdT[i][:, st * 512:(st + 1) * 512].rearrange("p (c j) -> p c j", c=4),
                    in0=PS[:, k, :].rearrange("p (c j) -> p c j", c=4),
                    in1=QD[:, h:h + 1, :].to_broadcast([128, 4, 128]), op=ALU.mult),
                    reads=[psb[k], B_dec], writes=[B_qd[i][st]])

            def kr(st):
                k = fgroup(256, st, 64)
                T.op(act, lambda: nc.scalar.activation(out=krT[i][:, st * 512:(st + 1) * 512], in_=PS[0:64, k, :],
                                                       func=AF.Copy), reads=[psb[k]], writes=[B_krT[i][st]])

            def zr(st):
                k = fgroup(128, st)
                T.op(act, lambda: nc.scalar.activation(out=tmpz, in_=PS[:, k, :], func=AF.Exp, scale=-1.0),
                     reads=[psb[k]], writes=[B_tz])
                T.op(dve, lambda: nc.vector.tensor_copy(out=zc, in_=PS[:, k, :]), reads=[psb[k]], writes=[B_zc])
                T.op(act, lambda: nc.scalar.activation(out=tmpz, in_=tmpz, func=AF.Ln, bias=one_col),
                     reads=[B_tz, B_c2], writes=[B_tz])
                T.op(act, lambda: nc.scalar.activation(out=tmpz, in_=tmpz, func=AF.Exp, scale=-1.0),
                     reads=[B_tz], writes=[B_tz])
                T.op(pool, lambda: nc.gpsimd.tensor_tensor(out=tzr[i][:, st * 512:(st + 1) * 512], in0=zc, in1=tmpz,
                                                           op=ALU.mult),
                     reads=[B_tz, B_zc], writes=[B_tzr[i][st]])

            def tpair(tp):
                k = pool_bank3()
                st = tp // 2
                fns = []
                for t2 in range(2):
                    tt = tp * 2 + t2
                    for kt in range(8):
                        fns.append(lambda kt=kt, tt=tt, t2=t2: nc.tensor.matmul(
                            PS[:, k, t2 * 256:(t2 + 1) * 256], lhsT=hT[:, kt, tt * 128:(tt + 1) * 128],
                            rhs=W[:, kt, 320:576], start=(kt == 0), stop=(kt == 7)))
                T.group(pe, fns, reads=[B_WR[i], B_hT[st]], writes=[psb[k]])
                pv = PS[:, k, :].rearrange("p (t c) -> p t c", t=2)
                T.op(dve, lambda: nc.vector.tensor_tensor(
                    out=kd[i][:, tp * 2:tp * 2 + 2, :], in0=pv[:, :, 0:128],
                    in1=KDh[i].to_broadcast([128, 2, 128]), op=ALU.mult),
                    reads=[psb[k], B_KDh[i]], writes=[B_kd[i][st]])
                T.op(act, lambda: nc.scalar.activation(out=vr[i][:, tp * 2:tp * 2 + 2, :], in_=pv[:, :, 128:256],
                                                       func=AF.Copy), reads=[psb[k]], writes=[B_vr[i][st]])

            def seq(*fs):
                return lambda: [f() for f in fs]

            def mk_kdh():
                T.op(pool, lambda: nc.gpsimd.tensor_copy(out=KDh[i][:, 0, 0:64],
                                                         in_=kdcol[:, h:h + 1].to_broadcast([128, 64])),
                     reads=[B_dec], writes=[B_KDh[i]])
                T.op(pool, lambda: nc.gpsimd.tensor_copy(out=KDh[i][:, 0, 64:128],
                                                         in_=kdcol[:, 8 + h:9 + h].to_broadcast([128, 64])),
                     reads=[B_dec], writes=[B_KDh[i]])

            mk_kdh()
            ch = []
            fch = [seq(lambda st=st: zr(st), lambda st=st: qq(st), lambda st=st: kr(st)) for st in range(3)]
            ch.append(seq(lambda: tpair(9), lambda: tpair(8)))
            ch.append(fch[0])
            ch.append(seq(lambda: tpair(7), lambda: tpair(6)))
            ch.append(seq(lambda: tpair(0), lambda: tpair(1)))
            ch.append(fch[1])
            ch.append(seq(lambda: tpair(2), lambda: tpair(3)))
            ch.append(fch[2])
            ch.append(seq(lambda: tpair(4), lambda: tpair(5)))
            return ch

        def mk_ret(h):
            i = h % 2
            ops = []
            cdX = CDv[0:64, h:h + 1]
            cdY = CDv[64:128, h:h + 1]
            ubank = {}

            def Ugrp(g4, bank):
                def f():
                    T.group(pe, [(lambda t4=t4: nc.tensor.matmul(
                        PS[:, bank, t4 * 128:(t4 + 1) * 128], lhsT=kd[i][:, g4 * 4 + t4, :],
                        rhs=vr[i][:, g4 * 4 + t4, :], start=True, stop=True)) for t4 in range(4)],
                        reads=[B_kd[i][g4], B_vr[i][g4]], writes=[psb[bank]])
                for t4 in range(4):
                    ubank[g4 * 4 + t4] = (bank, t4)
                return f

            def U(tile, lo, hi):
                k, t4 = ubank[tile]
                return PS[lo:hi, k, t4 * 128:(t4 + 1) * 128], psb[k]

            def chain(dst, src, cd, tile, lo, hi, wbuf):
                def f():
                    u, ub = U(tile, lo, hi)
                    T.op(dve, lambda: nc.vector.scalar_tensor_tensor(out=dst, in0=src, scalar=cd, in1=u,
                                                                     op0=ALU.mult, op1=ALU.add),
                         reads=[ub, B_st, B_dec, B_const], writes=[wbuf])
                return f

            def ucopy(dst, tile, lo, hi):
                def f():
                    u, ub = U(tile, lo, hi)
                    T.op(dve, lambda: nc.vector.tensor_copy(out=dst, in_=u), reads=[ub], writes=[B_st])
                return f

            ops.append(Ugrp(4, 0))
            ops.append(Ugrp(3, 1))
            prev = s0[64:128, h, :]
            for n_, tile in enumerate(range(19, 11, -1)):
                dst = yc[64:128, n_ % 2, :] if tile > 12 else S_[64:128, 11, :]
                ops.append(chain(dst, prev, cdY, tile, 64, 128, B_st))
                prev = dst
            ops.append(Ugrp(0, 2))
            ops.append(Ugrp(1, 3))
            ops.append(Ugrp(2, 4))
            ops.append(lambda: T.op(pool, lambda: nc.gpsimd.memset(S_[0:64, 0:4:2, :], 0.0), writes=[B_st]))
            ops.append(lambda: T.op(pool, lambda: nc.gpsimd.memset(S_[64:128, 1:4:2, :], 0.0), writes=[B_st]))
            for pb in range(2):
                c0 = pb * 2
                ops.append(ucopy(S_[0:64, c0 + 1, :], c0, 0, 64))
                ops.append(ucopy(S_[64:128, c0, :], c0 + 1, 64, 128))
                ops.append(chain(fin[i][0:64, pb, :], S_[0:64, c0 + 1, :], cdX, c0 + 1, 0, 64, B_fin[i]))
                ops.append(chain(fin[i][64:128, pb, :], S_[64:128, c0, :], cdY, c0, 64, 128, B_fin[i]))
            ops.append(lambda: T.dma(sp, [(nst_d[pb, h], fin[i][:, pb, :]) for pb in range(2)], reads=[B_fin[i]]))
            ops.append(lambda: T.op(dve, lambda: nc.vector.tensor_copy(out=S_[0:64, 4, :], in_=s0[0:64, h, :]),
                                    reads=[B_const], writes=[B_st]))
            for c in range(7):
                ops.append(chain(S_[0:64, 5 + c, :], S_[0:64, 4 + c, :], cdX, 4 + c, 0, 64, B_st))
                cy = 7 - c
                ops.append(chain(S_[64:128, 4 + cy - 1, :], S_[64:128, 4 + cy, :], cdY, 4 + cy, 64, 128, B_st))
            ops.append(lambda: T.op(act, lambda: nc.scalar.activation(out=stB, in_=S_, func=AF.Copy),
                                    reads=[B_st], writes=[B_stb]))
            n_early = len(ops)
            A_b = [0, 1, 2]
            O_b = [3, 4, 0]
            C_b = [1, 2, 3]
            V_b = [4, 0, 4]
            for g in range(3):
                ops.append(lambda g=g: T.group(pe, [(lambda c4=c4: nc.tensor.matmul(
                    PS[:, A_b[g], c4 * 128:(c4 + 1) * 128],
                    lhsT=krT[i][:, (g * 4 + c4) * 128:(g * 4 + c4 + 1) * 128],
                    rhs=qpl[i][:, (g * 4 + c4) * 128:(g * 4 + c4 + 1) * 128], start=True, stop=True))
                    for c4 in range(4)], reads=[B_krT[i][g], B_qpl[i][g]], writes=[psb[A_b[g]]]))
            for g in range(3):
                ops.append(lambda g=g: T.op(dve, lambda: nc.vector.tensor_tensor(
                    out=innT[g], in0=PS[:, A_b[g], :].rearrange("p (c j) -> p c j", c=4),
                    in1=DC[:, h:h + 1, :].to_broadcast([128, 4, 128]), op=ALU.mult),
                    reads=[psb[A_b[g]], B_dec], writes=[B_inn[g]]))
            for g in range(3):
                def og(g=g):
                    fns = []
                    for c4 in range(4):
                        s_ = g * 4 + c4
                        fns.append(lambda c4=c4, s_=s_: nc.tensor.matmul(
                            PS[:, O_b[g], c4 * 128:(c4 + 1) * 128], lhsT=vr[i][:, s_, :], rhs=innT[g][:, c4, :],
                            start=True, stop=False))
                        fns.append(lambda c4=c4, s_=s_: nc.tensor.matmul(
                            PS[:, O_b[g], c4 * 128:(c4 + 1) * 128], lhsT=stB[:, s_, :],
                            rhs=qdT[i][:, s_ * 128:(s_ + 1) * 128], start=False, stop=True))
                    T.group(pe, fns, reads=[B_vr[i][g], B_inn[g], B_stb, B_qd[i][g]], writes=[psb[O_b[g]]])
                ops.append(og)
            for g in range(3):
                ops.append(lambda g=g: T.op(act, lambda: nc.scalar.activation(out=ob[g], in_=PS[:, O_b[g], :],
                                                                              func=AF.Copy),
                                            reads=[psb[O_b[g]]], writes=[B_ob[g]]))
            for g in range(3):
                ops.append(lambda g=g: T.op(pe, lambda: nc.tensor.matmul(PS[:, C_b[g], :], lhsT=cmat, rhs=ob[g],
                                                                         start=True, stop=True),
                                            reads=[B_ob[g], B_c2], writes=[psb[C_b[g]]]))
            for g in range(3):
                ops.append(lambda g=g: T.op(act, lambda: nc.scalar.activation(out=sqb[g], in_=PS[:, C_b[g], :],
                                                                              func=AF.Square),
                                            reads=[psb[C_b[g]]], writes=[B_sqb[g]]))
            for g in range(3):
                ops.append(lambda g=g: T.op(pe, lambda: nc.tensor.matmul(PS[:, V_b[g], :], lhsT=jmat, rhs=sqb[g],
                                                                         start=True, stop=True),
                                            reads=[B_sqb[g], B_c2], writes=[psb[V_b[g]]]))
                ops.append(lambda g=g: T.op(dve, lambda: nc.vector.tensor_scalar(
                    out=rstd[g], in0=PS[:, V_b[g], :], scalar1=LN_EPS, scalar2=None, op0=ALU.add),
                    reads=[psb[V_b[g]]], writes=[B_rstd[g]]))
            for g in range(3):
                ops.append(lambda g=g: T.op(act, lambda: nc.scalar.activation(out=rstd[g], in_=rstd[g], func=AF.Ln),
                                            reads=[B_rstd[g]], writes=[B_rstd[g]]))
            for g in range(3):
                ops.append(lambda g=g: T.op(act, lambda: nc.scalar.activation(out=rstd[g], in_=rstd[g], func=AF.Exp,
                                                                              scale=-0.5),
                                            reads=[B_rstd[g]], writes=[B_rstd[g]]))
            for g in range(3):
                ops.append(lambda g=g: T.op(pool, lambda: nc.gpsimd.tensor_tensor(
                    out=rstd[g], in0=rstd[g], in1=tzr[i][:, g * 512:(g + 1) * 512], op=ALU.mult),
                    reads=[B_rstd[g], B_tzr[i][g]], writes=[B_rstd[g]]))
            for g in range(3):
                ops.append(lambda g=g: T.op(dve, lambda: nc.vector.scalar_tensor_tensor(
                    out=orgT[:, h, g * 512:(g + 1) * 512], in0=PS[:, C_b[g], :], scalar=gngT[:, h:h + 1],
                    in1=rstd[g], op0=ALU.mult, op1=ALU.mult),
                    reads=[psb[C_b[g]], B_rstd[g], B_const], writes=[B_org[h][g]]))
            return ops, n_early

        ph4pre = Arena(BIG + 88 * KB, BIG + 88 * KB + 2 * 9216)
        WG = [sb(f"WG{i}", [128, 8, 512], BF16, ph4pre) for i in range(2)]
        B_WG = [T.b("WG", i) for i in range(2)]

        def load_WG(j, extra=()):
            v = wg_d[j].rearrange("(kt p) c -> p kt c", p=128)
            T.dma(pool, [(WG[j % 2], v)], writes=[B_WG[j % 2]] + list(extra))

        for c_ in mk_chunks3(0):
            c_()
        for h in range(8):
            ret, n_early = mk_ret(h)
            if h + 2 < 8:
                load_WR(h + 2)
            if h == 7:
                load_WG(0, extra=[B_WR[0], B_WR[1]])
                load_WG(1, extra=[B_WR[0], B_WR[1]])
            chs = mk_chunks3(h + 1) if h + 1 < 8 else []
            nr, ncs = len(ret), len(chs)
            ri = 0
            n_e_ch = 2
            for ci, c_ in enumerate(chs):
                c_()
                if ci < n_e_ch:
                    tgt = (ci + 1) * n_early // n_e_ch
                else:
                    tgt = n_early + min(nr - n_early, (ci + 1 - n_e_ch) * (nr - n_early) * 5 // (4 * (ncs - n_e_ch)))
                while ri < tgt:
                    ret[ri]()
                    ri += 1
            while ri < nr:
                ret[ri]()
                ri += 1
        T.barrier()
        if STOP == 3:
            T.finish()
            return nc

    if True:
        ph = Arena(BIG + 112 * KB, TOP)
        ph4b = Arena(BIG, BIG + 88 * KB)
        WO = sb("WO", [128, 8, D], BF16, ph)
        mT = sb("mT", [128, 8, NQ_TOK], BF16, ph)
        lnrow = sb("lnrow", [128, 2, D], F32, ph)
        ta = [sb(f"ta{i}", [128, 512], F32, ph) for i in range(2)]
        tr_ = [sb(f"tr{i}", [128, 512], F32, ph) for i in range(2)]
        XT = [sb(f"XT4_{i}", [128, D], F32, ph4b) for i in range(4)]
        Z = [sb(f"Z{i}", [128, D], F32, ph4b) for i in range(4)]
        Y = [sb(f"Y{i}", [128, D], F32, ph4b) for i in range(4)]
        st6 = [sb(f"st6b_{i}", [128, 2, 6], F32, ph) for i in range(4)]
        mv = [sb(f"mvb{i}", [128, 2], F32, ph) for i in range(4)]
        rs = [sb(f"rsb{i}", [128, 1], F32, ph) for i in range(4)]
        B_WO = T.b("WO")
        B_mT = [[T.b("mT", j, t) for t in range(3)] for j in range(8)]
        B_ta = [T.b("ta", i) for i in range(2)]
        B_tr = [T.b("tr", i) for i in range(2)]
        B_x4 = [T.b("XT4", i) for i in range(4)]
        B_z = [T.b("Z", i) for i in range(4)]
        B_y = [T.b("Y", i) for i in range(4)]
        B_s4 = [T.b("s4", i) for i in range(4)]

        wov = wout_d.rearrange("(kt p) c -> p kt c", p=128)
        T.dma(pool, [(WO, wov)], writes=[B_WO])
        B_ln = T.b("lnrow")
        T.dma(sp, [(lnrow, lnrow_d)], writes=[B_ln])
        rr = [0]
        for j in range(8):
            i = j % 2
            W = WG[i]
            if j >= 1 and j + 1 < 8:
                load_WG(j + 1)
            for t in range(3):
                def grp(c0, rhs_fn, rbufs):
                    k = next_bank()
                    T.group(pe, [(lambda kt=kt, k=k: nc.tensor.matmul(
                        PS[:, k, :], lhsT=W[:, kt, c0:c0 + 128], rhs=rhs_fn(kt), start=(kt == 0), stop=(kt == 7)))
                        for kt in range(8)], reads=[B_WG[i]] + rbufs, writes=[psb[k]])
                    return k
                hsl = lambda kt, t=t: hT[:, kt, t * 512:(t + 1) * 512]
                kga = grp(0, hsl, [B_hT[t]])
                kgr = grp(128, hsl, [B_hT[t]])
                kpa = grp(256, lambda kt, t=t: oagT[:, kt, t * 512:(t + 1) * 512], [B_oag[hh][t] for hh in range(8)])
                kpr = grp(384, lambda kt, t=t: orgT[:, kt, t * 512:(t + 1) * 512], [B_org[hh][t] for hh in range(8)])
                rr[0] ^= 1
                u = rr[0]
                T.op(act, lambda u=u, kga=kga, j=j: nc.scalar.activation(
                    out=ta[u], in_=PS[:, kga, :], func=AF.Tanh, scale=0.5, bias=bgTh[:, j:j + 1]),
                    reads=[psb[kga], B_lam], writes=[B_ta[u]])
                T.op(act, lambda u=u, kgr=kgr, j=j: nc.scalar.activation(
                    out=tr_[u], in_=PS[:, kgr, :], func=AF.Tanh, scale=0.5, bias=bgTh[:, 8 + j:9 + j]),
                    reads=[psb[kgr], B_lam], writes=[B_tr[u]])
                T.op(dve, lambda u=u, kpa=kpa: nc.vector.scalar_tensor_tensor(
                    out=ta[u], in0=ta[u], scalar=1.0, in1=PS[:, kpa, :], op0=ALU.add, op1=ALU.mult),
                    reads=[B_ta[u], psb[kpa]], writes=[B_ta[u]])
                T.op(dve, lambda u=u, kpr=kpr: nc.vector.scalar_tensor_tensor(
                    out=tr_[u], in0=tr_[u], scalar=1.0, in1=PS[:, kpr, :], op0=ALU.add, op1=ALU.mult),
                    reads=[B_tr[u], psb[kpr]], writes=[B_tr[u]])
                T.op(pool, lambda u=u, j=j, t=t: nc.gpsimd.tensor_tensor(
                    out=mT[:, j, t * 512:(t + 1) * 512], in0=ta[u], in1=tr_[u], op=ALU.add),
                    reads=[B_ta[u], B_tr[u]], writes=[B_mT[j][t]])
        T.barrier()
        def f_A(tt):
            u = tt % 4
            ci = 0 if tt < 4 else 1
            t = tt // 4
            kk = []
            for hf in range(2):
                k = next_bank()
                kk.append(k)
                T.group(pe, [(lambda kt=kt, k=k, hf=hf: nc.tensor.matmul(
                    PS[:, k, :], lhsT=mT[:, kt, tt * 128:(tt + 1) * 128], rhs=WO[:, kt, hf * 512:(hf + 1) * 512],
                    start=(kt == 0), stop=(kt == 7))) for kt in range(8)],
                    reads=[B_WO] + [B_mT[jj][t] for jj in range(8)], writes=[psb[k]])
            for hf in range(2):
                T.op(dve, lambda hf=hf, k=kk[hf]: nc.vector.tensor_tensor(
                    out=Z[u][:, hf * 512:(hf + 1) * 512], in0=PS[:, k, :],
                    in1=gate_bc[:, ci, hf * 512:(hf + 1) * 512], op=ALU.mult),
                    reads=[psb[kk[hf]], B_gate], writes=[B_z[u]])
            T.op(dve, lambda: nc.vector.scalar_tensor_tensor(
                out=Z[u], in0=XT[u], scalar=ALPHA, in1=Z[u], op0=ALU.mult, op1=ALU.add),
                reads=[B_x4[u], B_z[u]], writes=[B_z[u]])
            for c2 in range(2):
                T.op(dve, lambda c2=c2: nc.vector.bn_stats(out=st6[u][:, c2, :],
                                                           in_=Z[u][:, c2 * 512:(c2 + 1) * 512]),
                     reads=[B_z[u]], writes=[B_s4[u]])
            T.op(dve, lambda: nc.vector.bn_aggr(out=mv[u], in_=st6[u]), reads=[B_s4[u]], writes=[B_s4[u]])
            T.op(dve, lambda: nc.vector.tensor_scalar(out=rs[u], in0=mv[u][:, 1:2], scalar1=LN_EPS,
                                                      scalar2=None, op0=ALU.add),
                 reads=[B_s4[u]], writes=[B_s4[u]])
            T.op(act, lambda: nc.scalar.activation(out=rs[u], in_=rs[u], func=AF.Sqrt),
                 reads=[B_s4[u]], writes=[B_s4[u]])

        def f_B(tt):
            u = tt % 4
            T.op(dve, lambda: nc.vector.reciprocal(out=rs[u], in_=rs[u]), reads=[B_s4[u]], writes=[B_s4[u]])
            T.op(dve, lambda: nc.vector.scalar_tensor_tensor(
                out=mv[u][:, 1:2], in0=mv[u][:, 0:1], scalar=-1.0, in1=rs[u], op0=ALU.mult, op1=ALU.mult),
                reads=[B_s4[u]], writes=[B_s4[u]])
            T.op(act, lambda: nc.scalar.activation(out=Y[u], in_=Z[u], func=AF.Identity, scale=rs[u],
                                                   bias=mv[u][:, 1:2]),
                 reads=[B_z[u], B_s4[u]], writes=[B_y[u]])
            T.op(pool, lambda: nc.gpsimd.tensor_tensor(out=Y[u], in0=Y[u], in1=lnrow[:, 0, :], op=ALU.mult),
                 reads=[B_y[u], B_ln], writes=[B_y[u]])

        def f_C(tt):
            u = tt % 4
            T.op(dve, lambda: nc.vector.tensor_tensor(out=Y[u], in0=Y[u], in1=lnrow[:, 1, :], op=ALU.add),
                 reads=[B_y[u], B_ln], writes=[B_y[u]])
            if tt + 4 < 12:
                T.dma(sp, [(XT[u], xs[(tt + 4) * 128:(tt + 5) * 128, :])], writes=[B_x4[u]])
            T.dma(sp, [(y_d[tt * 128:(tt + 1) * 128, :], Y[u])], reads=[B_y[u]])

        for t_ in range(4):
            T.dma(sp, [(XT[t_], xs[t_ * 128:(t_ + 1) * 128, :])], writes=[B_x4[t_]])
        for i_ in range(12 + 2):
            if i_ < 12:
                f_A(i_)
            if 0 <= i_ - 1 < 12:
                f_B(i_ - 1)
            if 0 <= i_ - 2 < 12:
                f_C(i_ - 2)
        T.finish()
    es.close()
    return nc


def _consts():
    p = np.arange(128)
    i = np.arange(128)
    c = {}
    c["ident"] = np.eye(128, dtype=np.float32)
    c["cmat"] = (np.eye(128) - 1.0 / 128.0).astype(np.float32)
    iq = np.zeros((128, 128), np.float32)
    iq[0:64, :] = (i + 1)[None, :]
    iq[64:128, :] = (128 - i)[None, :]
    c["iq"] = iq
    c["jp"] = np.stack([127 - p, p], axis=1).astype(np.float32)
    diff = (i[None, :] - p[:, None]).astype(np.float32)
    c["diff"] = diff
    c["mx8"] = (0.125 * (diff >= 0)).astype(np.float32)
    c["my8"] = (0.125 * (diff <= 0)).astype(np.float32)
    sel2 = np.zeros((2, 2, 128), np.float32)
    sel2[0, 0, :] = 1.0
    sel2[1, 1, :] = 1.0
    c["sel2"] = sel2
    return c


def _rope_tables(pos):
    pos = np.asarray(pos)
    r = (pos // 64).astype(np.float32)
    col = (pos % 64).astype(np.float32)
    inv = (np.float32(10000.0) ** (-np.arange(16, dtype=np.float32) / np.float32(16))).astype(np.float32)
    ang = np.concatenate([r[:, None] * inv[None, :], col[:, None] * inv[None, :]], axis=1).astype(np.float32)
    cos = np.cos(ang).astype(np.float32)
    sin = np.sin(ang).astype(np.float32)
    pidx = (np.arange(128) % 64) // 2
    sgn = np.where(np.arange(128) % 2 == 1, 1.0, -1.0).astype(np.float32)
    C = cos[:, pidx].T.copy()
    S = (sin[:, pidx] * sgn[None, :]).T.copy()
    return np.ascontiguousarray(C, np.float32), np.ascontiguousarray(S, np.float32)


_PROGRAM = None


def kernel(x_prompt, x_sample, cache_attn_k, cache_attn_v, state_ret_fwd, state_ret_bwd,
           c, c_ctx, w_mod, b_mod, w_in, lam_params, subln_g, ret_decay, ret_gn_g,
           w_pa, w_pr, w_gate, b_gate, w_out, ln_g, ln_b):
    global _PROGRAM
    f = lambda a: np.ascontiguousarray(np.asarray(a), dtype=np.float32)
    x_prompt, x_sample = f(x_prompt), f(x_sample)
    cache_attn_k, cache_attn_v = f(cache_attn_k), f(cache_attn_v)
    state_ret_fwd, state_ret_bwd = f(state_ret_fwd), f(state_ret_bwd)
    c, c_ctx = f(c), f(c_ctx)
    w_in0 = f(w_in)[0]
    w_gate0, w_pa0, w_pr0 = f(w_gate)[0], f(w_pa)[0], f(w_pr)[0]
    ret_decay0 = f(ret_decay)[0]

    sw = np.arange(128) ^ 1
    wa = np.empty((8, D, WA_COLS), np.float32)
    wr = np.empty((8, D, WR_COLS), np.float32)
    wg = np.empty((8, D, 512), np.float32)
    for h in range(8):
        qa = w_in0[:, h * 128:(h + 1) * 128]
        ka = w_in0[:, 1024 + h * 128:1024 + (h + 1) * 128]
        va = w_in0[:, 2048 + h * 128:2048 + (h + 1) * 128]
        za = w_in0[:, 3072 + h * 128:3072 + (h + 1) * 128]
        qr = w_in0[:, 4096 + h * 64:4096 + (h + 1) * 64]
        kr = w_in0[:, 4608 + h * 64:4608 + (h + 1) * 64]
        vr = w_in0[:, 5120 + h * 128:5120 + (h + 1) * 128]
        zr = w_in0[:, 6144 + h * 128:6144 + (h + 1) * 128]
        wa[h] = np.concatenate([qa, qa[:, sw], ka[:, sw], za, ka, va], axis=1)
        wr[h] = np.concatenate([qr, qr, zr, kr, kr, kr, vr], axis=1)
        wg[h] = np.concatenate([w_gate0[:, h * 128:(h + 1) * 128], w_gate0[:, 1024 + h * 128:1024 + (h + 1) * 128],
                                w_pa0[:, h * 128:(h + 1) * 128], w_pr0[:, h * 128:(h + 1) * 128]], axis=1)
    consts = _consts()
    shared = {
        "w_mod": f(w_mod)[0], "bmod2": np.ascontiguousarray(np.broadcast_to(f(b_mod)[0][None, :], (2, 3 * D))),
        "wa": wa, "wr": wr, "wg": wg, "w_out": f(w_out)[0],
        "bgT": np.ascontiguousarray(f(b_gate)[0].reshape(16, 128).T),
        "lnrow": np.ascontiguousarray(np.broadcast_to(np.stack([f(ln_g)[0], f(ln_b)[0]])[None], (128, 2, D))),
        "lamp": np.ascontiguousarray(np.broadcast_to(f(lam_params)[0].reshape(1, 256), (128, 256))),
        "sublnT": np.ascontiguousarray(f(subln_g)[0].reshape(128, 1)),
        "gngT": np.ascontiguousarray(f(ret_gn_g)[0].reshape(8, 128).T),
    }
    shared.update(consts)

    in_maps = []
    for core in range(NCORES):
        b, half = core // 2, core % 2
        rev = half == 1
        o = (lambda a: a[::-1]) if rev else (lambda a: a)
        p0, p1 = o(x_prompt[2 * core]), o(x_prompt[2 * core + 1])
        mine = o(x_sample[b, half * 1024:(half + 1) * 1024])
        other = o(x_sample[b, (1 - half) * 1024:(2 - half) * 1024])
        pos = np.concatenate([o(np.arange(half * 1024, (half + 1) * 1024)),
                              o(np.arange((1 - half) * 1024, (2 - half) * 1024))])
        rc, rs_ = _rope_tables(pos)
        sX = state_ret_fwd[b, 0] if not rev else state_ret_bwd[b, 0]
        sY = state_ret_bwd[b, 0] if not rev else state_ret_fwd[b, 0]
        dX = ret_decay0[0] if not rev else ret_decay0[1]
        dY = ret_decay0[1] if not rev else ret_decay0[0]
        m = dict(shared)
        m["xs"] = np.ascontiguousarray(np.concatenate([p0, p1, mine, other], axis=0))
        m["ck"] = np.ascontiguousarray(cache_attn_k[b, 0].reshape(256, D))
        m["cv"] = np.ascontiguousarray(cache_attn_v[b, 0].reshape(256, D))
        m["s0"] = np.ascontiguousarray(np.concatenate([sX.transpose(1, 0, 2), sY.transpose(1, 0, 2)], axis=0))
        m["condT"] = np.ascontiguousarray(np.stack([c_ctx.reshape(8, 128).T, c[b].reshape(8, 128).T], axis=2))
        m["rdsel"] = np.ascontiguousarray(np.concatenate([np.broadcast_to(dX[None], (64, 8)),
                                                           np.broadcast_to(dY[None], (64, 8))], axis=0))
        m["rdrow"] = np.ascontiguousarray(np.broadcast_to(np.concatenate([dX, dY])[None], (128, 16)))
        m["ropeC"] = rc
        m["ropeS"] = rs_
        in_maps.append(m)

    if _PROGRAM is None:
        import os as _os
        _PROGRAM = build_program(int(_os.environ.get('MK_STOP', '99')))
    res = run_bass_kernel_spmd(_PROGRAM, in_maps, core_ids=list(range(NCORES)))

    y_prompt = np.empty((16, 256, D), np.float32)
    y_sample = np.empty((4, 2048, D), np.float32)
    new_k = np.empty((16, 1, 256, 8, 128), np.float32)
    new_v = np.empty((16, 1, 256, 8, 128), np.float32)
    new_f = np.empty((16, 1, 8, 64, 128), np.float32)
    new_b = np.empty((16, 1, 8, 64, 128), np.float32)
    for core in range(NCORES):
        r = res.results[core]
        b, half = core // 2, core % 2
        rev = half == 1
        o = (lambda a: a[::-1]) if rev else (lambda a: a)
        y = r["y"]
        for pb in range(2):
            y_prompt[2 * core + pb] = o(y[pb * 256:(pb + 1) * 256])
            new_k[2 * core + pb, 0] = o(r["nk"][pb * 256:(pb + 1) * 256]).reshape(256, 8, 128)
            new_v[2 * core + pb, 0] = o(r["nv"][pb * 256:(pb + 1) * 256]).reshape(256, 8, 128)
            X = r["nst"][pb][:, 0:64, :]
            Yv = r["nst"][pb][:, 64:128, :]
            new_f[2 * core + pb, 0] = Yv if rev else X
            new_b[2 * core + pb, 0] = X if rev else Yv
        y_sample[b, half * 1024:(half + 1) * 1024] = o(y[512:1536])
    return (y_prompt, y_sample, new_k, new_v, new_f, new_b)
```

```python
import math
from contextlib import ExitStack

import numpy as np
import concourse.bass as bass
import concourse.mybir as mybir
from concourse.bass_utils import run_bass_kernel_spmd

F32 = mybir.dt.float32
BF16 = mybir.dt.bfloat16
AF = mybir.ActivationFunctionType
ALU = mybir.AluOpType
AX = mybir.AxisListType

NCORES = 8
D = 1024
NP_TOK = 512
NM_TOK = 1024
NO_TOK = 1024
NTOK = NP_TOK + NM_TOK + NO_TOK
NQ_TOK = NP_TOK + NM_TOK
WA_COLS = 768
WR_COLS = 576
LAM_INIT = 0.8 - 0.6 * math.exp(-0.3 * 0)
ALPHA = (2.0 * 1) ** 0.25
MOD_EPS = 1e-6
LN_EPS = 1e-5


class _Sem:
    _n = 0

    def __init__(self, nc, name):
        _Sem._n += 1
        self.uid = _Sem._n
        self.h = nc.alloc_semaphore(f"{name}_{self.uid}")


class _Eng:
    def __init__(self, nc, name, handle):
        self.name = name
        self.h = handle
        self.sem = _Sem(nc, "e" + name)
        self.n = 0
        self.seen = {}


class _Buf:
    __slots__ = ("name", "w", "rs", "dsem", "dcnt", "excl")

    def __init__(self, name):
        self.name = name
        self.excl = False
        self.w = None
        self.rs = {}
        self.dsem = None
        self.dcnt = 0


class Tracker:
    def __init__(self, nc):
        self.nc = nc
        self.pe = _Eng(nc, "pe", nc.tensor)
        self.act = _Eng(nc, "act", nc.scalar)
        self.dve = _Eng(nc, "dve", nc.vector)
        self.pool = _Eng(nc, "pool", nc.gpsimd)
        self.sp = _Eng(nc, "sp", nc.sync)
        self.engs = [self.pe, self.act, self.dve, self.pool, self.sp]
        self.bufs = {}
        self.dsems = []

    def b(self, *key):
        bb = self.bufs.get(key)
        if bb is None:
            bb = _Buf(key)
            self.bufs[key] = bb
        return bb

    def _wait(self, eng, tok):
        if tok is None:
            return
        sem, val = tok
        if eng is self.pe and sem is self.pe.sem:
            return
        if eng.seen.get(sem.uid, 0) >= val:
            return
        eng.h.wait_ge(sem.h, val)
        eng.seen[sem.uid] = val

    def _deps(self, eng, reads, writes):
        for r in reads:
            self._wait(eng, r.w)
            if r.excl:
                for t in r.rs.values():
                    if t[0] is not eng.sem:
                        self._wait(eng, t)
        for w in writes:
            self._wait(eng, w.w)
            for t in w.rs.values():
                if t[0] is eng.sem and eng is not self.pool:
                    continue
                self._wait(eng, t)

    def _mark(self, tok, reads, writes):
        for w in writes:
            w.w = tok
            w.rs = {}
        for r in reads:
            r.rs[tok[0].uid] = tok

    def group(self, eng, fns, reads=(), writes=()):
        self._deps(eng, reads, writes)
        ins = None
        for f in fns:
            ins = f()
        eng.n += 1
        ins.then_inc(eng.sem.h, 1)
        self._mark((eng.sem, eng.n), reads, writes)

    def op(self, eng, fn, reads=(), writes=()):
        self.group(eng, [fn], reads, writes)

    def dma(self, eng, pairs, reads=(), writes=()):
        owner = (list(writes) + list(reads))[0]
        kind = "sw" if eng is self.pool else "hw"
        if owner.dsem is None:
            owner.dsem = {}
        if kind not in owner.dsem:
            owner.dsem[kind] = [_Sem(self.nc, "d" + kind), 0]
            self.dsems.append(owner.dsem[kind])
        ent = owner.dsem[kind]
        self._deps(eng, reads, writes)
        for (o, i) in pairs:
            eng.h.dma_start(out=o, in_=i).then_inc(ent[0].h, 16)
            ent[1] += 16
        self._mark((ent[0], ent[1]), reads, writes)

    def barrier(self, skip=()):
        skip_ids = set()
        for b_ in skip:
            for ent in (b_.dsem or {}).values():
                skip_ids.add(ent[0].uid)
        for e in self.engs:
            for f in self.engs:
                if f is not e and f.n > 0:
                    self._wait(e, (f.sem, f.n))
            for d in self.dsems:
                if d[0].uid not in skip_ids:
                    self._wait(e, (d[0], d[1]))

    def finish(self):
        for d in self.dsems:
            self._wait(self.sp, (d[0], d[1]))
        for f in self.engs:
            if f is not self.sp and f.n > 0:
                self._wait(self.sp, (f.sem, f.n))


class _StopBuild(Exception):
    pass


def build_program(STOP=99):
    import os as _os
    tag_stop = _os.environ.get("MK_TAG", "")
    holder = {}

    def chk(tag):
        if tag_stop and tag == tag_stop:
            raise _StopBuild()

    try:
        return _build_program(STOP, chk, holder)
    except _StopBuild:
        holder["T"].finish()
        return holder["nc"]


def _build_program(STOP, chk, holder):
    nc = bass.Bass("TRN2", target_bir_lowering=False)
    T = Tracker(nc)
    holder["T"] = T
    holder["nc"] = nc
    pe, act, dve, pool, sp = T.pe, T.act, T.dve, T.pool, T.sp

    def din(name, shape):
        return nc.dram_tensor(name, list(shape), F32, kind="ExternalInput").ap()

    def dout(name, shape):
        return nc.dram_tensor(name, list(shape), F32, kind="ExternalOutput").ap()

    xs = din("xs", [NTOK, D])
    ck = din("ck", [256, D])
    cv = din("cv", [256, D])
    s0_d = din("s0", [128, 8, 128])
    condT_d = din("condT", [128, 8, 2])
    wmod_d = din("w_mod", [D, 3 * D])
    bmod_d = din("bmod2", [2, 3 * D])
    wa_d = din("wa", [8, D, WA_COLS])
    wr_d = din("wr", [8, D, WR_COLS])
    wg_d = din("wg", [8, D, 512])
    wout_d = din("w_out", [D, D])
    bgT_d = din("bgT", [128, 16])
    lnrow_d = din("lnrow", [128, 2, D])
    lamp_d = din("lamp", [128, 256])
    sublnT_d = din("sublnT", [128, 1])
    gngT_d = din("gngT", [128, 8])
    rdsel_d = din("rdsel", [128, 8])
    rdrow_d = din("rdrow", [128, 16])
    ident_d = din("ident", [128, 128])
    cmat_d = din("cmat", [128, 128])
    iq_d = din("iq", [128, 128])
    jp_d = din("jp", [128, 2])
    diff_d = din("diff", [128, 128])
    mx8_d = din("mx8", [128, 128])
    my8_d = din("my8", [128, 128])
    ropeC_d = din("ropeC", [128, 2048])
    ropeS_d = din("ropeS", [128, 2048])
    sel2_d = din("sel2", [2, 2, 128])

    y_d = dout("y", [NQ_TOK, D])
    nk_d = dout("nk", [NP_TOK, D])
    nv_d = dout("nv", [NP_TOK, D])
    nst_d = dout("nst", [2, 8, 128, 128])

    es = ExitStack()
    KB = 1024
    BASE = 16640
    TOP = 228992
    BIG = BASE + 30 * KB
    _esz = {F32: 4, BF16: 2}

    class Arena:
        def __init__(self, start, end):
            self.start, self.end, self.cur = start, end, start

        def alloc(self, name, shape, dt):
            n = 1
            for d_ in shape[1:]:
                n *= d_
            nbytes = (n * _esz[dt] + 63) // 64 * 64
            off = self.cur
            assert off + nbytes <= self.end, (name, off, nbytes, self.end)
            self.cur = off + nbytes
            Arena.uid += 1
            return nc.alloc_sbuf_tensor_at(f"s_{name}_{Arena.uid}", list(shape), dt, offset=off).ap()

    Arena.uid = 0
    A_const = Arena(BASE, BIG)
    A_pers = Arena(BIG, BIG + 112 * KB)

    def sb(name, shape, dt=F32, arena=None):
        return (arena or A_const).alloc(name, shape, dt)

    PS = es.enter_context(nc.psum_tensor("PS", [128, 8, 512], F32)).ap()
    psb = [T.b("ps", k) for k in range(8)]
    for _b in psb:
        _b.excl = True
    bank_rr = [0]

    def next_bank():
        k = bank_rr[0]
        bank_rr[0] = (k + 1) % 8
        return k

    hT = sb("hT", [128, 8, NTOK], BF16, A_pers)
    oagT = sb("oagT", [128, 8, NQ_TOK], BF16, A_pers)
    orgT = sb("orgT", [128, 8, NQ_TOK], BF16, A_pers)
    ident = sb("ident", [128, 128])
    cmat_f = sb("cmat_f", [128, 128])
    cmat = sb("cmat", [128, 128], BF16)
    jmat = sb("jmat", [128, 128], BF16)
    ones_b = sb("ones_b", [128, 128], BF16)
    neghalf = sb("neghalf", [128, 512])
    one_col = sb("one_col", [128, 1])
    QD = sb("QD", [128, 8, 128])
    kdcol = sb("kdcol", [128, 16])
    DC = sb("DC", [128, 8, 128])
    CDv = sb("CDv", [128, 8])
    s0 = sb("s0", [128, 8, 128])
    shiftT = sb("shiftT", [128, 8, 2])
    scale1T = sb("scale1T", [128, 8, 2])
    gate_bc = sb("gate_bc", [128, 2, D])
    bgT = sb("bgT", [128, 16])
    bgTh = sb("bgTh", [128, 16])
    neglam = sb("neglam", [128, 1])
    gsc = sb("gsc", [128, 1])
    gngT = sb("gngT", [128, 8])
    gngTh = sb("gngTh", [128, 8])
    sublnT = sb("sublnT", [128, 1])

    B_const = T.b("const")
    B_hT = [T.b("hT", st) for st in range(5)]

    if True:
        ph = Arena(TOP - 32 * KB, TOP)
        condT = sb("condT", [128, 8, 2], F32, ph)
        silT = sb("silT", [128, 8, 2], BF16, ph)
        ctmp = sb("ctmp", [128, 8, 2], F32, ph)
        lamp = sb("lamp", [128, 256], F32, ph)
        ltmp = sb("ltmp", [128, 128], F32, ph)
        lsum = sb("lsum", [128, 2], F32, ph)
        rdsel = sb("rdsel", [128, 8], F32)
        rdrow = sb("rdrow", [128, 16], F32)
        lgsel = sb("lgsel", [128, 8], F32)
        lgrow = sb("lgrow", [128, 16], F32)
        nlgrow = sb("nlgrow", [128, 16], F32)
        lg128 = sb("lg128", [128, 8], F32, ph)
        iq = sb("iq", [128, 128], F32)
        jp = sb("jp", [128, 2], F32)
        diff = sb("diff", [128, 128], F32)
        mx8 = sb("mx8", [128, 1, 128], F32)
        my8 = sb("my8", [128, 1, 128], F32)
        sel2 = sb("sel2", [2, 2, 128], F32, ph)
        modrow = sb("modrow", [2, 3 * D], F32, ph)
        wm = [sb(f"wm{i}", [128, 8, 512], BF16, ph) for i in range(2)]
        B_wm = [T.b("wm", i) for i in range(2)]

        T.dma(sp, [(ident, ident_d), (cmat_f, cmat_d), (s0, s0_d), (condT, condT_d), (bgT, bgT_d),
                   (lamp, lamp_d), (sublnT, sublnT_d), (gngT, gngT_d),
                   (rdsel, rdsel_d), (rdrow, rdrow_d), (iq, iq_d), (jp, jp_d), (diff, diff_d),
                   (mx8, mx8_d.rearrange("p (o i) -> p o i", o=1)), (my8, my8_d.rearrange("p (o i) -> p o i", o=1)), (sel2, sel2_d), (modrow, bmod_d)],
              writes=[B_const])
        B_c2 = T.b("const2")
        T.op(pool, lambda: nc.gpsimd.memset(jmat, 1.0 / 128.0), writes=[B_c2])
        T.op(pool, lambda: nc.gpsimd.memset(ones_b, 1.0), writes=[B_c2])
        T.op(pool, lambda: nc.gpsimd.memset(neghalf, -0.5), writes=[B_c2])
        T.op(pool, lambda: nc.gpsimd.memset(one_col, 1.0), writes=[B_c2])
        T.op(dve, lambda: nc.vector.tensor_copy(out=cmat, in_=cmat_f), reads=[B_const], writes=[B_c2])

        chk('a')
        B_sil = T.b("sil")
        T.op(act, lambda: nc.scalar.activation(out=ctmp, in_=condT, func=AF.Tanh, scale=0.5),
             reads=[B_const], writes=[B_sil])
        T.op(dve, lambda: nc.vector.scalar_tensor_tensor(out=ctmp, in0=ctmp, scalar=1.0, in1=condT,
                                                         op0=ALU.add, op1=ALU.mult),
             reads=[B_sil, B_const], writes=[B_sil])
        T.op(dve, lambda: nc.vector.tensor_scalar(out=silT, in0=ctmp, scalar1=0.5, scalar2=None, op0=ALU.mult),
             reads=[B_sil], writes=[B_sil])

        chk('b')
        B_mod = T.b("modrow")
        wmod_v = wmod_d.rearrange("(kt p) c -> p kt c", p=128)
        for blk in range(6):
            slot = blk % 2
            T.dma(pool, [(wm[slot], wmod_v[:, :, blk * 512:(blk + 1) * 512])], writes=[B_wm[slot]])
            k = next_bank()
            T.group(pe, [(lambda kt=kt, k=k, slot=slot: nc.tensor.matmul(
                PS[0:2, k, :], lhsT=silT[:, kt, :], rhs=wm[slot][:, kt, :], start=(kt == 0), stop=(kt == 7)))
                for kt in range(8)], reads=[B_sil, B_wm[slot]], writes=[psb[k]])
            T.op(dve, lambda k=k, blk=blk: nc.vector.tensor_tensor(
                out=modrow[:, blk * 512:(blk + 1) * 512], in0=PS[0:2, k, :],
                in1=modrow[:, blk * 512:(blk + 1) * 512], op=ALU.add),
                reads=[psb[k], B_const], writes=[B_mod])
        chk('c')
        k = next_bank()
        T.group(pe, [(lambda j=j, k=k: nc.tensor.transpose(
            PS[:, k, 2 * j:2 * j + 2], modrow[0:2, j * 128:(j + 1) * 128], ident[0:2, 0:2]))
            for j in range(16)], reads=[B_mod, B_const], writes=[psb[k]])
        B_modT = T.b("modT")
        T.op(dve, lambda k=k: nc.vector.tensor_copy(
            out=shiftT, in_=PS[:, k, 0:16].rearrange("p (j c) -> p j c", c=2)),
            reads=[psb[k]], writes=[B_modT])
        T.op(dve, lambda k=k: nc.vector.tensor_scalar(
            out=scale1T, in0=PS[:, k, 16:32].rearrange("p (j c) -> p j c", c=2), scalar1=1.0, scalar2=None,
            op0=ALU.add), reads=[psb[k]], writes=[B_modT])
        chk('d')
        B_gate = T.b("gate_bc")
        for ci in range(2):
            for hf in range(2):
                k = next_bank()
                T.op(pe, lambda k=k, ci=ci, hf=hf: nc.tensor.matmul(
                    PS[:, k, :], lhsT=sel2[:, ci, :], rhs=modrow[0:2, 2048 + hf * 512:2048 + (hf + 1) * 512],
                    start=True, stop=True), reads=[B_mod, B_const], writes=[psb[k]])
                T.op(act, lambda k=k, ci=ci, hf=hf: nc.scalar.mul(
                    out=gate_bc[:, ci, hf * 512:(hf + 1) * 512], in_=PS[:, k, :], mul=0.5),
                    reads=[psb[k]], writes=[B_gate])

        chk('e')
        B_lam = T.b("lam")
        T.op(dve, lambda: nc.vector.tensor_tensor(
            out=ltmp.rearrange("p (a b) -> p a b", a=2),
            in0=lamp.rearrange("p (a r b) -> p a r b", a=2, r=2)[:, :, 0, :],
            in1=lamp.rearrange("p (a r b) -> p a r b", a=2, r=2)[:, :, 1, :], op=ALU.mult),
            reads=[B_const], writes=[B_lam])
        T.op(dve, lambda: nc.vector.reduce_sum(out=lsum, in_=ltmp.rearrange("p (a b) -> p a b", a=2), axis=AX.X),
             reads=[B_lam], writes=[B_lam])
        T.op(act, lambda: nc.scalar.activation(out=lsum, in_=lsum, func=AF.Exp), reads=[B_lam], writes=[B_lam])
        T.op(dve, lambda: nc.vector.tensor_tensor(out=neglam, in0=lsum[:, 1:2], in1=lsum[:, 0:1], op=ALU.subtract),
             reads=[B_lam], writes=[B_lam])
        T.op(dve, lambda: nc.vector.tensor_scalar(out=neglam, in0=neglam, scalar1=-LAM_INIT, scalar2=None,
                                                  op0=ALU.add), reads=[B_lam], writes=[B_lam])
        T.op(dve, lambda: nc.vector.tensor_scalar(out=gsc, in0=sublnT, scalar1=(1.0 - LAM_INIT),
                                                  scalar2=None, op0=ALU.mult), reads=[B_const], writes=[B_lam])
        T.op(dve, lambda: nc.vector.tensor_scalar(out=gngTh, in0=gngT, scalar1=0.5, scalar2=None, op0=ALU.mult),
             reads=[B_const], writes=[B_lam])
        T.op(dve, lambda: nc.vector.tensor_scalar(out=bgTh, in0=bgT, scalar1=0.5, scalar2=None, op0=ALU.mult),
             reads=[B_const], writes=[B_lam])

        B_dec = T.b("dec")
        ph0 = ph

        def emit_decay_tables():
            chk('f')
            B_dec = T.b("dec")

            def log1p_neg(dst, src, n):
                T.op(act, lambda: nc.scalar.activation(out=src, in_=src, func=AF.Exp, scale=math.log(2.0)),
                     reads=[B_const, B_dec], writes=[B_dec])
                T.op(dve, lambda: nc.vector.tensor_scalar(out=dst, in0=src, scalar1=1.0 / 8.0, scalar2=None,
                                                          op0=ALU.mult), reads=[B_dec], writes=[B_dec])
                for c in (1.0 / 7.0, 1.0 / 6.0, 1.0 / 5.0, 0.25, 1.0 / 3.0, 0.5, 1.0):
                    T.op(dve, lambda c=c: nc.vector.scalar_tensor_tensor(
                        out=dst, in0=dst, scalar=c, in1=src, op0=ALU.add, op1=ALU.mult),
                        reads=[B_dec], writes=[B_dec])
                T.op(dve, lambda: nc.vector.tensor_scalar(out=dst, in0=dst, scalar1=-1.0, scalar2=None, op0=ALU.mult),
                     reads=[B_dec], writes=[B_dec])

            chk('g')
            log1p_neg(lgsel, rdsel, 8)
            log1p_neg(lgrow, rdrow, 16)
            T.op(dve, lambda: nc.vector.tensor_scalar(out=nlgrow, in0=lgrow, scalar1=-1.0, scalar2=None, op0=ALU.mult),
                 reads=[B_dec], writes=[B_dec])
            T.op(act, lambda: nc.scalar.activation(out=CDv, in_=lgsel, func=AF.Exp, scale=128.0),
                 reads=[B_dec], writes=[B_dec])
            chk('h')
            ln8 = math.log(0.125)
            lnb = sb("lnb", [128, 1], F32)
            T.op(pool, lambda: nc.gpsimd.memset(lnb, ln8), writes=[B_dec])
            T.op(dve, lambda: nc.vector.tensor_scalar(out=kdcol[:, 0:8], in0=lgrow[:, 0:8], scalar1=jp[:, 0:1], scalar2=None,
                                                      op0=ALU.mult), reads=[B_dec, B_const], writes=[B_dec])
            T.op(dve, lambda: nc.vector.tensor_scalar(out=kdcol[:, 8:16], in0=lgrow[:, 8:16], scalar1=jp[:, 1:2],
                                                      scalar2=None, op0=ALU.mult), reads=[B_dec, B_const], writes=[B_dec])
            T.op(act, lambda: nc.scalar.activation(out=kdcol, in_=kdcol, func=AF.Exp, bias=lnb),
                 reads=[B_dec], writes=[B_dec])
            chk('i')
            for h in range(8):
                T.op(act, lambda h=h: nc.scalar.activation(out=DC[:, h, :], in_=diff, func=AF.Exp,
                                                           scale=lgrow[:, h:h + 1]),
                     reads=[B_dec, B_const], writes=[B_dec])
                T.op(act, lambda h=h: nc.scalar.activation(out=QD[:, h, :], in_=diff, func=AF.Exp,
                                                           scale=nlgrow[:, 8 + h:9 + h]),
                     reads=[B_dec, B_const], writes=[B_dec])
            T.op(dve, lambda: nc.vector.tensor_tensor(out=DC, in0=DC, in1=mx8.to_broadcast([128, 8, 128]),
                                                      op=ALU.mult), reads=[B_dec, B_const], writes=[B_dec])
            T.op(dve, lambda: nc.vector.tensor_tensor(out=QD, in0=QD, in1=my8.to_broadcast([128, 8, 128]),
                                                      op=ALU.mult), reads=[B_dec, B_const], writes=[B_dec])
            T.op(dve, lambda: nc.vector.tensor_tensor(out=DC, in0=DC, in1=QD, op=ALU.add),
                 reads=[B_dec], writes=[B_dec])
            for h in range(8):
                T.op(act, lambda h=h: nc.scalar.activation(out=QD[:, h, :], in_=iq, func=AF.Exp,
                                                           scale=lgsel[:, h:h + 1]),
                     reads=[B_dec, B_const], writes=[B_dec])

        if STOP == 0:
            T.finish()
            return nc

    if True:
        ph2 = Arena(BIG + 64 * KB, TOP - 65 * KB)
        ropeC = sb("ropeC", [128, 2048], F32, ph2)
        ropeS = sb("ropeS", [128, 2048], F32, ph2)
        ckT = sb("ckT", [128, 8, 256], BF16, ph2)
        cvb = sb("cvb", [128, 2, D], BF16, ph2)
        WA0 = sb("WA0", [128, 8, WA_COLS], BF16, ph2)
        _wa1_off = ph2.cur
        cst = sb("cst", [128, D], F32, ph2)
        ph2.cur = _wa1_off
        WA1 = sb("WA1", [128, 8, WA_COLS], BF16, ph2)
        WA = [WA0, WA1]
        B_rope = T.b("rope")
        B_WA = [T.b("WA", i) for i in range(2)]
        B_ctx = T.b("ctx")

        def load_WA(h):
            v = wa_d[h].rearrange("(kt p) c -> p kt c", p=128)
            T.dma(pool, [(WA[h % 2], v)], writes=[B_WA[h % 2]])

        load_WA(0)

    if True:
        ph = Arena(TOP - 65 * KB, TOP - 32 * KB)
        NB1 = 4
        XT = [sb(f"XT{i}", [128, D], F32, ph) for i in range(NB1)]
        XN = [sb(f"XN{i}", [128, D], F32, ph) for i in range(NB1)]
        st6 = [sb(f"st6_{i}", [128, 2, 6], F32, ph) for i in range(NB1)]
        mv = [sb(f"mv{i}", [128, 2], F32, ph) for i in range(NB1)]
        rs = [sb(f"rs{i}", [128, 1], F32, ph) for i in range(NB1)]
        B_xt = [T.b("XT", i) for i in range(NB1)]
        B_xn = [T.b("XN", i) for i in range(NB1)]
        B_stt = [T.b("stt", i) for i in range(NB1)]

        def p1_A(g):
            s = g % NB1
            T.dma(sp, [(XT[s], xs[g * 128:(g + 1) * 128, :])], writes=[B_xt[s]])
            for c2 in range(2):
                T.op(dve, lambda c2=c2: nc.vector.bn_stats(out=st6[s][:, c2, :],
                                                           in_=XT[s][:, c2 * 512:(c2 + 1) * 512]),
                     reads=[B_xt[s]], writes=[B_stt[s]])
            T.op(dve, lambda: nc.vector.bn_aggr(out=mv[s], in_=st6[s]), reads=[B_stt[s]], writes=[B_stt[s]])
            T.op(dve, lambda: nc.vector.tensor_scalar(out=rs[s], in0=mv[s][:, 1:2], scalar1=MOD_EPS,
                                                      scalar2=None, op0=ALU.add),
                 reads=[B_stt[s]], writes=[B_stt[s]])
            T.op(act, lambda: nc.scalar.activation(out=rs[s], in_=rs[s], func=AF.Sqrt),
                 reads=[B_stt[s]], writes=[B_stt[s]])

        def p1_B(g):
            s = g % NB1
            T.op(dve, lambda: nc.vector.reciprocal(out=rs[s], in_=rs[s]), reads=[B_stt[s]], writes=[B_stt[s]])
            T.op(dve, lambda: nc.vector.scalar_tensor_tensor(
                out=mv[s][:, 1:2], in0=mv[s][:, 0:1], scalar=-1.0, in1=rs[s], op0=ALU.mult, op1=ALU.mult),
                reads=[B_stt[s]], writes=[B_stt[s]])
            T.op(act, lambda: nc.scalar.activation(out=XN[s], in_=XT[s], func=AF.Identity,
                                                   scale=rs[s], bias=mv[s][:, 1:2]),
                 reads=[B_xt[s], B_stt[s]], writes=[B_xn[s]])

        def p1_C(g):
            s = g % NB1
            st, tt = g // 4, g % 4
            ci = 0 if st == 0 else 1
            for kt in range(8):
                T.op(pe, lambda kt=kt: nc.tensor.transpose(
                    PS[:, kt, tt * 128:(tt + 1) * 128], XN[s][:, kt * 128:(kt + 1) * 128], ident),
                    reads=[B_xn[s], B_const], writes=[psb[kt]])
            if tt == 3:
                for kt in range(8):
                    if kt % 4 == 0:
                        T.op(dve, lambda kt=kt: nc.vector.tensor_scalar(
                            out=hT[:, kt, st * 512:(st + 1) * 512], in0=PS[:, kt, :],
                            scalar1=scale1T[:, kt, ci:ci + 1], scalar2=shiftT[:, kt, ci:ci + 1],
                            op0=ALU.mult, op1=ALU.add), reads=[psb[kt], B_modT], writes=[B_hT[st]])
                    else:
                        T.op(act, lambda kt=kt: nc.scalar.activation(
                            out=hT[:, kt, st * 512:(st + 1) * 512], in_=PS[:, kt, :], func=AF.Identity,
                            scale=scale1T[:, kt, ci:ci + 1], bias=shiftT[:, kt, ci:ci + 1]),
                            reads=[psb[kt], B_modT], writes=[B_hT[st]])

        for i_ in range(20 + 2):
            if i_ < 20:
                p1_A(i_)
            if 0 <= i_ - 1 < 20:
                p1_B(i_ - 1)
            if 0 <= i_ - 2 < 20:
                p1_C(i_ - 2)
        T.dma(sp, [(ropeC, ropeC_d), (ropeS, ropeS_d)], writes=[B_rope])
        for t in range(2):
            T.dma(sp, [(cst, ck[t * 128:(t + 1) * 128, :])], writes=[B_WA[1]])
            for hq in range(2):
                k = next_bank()
                T.group(pe, [(lambda h=h, k=k: nc.tensor.transpose(
                    PS[:, k, (h % 4) * 128:(h % 4 + 1) * 128], cst[:, h * 128:(h + 1) * 128], ident))
                    for h in range(hq * 4, hq * 4 + 4)], reads=[B_WA[1], B_const], writes=[psb[k]])
                T.op(act, lambda k=k, hq=hq, t=t: nc.scalar.activation(
                    out=ckT[:, hq * 4:(hq + 1) * 4, t * 128:(t + 1) * 128],
                    in_=PS[:, k, :].rearrange("p (h j) -> p h j", h=4), func=AF.Copy),
                    reads=[psb[k]], writes=[B_ctx])
        for t in range(2):
            T.dma(sp, [(cst, cv[t * 128:(t + 1) * 128, :])], writes=[B_WA[1]])
            T.op(dve, lambda t=t: nc.vector.tensor_copy(out=cvb[:, t, :], in_=cst), reads=[B_WA[1]], writes=[B_ctx])
        load_WA(1)

        T.barrier(skip=[B_rope, B_WA[1]])
        if STOP == 1:
            T.finish()
            return nc

    if True:
        ph = ph2
        ph.end = TOP
        qT = [sb(f"qT{i}", [128, NQ_TOK], BF16, ph) for i in range(2)]
        kT = [sb(f"kT{i}", [128, NTOK], BF16, ph) for i in range(2)]
        vS = [sb(f"vS{i}", [128, 20, 128], BF16, ph) for i in range(2)]
        tza = [sb(f"tza{i}", [128, NQ_TOK], BF16, ph) for i in range(2)]
        E = [sb(f"E{i}", [128, 2, 256], BF16, ph) for i in range(3)]
        r1 = [sb(f"r1_{i}", [128, 512], F32, ph) for i in range(2)]
        r2 = [sb(f"r2_{i}", [128, 512], F32, ph) for i in range(2)]
        tmpz = [sb(f"tmpz{i}", [128, 512], F32, ph) for i in range(2)]
        stage = [sb(f"stage{i}", [128, 256], F32, ph) for i in range(2)]
        dn = sb("dn", [128, 2, 256], F32, ph)
        o2 = sb("o2", [128, 256], F32, ph)
        osq_ = [sb(f"osq{i}", [128, 256], BF16, ph) for i in range(2)]
        dd_ = [sb(f"dd{i}", [128, 256], F32, ph) for i in range(2)]
        o_all_ = [sb(f"o_all{i}", [128, 512], F32, ph) for i in range(2)]
        v_all_ = [sb(f"v_all{i}", [128, 512], F32, ph) for i in range(2)]

        B_qT = [[T.b("qT", i, g) for g in range(3)] for i in range(2)]
        B_kT = [[T.b("kT", i, g) for g in range(5)] for i in range(2)]
        B_vS = [[T.b("vS", i, g) for g in range(5)] for i in range(2)]
        B_tza = [[T.b("tza", i, g) for g in range(3)] for i in range(2)]
        B_E = [T.b("E", i) for i in range(3)]
        B_r1 = [T.b("r1", i) for i in range(2)]
        B_r2 = [T.b("r2", i) for i in range(2)]
        B_tz = [T.b("tmpz", i) for i in range(2)]
        B_stage = [T.b("stage", i) for i in range(2)]
        B_dn = T.b("dn")
        B_o2 = T.b("o2")
        B_osq_ = [T.b("osq", i) for i in range(2)]
        B_dd_ = [T.b("dd", i) for i in range(2)]
        B_oall_ = [[T.b("o_all", bp, i) for i in range(2)] for bp in range(2)]
        B_vall_ = [[T.b("v_all", bp, i) for i in range(2)] for bp in range(2)]
        gcount = [0]
        bcount = [0]
        B_oag = [[T.b("oag", h, g) for g in range(3)] for h in range(8)]

        chk('p2a')
        rr = [0]

        def rot2():
            rr[0] ^= 1
            return rr[0]

        pbank = [0]
        PB = [5, 6, 7]

        def pool_bank():
            pbank[0] = (pbank[0] + 1) % 3
            return PB[pbank[0]]

        Q2 = [sb(f"Q2_{i}", [128, 512], BF16, ph) for i in range(2)]
        B_Q2 = [T.b("Q2", i) for i in range(2)]
        for q_ in range(2):
            T.op(pool, lambda q_=q_: nc.gpsimd.memset(Q2[q_], 0.0), writes=[B_Q2[q_]])
        q2c = [0]
        q2carry = {}

        def build_Q2(i, qc0, g):
            q2c[0] ^= 1
            u = q2c[0]
            T.op(pool, lambda: nc.gpsimd.tensor_copy(out=Q2[u][0:64, 0:256], in_=qT[i][0:64, qc0:qc0 + 256]),
                 reads=[B_qT[i][g]], writes=[B_Q2[u]])
            T.op(pool, lambda: nc.gpsimd.tensor_copy(out=Q2[u][64:128, 256:512], in_=qT[i][64:128, qc0:qc0 + 256]),
                 reads=[B_qT[i][g]], writes=[B_Q2[u]])
            return u

        def mk_chunks(h):
            i = h % 2
            W = WA[i]
            ch = []

            def fgroup(c0, st, ncols=128):
                k = pool_bank()
                T.group(pe, [(lambda kt=kt, k=k: nc.tensor.matmul(
                    PS[0:ncols, k, :], lhsT=W[:, kt, c0:c0 + ncols], rhs=hT[:, kt, st * 512:(st + 1) * 512],
                    start=(kt == 0), stop=(kt == 7))) for kt in range(8)],
                    reads=[B_WA[i], B_hT[st]], writes=[psb[k]])
                return k

            def plain(c0, dst, dbuf):
                k = fgroup(c0, 0)
                T.op(dve, lambda k=k: nc.vector.tensor_copy(out=dst, in_=PS[:, k, :]),
                     reads=[psb[k]], writes=[dbuf])

            def rope_evac(c_plain, c_sw, st, dst, dbuf):
                ka_ = fgroup(c_plain, st)
                kb_ = fgroup(c_sw, st)
                j = rot2()
                rc = (st - 1) * 512
                T.op(dve, lambda: nc.vector.tensor_tensor(out=r1[j], in0=PS[:, ka_, :],
                                                          in1=ropeC[:, rc:rc + 512], op=ALU.mult),
                     reads=[psb[ka_], B_rope], writes=[B_r1[j]])
                T.op(dve, lambda: nc.vector.tensor_tensor(out=r2[j], in0=PS[:, kb_, :],
                                                          in1=ropeS[:, rc:rc + 512], op=ALU.mult),
                     reads=[psb[kb_], B_rope], writes=[B_r2[j]])
                T.op(pool, lambda: nc.gpsimd.tensor_tensor(out=dst, in0=r1[j], in1=r2[j], op=ALU.add),
                     reads=[B_r1[j], B_r2[j]], writes=[dbuf])

            def zgate(st):
                k = fgroup(384, st)
                j = rot2()
                T.op(act, lambda: nc.scalar.activation(out=tmpz[j], in_=PS[:, k, :], func=AF.Exp, scale=-1.0),
                     reads=[psb[k]], writes=[B_tz[j]])
                T.op(dve, lambda: nc.vector.tensor_copy(out=r2[j], in_=PS[:, k, :]),
                     reads=[psb[k]], writes=[B_r2[j]])
                T.op(act, lambda: nc.scalar.activation(out=tmpz[j], in_=tmpz[j], func=AF.Ln, bias=one_col),
                     reads=[B_tz[j], B_c2], writes=[B_tz[j]])
                T.op(act, lambda: nc.scalar.activation(out=tmpz[j], in_=tmpz[j], func=AF.Exp, scale=-1.0),
                     reads=[B_tz[j]], writes=[B_tz[j]])
                T.op(pool, lambda: nc.gpsimd.tensor_tensor(
                    out=tza[i][:, st * 512:(st + 1) * 512], in0=r2[j], in1=tmpz[j], op=ALU.mult),
                    reads=[B_tz[j], B_r2[j]], writes=[B_tza[i][st]])

            def ptile(tt):
                k = pool_bank()
                T.group(pe, [(lambda kt=kt: nc.tensor.matmul(
                    PS[:, k, 0:256], lhsT=hT[:, kt, tt * 128:(tt + 1) * 128], rhs=W[:, kt, 512:768],
                    start=(kt == 0), stop=(kt == 7))) for kt in range(8)],
                    reads=[B_WA[i], B_hT[0]], writes=[psb[k]])
                j = rot2()
                T.op(dve, lambda: nc.vector.tensor_copy(out=stage[j], in_=PS[:, k, 0:256]),
                     reads=[psb[k]], writes=[B_stage[j]])
                T.op(dve, lambda: nc.vector.tensor_copy(out=vS[i][:, tt, :], in_=PS[:, k, 128:256]),
                     reads=[psb[k]], writes=[B_vS[i][0]])
                T.dma(sp, [(nk_d[tt * 128:(tt + 1) * 128, h * 128:(h + 1) * 128], stage[j][:, 0:128]),
                           (nv_d[tt * 128:(tt + 1) * 128, h * 128:(h + 1) * 128], stage[j][:, 128:256])],
                      reads=[B_stage[j]])

            def vtiles(st):
                k = pool_bank()
                fns = []
                for t4 in range(4):
                    tt = st * 4 + t4
                    for kt in range(8):
                        fns.append(lambda kt=kt, tt=tt, t4=t4: nc.tensor.matmul(
                            PS[:, k, t4 * 128:(t4 + 1) * 128], lhsT=hT[:, kt, tt * 128:(tt + 1) * 128],
                            rhs=W[:, kt, 640:768], start=(kt == 0), stop=(kt == 7)))
                T.group(pe, fns, reads=[B_WA[i], B_hT[st]], writes=[psb[k]])
                T.op(dve, lambda: nc.vector.tensor_copy(
                    out=vS[i][:, st * 4:(st + 1) * 4, :], in_=PS[:, k, :].rearrange("p (t d) -> p t d", t=4)),
                    reads=[psb[k]], writes=[B_vS[i][st]])

            def seq(*fs):
                return lambda: [f() for f in fs]

            qrope = lambda st: (lambda: rope_evac(0, 128, st, qT[i][:, st * 512:(st + 1) * 512], B_qT[i][st]))
            krope = lambda st: (lambda: rope_evac(512, 256, st, kT[i][:, st * 512:(st + 1) * 512], B_kT[i][st]))
            ch.append(seq(lambda: plain(0, qT[i][:, 0:512], B_qT[i][0]), lambda: plain(512, kT[i][:, 0:512], B_kT[i][0])))
            ch.append(seq(lambda: ptile(0), lambda: ptile(1)))
            ch.append(seq(lambda: ptile(2), lambda: ptile(3)))
            ch.append(seq(lambda: zgate(0), qrope(1)))
            ch.append(qrope(2))
            ch.append(seq(lambda: zgate(1), krope(1)))
            ch.append(krope(2))
            ch.append(seq(lambda: zgate(2), krope(3)))
            ch.append(krope(4))
            ch.append(seq(lambda: vtiles(1), lambda: vtiles(2)))
            ch.append(seq(lambda: vtiles(3), lambda: vtiles(4)))
            return ch

        deferred = []
        act_heavy = []
        stepc = [0]

        def run_deferred(force=False):
            while deferred and (force or deferred[0][0] <= stepc[0]):
                deferred.pop(0)[1]()

        def attention_head(h, pend):
            i = h % 2
            groups = []
            for pb in range(2):
                keys = [(kT[i][:, pb * 256 + t * 128:pb * 256 + (t + 1) * 128], vS[i][:, pb * 2 + t, :],
                         (B_kT[i][0], B_vS[i][0])) for t in range(2)]
                groups.append((pb * 256, keys, 0, pb))
            skeys = [(ckT[:, h, t * 128:(t + 1) * 128], cvb[:, t, h * 128:(h + 1) * 128], (B_ctx,))
                     for t in range(2)]
            for t in range(16):
                st = 1 + t // 4
                skeys.append((kT[i][:, 512 + t * 128:512 + (t + 1) * 128], vS[i][:, 4 + t, :],
                              (B_kT[i][st], B_vS[i][st])))
            for qt in range(4):
                groups.append((512 + qt * 256, skeys, 1 + qt // 2, qt % 2))
            nsteps = sum(len(g_[1]) for g_ in groups)
            done = 0
            emitted = 0
            nq = 256
            nextq = q2carry.pop('q') if 'q' in q2carry else build_Q2(i, groups[0][0], groups[0][2])
            for gix, (qc0, keys, g, half) in enumerate(groups):
                n = len(keys)
                qu = nextq

                def S(kt):
                    kap, vap, kb = keys[kt]
                    T.op(pe, lambda: nc.tensor.matmul(PS[:, kt % 3, :], lhsT=kap, rhs=Q2[qu], start=True, stop=True),
                         reads=list(kb) + [B_Q2[qu]], writes=[psb[kt % 3]])

                S(0)
                if n > 1:
                    S(1)
                if gix + 1 < len(groups):
                    nextq = build_Q2(i, groups[gix + 1][0], groups[gix + 1][2])
                elif h + 1 < 8:
                    q2carry['q'] = build_Q2((h + 1) % 2, 0, 0)
                for kt in range(n):
                    if kt + 2 < n:
                        S(kt + 2)
                    a = kt % 3
                    e = kt % 3
                    T.op(act, lambda: nc.scalar.activation(
                        out=E[e], in_=PS[:, a, :].rearrange("p (m q) -> p m q", m=2), func=AF.Exp, scale=0.125),
                        reads=[psb[a]], writes=[B_E[e]])
                    kap, vap, kb = keys[kt]
                    fns = []
                    for m in range(2):
                        fns.append(lambda m=m: nc.tensor.matmul(
                            PS[:, 3, m * nq:(m + 1) * nq], lhsT=vap, rhs=E[e][:, m, :], start=(kt == 0 and m == 0),
                            stop=(kt == n - 1), skip_group_check=True))
                        fns.append(lambda m=m: nc.tensor.matmul(
                            PS[:, 4, m * nq:(m + 1) * nq], lhsT=ones_b, rhs=E[e][:, m, :], start=(kt == 0 and m == 0),
                            stop=(kt == n - 1), skip_group_check=True))
                    T.group(pe, fns, reads=[B_E[e]] + list(kb) + [B_c2], writes=[psb[3], psb[4]])
                    done += 1
                    stepc[0] += 1
                    run_deferred()
                    while emitted < len(pend) and emitted * nsteps < done * len(pend):
                        while act_heavy and act_heavy[0][0] <= stepc[0] and not any(d_[2] <= act_heavy[0][2] for d_ in deferred):
                            act_heavy.pop(0)[1]()
                        pend[emitted]()
                        emitted += 1
                    if not pend:
                        while act_heavy and act_heavy[0][0] <= stepc[0] and not any(d_[2] <= act_heavy[0][2] for d_ in deferred):
                            act_heavy.pop(0)[1]()
                gi = gcount[0]
                gcount[0] += 1
                gp = gi % 2
                bp = bcount[0] % 2
                if half == 1:
                    bcount[0] += 1
                while deferred and deferred[0][2] <= gi - 2:
                    deferred.pop(0)[1]()
                while act_heavy and act_heavy[0][2] <= gi - 2:
                    act_heavy.pop(0)[1]()
                while deferred and deferred[0][2] <= gi - 2:
                    deferred.pop(0)[1]()
                for d_ in [d_ for d_ in deferred if d_[2] == gi - 1 and len(d_) > 3 and d_[3] == "st1"]:
                    deferred.remove(d_)
                    d_[1]()
                o_all, v_all = o_all_[bp], v_all_[bp]
                B_oall, B_vall = B_oall_[bp], B_vall_[bp]
                osq, dd, B_osq, B_dd = osq_[gp], dd_[gp], B_osq_[gp], B_dd_[gp]
                oa = o_all[:, half * nq:(half + 1) * nq]
                va = v_all[:, half * nq:(half + 1) * nq]
                T.op(dve, lambda: nc.vector.tensor_scalar(
                    out=dn, in0=PS[:, 4, :].rearrange("p (m q) -> p m q", m=2), scalar1=2.0 ** -12, scalar2=None,
                    op0=ALU.mult), reads=[psb[4]], writes=[B_dn])
                T.op(dve, lambda: nc.vector.tensor_tensor(out=oa, in0=PS[:, 3, 0:nq], in1=dn[:, 1, :], op=ALU.mult),
                     reads=[psb[3], B_dn], writes=[B_oall[half]])
                T.op(dve, lambda: nc.vector.tensor_tensor(out=o2, in0=PS[:, 3, nq:2 * nq], in1=dn[:, 0, :],
                                                          op=ALU.mult), reads=[psb[3], B_dn], writes=[B_o2])
                T.op(pool, lambda: nc.gpsimd.tensor_tensor(out=dd, in0=dn[:, 0, :], in1=dn[:, 1, :], op=ALU.mult),
                     reads=[B_dn], writes=[B_dd])
                if emitted < len(pend):
                    pend[emitted]()
                    emitted += 1

                def st1(oa=oa, half=half, B_oall=B_oall, dd=dd, B_dd=B_dd):
                    T.op(dve, lambda: nc.vector.scalar_tensor_tensor(
                        out=oa, in0=o2, scalar=neglam, in1=oa, op0=ALU.mult, op1=ALU.add),
                        reads=[B_o2, B_oall[half], B_lam], writes=[B_oall[half]])
                    T.op(pool, lambda: nc.gpsimd.tensor_tensor(out=dd, in0=dd, in1=dd, op=ALU.mult),
                         reads=[B_dd], writes=[B_dd])

                def st2(oa=oa, half=half, B_oall=B_oall, osq=osq, B_osq=B_osq):
                    T.op(dve, lambda: nc.vector.tensor_tensor(out=osq, in0=oa, in1=oa, op=ALU.mult),
                         reads=[B_oall[half]], writes=[B_osq])

                hold = {}

                def st3(osq=osq, B_osq=B_osq, hold=hold):
                    k = pool_bank()
                    hold["k"] = k
                    T.op(pe, lambda: nc.tensor.matmul(PS[:, k, 0:nq], lhsT=jmat, rhs=osq, start=True, stop=True),
                         reads=[B_osq, B_c2], writes=[psb[k]])

                def st4(va=va, half=half, B_vall=B_vall, dd=dd, B_dd=B_dd, hold=hold):
                    k = hold["k"]
                    T.op(dve, lambda: nc.vector.scalar_tensor_tensor(
                        out=va, in0=dd, scalar=LN_EPS * (2.0 ** 24), in1=PS[:, k, 0:nq], op0=ALU.mult, op1=ALU.add),
                        reads=[B_dd, psb[k]], writes=[B_vall[half]])

                deferred.append((stepc[0] + 1, st1, gi, "st1"))
                deferred.append((stepc[0] + 2, st2, gi))
                deferred.append((stepc[0] + 3, st3, gi))
                deferred.append((stepc[0] + 5, st4, gi))
                if half == 1:
                    def b1(v_all=v_all, B_vall=B_vall):
                        T.op(act, lambda: nc.scalar.activation(out=v_all, in_=v_all, func=AF.Ln),
                             reads=B_vall, writes=B_vall)
                        T.op(act, lambda: nc.scalar.activation(out=v_all, in_=v_all, func=AF.Exp, scale=-0.5),
                             reads=B_vall, writes=B_vall)

                    def b2(v_all=v_all, B_vall=B_vall, g=g, i=i):
                        c0 = g * 512
                        T.op(pool, lambda: nc.gpsimd.tensor_tensor(out=v_all, in0=v_all, in1=tza[i][:, c0:c0 + 512],
                                                                   op=ALU.mult),
                             reads=B_vall + [B_tza[i][g]], writes=B_vall)

                    def b3(v_all=v_all, B_vall=B_vall, o_all=o_all, B_oall=B_oall, g=g, h=h):
                        c0 = g * 512
                        T.op(dve, lambda: nc.vector.scalar_tensor_tensor(
                            out=oagT[:, h, c0:c0 + 512], in0=o_all, scalar=gsc, in1=v_all, op0=ALU.mult,
                            op1=ALU.mult), reads=B_oall + B_vall + [B_lam], writes=[B_oag[h][g]])

                    def b1x(b1=b1, b2=b2, b3=b3, gi=gi):
                        b1()
                        deferred.append((stepc[0] + 3, b2, gi))
                        deferred.append((stepc[0] + 5, b3, gi))
                    act_heavy.append((stepc[0] + 7, b1x, gi))
            while emitted < len(pend):
                pend[emitted]()
                emitted += 1

        ph3pre = Arena(BIG + 88 * KB, BIG + 88 * KB + 2 * 9216)
        WR = [sb(f"WR{i}", [128, 8, WR_COLS], BF16, ph3pre) for i in range(2)]
        B_WR = [T.b("WR", i) for i in range(2)]

        def load_WR(h, extra=()):
            v = wr_d[h].rearrange("(kt p) c -> p kt c", p=128)
            T.dma(pool, [(WR[h % 2], v)], writes=[B_WR[h % 2]] + list(extra))

        for c_ in mk_chunks(0):
            c_()
        emit_decay_tables()
        for h in range(8):
            if h == 7:
                load_WR(0, extra=[B_WA[0], B_WA[1]])
                load_WR(1, extra=[B_WA[0], B_WA[1]])
            pend = mk_chunks(h + 1) if h + 1 < 8 else []
            if h + 2 < 8:
                load_WA(h + 2)
            attention_head(h, pend)
        run_deferred(force=True)
        while act_heavy:
            act_heavy.pop(0)[1]()
        run_deferred(force=True)
        T.barrier()
        if STOP == 2:
            T.finish()
            return nc

    if True:
        ph = Arena(BIG + 88 * KB + 2 * 9216, TOP)
        qpl = [sb(f"qpl{i}", [64, NQ_TOK], BF16, ph) for i in range(2)]
        qdT = [sb(f"qdT{i}", [128, NQ_TOK], BF16, ph) for i in range(2)]
        krT = [sb(f"krT{i}", [64, NQ_TOK], BF16, ph) for i in range(2)]
        kd = [sb(f"kd{i}", [128, 20, 128], BF16, ph) for i in range(2)]
        vr = [sb(f"vr{i}", [128, 20, 128], BF16, ph) for i in range(2)]
        tzr = [sb(f"tzr{i}", [128, NQ_TOK], BF16, ph) for i in range(2)]
        tmpz = sb("tmpzr", [128, 512], F32, ph)
        KDh = [sb(f"KDh{i}", [128, 1, 128], F32, ph) for i in range(2)]
        B_KDh = [T.b("KDh", i) for i in range(2)]
        zc = sb("zcr", [128, 512], F32, ph)
        S_ = sb("stS", [128, 12, 128], F32, ph)
        stB = sb("stB", [128, 12, 128], BF16, ph)
        yc = sb("ycar", [128, 2, 128], F32, ph)
        fin = [sb(f"fin{i}", [128, 2, 128], F32, ph) for i in range(2)]
        innT = [sb(f"innT{i}", [128, 4, 128], BF16, ph) for i in range(3)]
        ob = [t_.rearrange("p c j -> p (c j)") for t_ in innT]
        sqb = ob
        rstd = [sb(f"rstd3_{i}", [128, 512], F32, ph) for i in range(3)]

        B_qpl = [[T.b("qpl", i, g) for g in range(3)] for i in range(2)]
        B_qd = [[T.b("qd", i, g) for g in range(3)] for i in range(2)]
        B_krT = [[T.b("krT", i, g) for g in range(3)] for i in range(2)]
        B_kd = [[T.b("kd", i, g) for g in range(5)] for i in range(2)]
        B_vr = [[T.b("vr", i, g) for g in range(5)] for i in range(2)]
        B_tzr = [[T.b("tzr", i, g) for g in range(3)] for i in range(2)]
        B_tz = T.b("tmpzr")
        B_zc = T.b("zcr")
        B_st = T.b("stS")
        B_stb = T.b("stB")
        B_fin = [T.b("fin", i) for i in range(2)]
        B_inn = [T.b("innT", i) for i in range(3)]
        B_ob = B_inn
        B_sqb = B_inn
        B_rstd = [T.b("rstd3", i) for i in range(3)]
        B_org = [[T.b("org", h, g) for g in range(3)] for h in range(8)]

        pb3 = [0]
        PB3 = [5, 6, 7]

        def pool_bank3():
            pb3[0] = (pb3[0] + 1) % 3
            return PB3[pb3[0]]

        def mk_chunks3(h):
            i = h % 2
            W = WR[i]

            def fgroup(c0, st, ncols=128):
                k = pool_bank3()
                T.group(pe, [(lambda kt=kt: nc.tensor.matmul(
                    PS[0:ncols, k, :], lhsT=W[:, kt, c0:c0 + ncols], rhs=hT[:, kt, st * 512:(st + 1) * 512],
                    start=(kt == 0), stop=(kt == 7))) for kt in range(8)],
                    reads=[B_WR[i], B_hT[st]], writes=[psb[k]])
                return k

            def qq(st):
                k = fgroup(0, st)
                T.op(dve, lambda: nc.vector.tensor_copy(out=qpl[i][:, st * 512:(st + 1) * 512], in_=PS[0:64, k, :]),
                     reads=[psb[k]], writes=[B_qpl[i][st]])
                T.op(dve, lambda: nc.vector.tensor_tensor(
                    out=qdT[i][:, st * 512:(st + 1) * 512].rearrange("p (c j) -> p c j", c=4),
                    in0=PS[:, k, :].rearrange("p (c j) -> p c j", c=4),
                    in1=QD[:, h:h + 1, :].to_broadcast([128, 4, 128]), op=ALU.mult),
                    reads=[psb[k], B_dec], writes=[B_qd[i][st]])

            def kr(st):
                k = fgroup(256, st, 64)
                T.op(act, lambda: nc.scalar.activation(out=krT[i][:, st * 512:(st + 1) * 512], in_=PS[0:64, k, :],
                                                       func=AF.Copy), reads=[psb[k]], writes=[B_krT[i][st]])

            def zr(st):
                k = fgroup(128, st)
                T.op(act, lambda: nc.scalar.activation(out=tmpz, in_=PS[:, k, :], func=AF.Exp, scale=-1.0),
                     reads=[psb[k]], writes=[B_tz])
                T.op(dve, lambda: nc.vector.tensor_copy(out=zc, in_=PS[:, k, :]), reads=[psb[k]], writes=[B_zc])
                T.op(act, lambda: nc.scalar.activation(out=tmpz, in_=tmpz, func=AF.Ln, bias=one_col),
                     reads=[B_tz, B_c2], writes=[B_tz])
                T.op(act, lambda: nc.scalar.activation(out=tmpz, in_=tmpz, func=AF.Exp, scale=-1.0),
                     reads=[B_tz], writes=[B_tz])
                T.op(pool, lambda: nc.gpsimd.tensor_tensor(out=tzr[i][:, st * 512:(st + 1) * 512], in0=zc, in1=tmpz,
                                                           op=ALU.mult),
                     reads=[B_tz, B_zc], writes=[B_tzr[i][st]])

            def tpair(tp):
                k = pool_bank3()
                st = tp // 2
                fns = []
                for t2 in range(2):
                    tt = tp * 2 + t2
                    for kt in range(8):
                        fns.append(lambda kt=kt, tt=tt, t2=t2: nc.tensor.matmul(
                            PS[:, k, t2 * 256:(t2 + 1) * 256], lhsT=hT[:, kt, tt * 128:(tt + 1) * 128],
                            rhs=W[:, kt, 320:576], start=(kt == 0), stop=(kt == 7)))
                T.group(pe, fns, reads=[B_WR[i], B_hT[st]], writes=[psb[k]])
                pv = PS[:, k, :].rearrange("p (t c) -> p t c", t=2)
                T.op(dve, lambda: nc.vector.tensor_tensor(
                    out=kd[i][:, tp * 2:tp * 2 + 2, :], in0=pv[:, :, 0:128],
                    in1=KDh[i].to_broadcast([128, 2, 128]), op=ALU.mult),
                    reads=[psb[k], B_KDh[i]], writes=[B_kd[i][st]])
                T.op(act, lambda: nc.scalar.activation(out=vr[i][:, tp * 2:tp * 2 + 2, :], in_=pv[:, :, 128:256],
                                                       func=AF.Copy), reads=[psb[k]], writes=[B_vr[i][st]])

            def seq(*fs):
                return lambda: [f() for f in fs]

            def mk_kdh():
                T.op(pool, lambda: nc.gpsimd.tensor_copy(out=KDh[i][:, 0, 0:64],
                                                         in_=kdcol[:, h:h + 1].to_broadcast([128, 64])),
                     reads=[B_dec], writes=[B_KDh[i]])
                T.op(pool, lambda: nc.gpsimd.tensor_copy(out=KDh[i][:, 0, 64:128],
                                                         in_=kdcol[:, 8 + h:9 + h].to_broadcast([128, 64])),
                     reads=[B_dec], writes=[B_KDh[i]])

            mk_kdh()
            ch = []
            fch = [seq(lambda st=st: zr(st), lambda st=st: qq(st), lambda st=st: kr(st)) for st in range(3)]
            ch.append(seq(lambda: tpair(9), lambda: tpair(8)))
            ch.append(fch[0])
            ch.append(seq(lambda: tpair(7), lambda: tpair(6)))
            ch.append(seq(lambda: tpair(0), lambda: tpair(1)))
            ch.append(fch[1])
            ch.append(seq(lambda: tpair(2), lambda: tpair(3)))
            ch.append(fch[2])
            ch.append(seq(lambda: tpair(4), lambda: tpair(5)))
            return ch

        def mk_ret(h):
            i = h % 2
            ops = []
            cdX = CDv[0:64, h:h + 1]
            cdY = CDv[64:128, h:h + 1]
            ubank = {}

            def Ugrp(g4, bank):
                def f():
                    T.group(pe, [(lambda t4=t4: nc.tensor.matmul(
                        PS[:, bank, t4 * 128:(t4 + 1) * 128], lhsT=kd[i][:, g4 * 4 + t4, :],
                        rhs=vr[i][:, g4 * 4 + t4, :], start=True, stop=True)) for t4 in range(4)],
                        reads=[B_kd[i][g4], B_vr[i][g4]], writes=[psb[bank]])
                for t4 in range(4):
                    ubank[g4 * 4 + t4] = (bank, t4)
                return f

            def U(tile, lo, hi):
                k, t4 = ubank[tile]
                return PS[lo:hi, k, t4 * 128:(t4 + 1) * 128], psb[k]

            def chain(dst, src, cd, tile, lo, hi, wbuf):
                def f():
                    u, ub = U(tile, lo, hi)
                    T.op(dve, lambda: nc.vector.scalar_tensor_tensor(out=dst, in0=src, scalar=cd, in1=u,
                                                                     op0=ALU.mult, op1=ALU.add),
                         reads=[ub, B_st, B_dec, B_const], writes=[wbuf])
                return f

            def ucopy(dst, tile, lo, hi):
                def f():
                    u, ub = U(tile, lo, hi)
                    T.op(dve, lambda: nc.vector.tensor_copy(out=dst, in_=u), reads=[ub], writes=[B_st])
                return f

            ops.append(Ugrp(4, 0))
            ops.append(Ugrp(3, 1))
            prev = s0[64:128, h, :]
            for n_, tile in enumerate(range(19, 11, -1)):
                dst = yc[64:128, n_ % 2, :] if tile > 12 else S_[64:128, 11, :]
                ops.append(chain(dst, prev, cdY, tile, 64, 128, B_st))
                prev = dst
            ops.append(Ugrp(0, 2))
            ops.append(Ugrp(1, 3))
            ops.append(Ugrp(2, 4))
            ops.append(lambda: T.op(pool, lambda: nc.gpsimd.memset(S_[0:64, 0:4:2, :], 0.0), writes=[B_st]))
            ops.append(lambda: T.op(pool, lambda: nc.gpsimd.memset(S_[64:128, 1:4:2, :], 0.0), writes=[B_st]))
            for pb in range(2):
                c0 = pb * 2
                ops.append(ucopy(S_[0:64, c0 + 1, :], c0, 0, 64))
                ops.append(ucopy(S_[64:128, c0, :], c0 + 1, 64, 128))
                ops.append(chain(fin[i][0:64, pb, :], S_[0:64, c0 + 1, :], cdX, c0 + 1, 0, 64, B_fin[i]))
                ops.append(chain(fin[i][64:128, pb, :], S_[64:128, c0, :], cdY, c0, 64, 128, B_fin[i]))
            ops.append(lambda: T.dma(sp, [(nst_d[pb, h], fin[i][:, pb, :]) for pb in range(2)], reads=[B_fin[i]]))
            ops.append(lambda: T.op(dve, lambda: nc.vector.tensor_copy(out=S_[0:64, 4, :], in_=s0[0:64, h, :]),
                                    reads=[B_const], writes=[B_st]))
            for c in range(7):
                ops.append(chain(S_[0:64, 5 + c, :], S_[0:64, 4 + c, :], cdX, 4 + c, 0, 64, B_st))
                cy = 7 - c
                ops.append(chain(S_[64:128, 4 + cy - 1, :], S_[64:128, 4 + cy, :], cdY, 4 + cy, 64, 128, B_st))
            ops.append(lambda: T.op(act, lambda: nc.scalar.activation(out=stB, in_=S_, func=AF.Copy),
                                    reads=[B_st], writes=[B_stb]))
            n_early = len(ops)
            A_b = [0, 1, 2]
            O_b = [3, 4, 0]
            C_b = [1, 2, 3]
            V_b = [4, 0, 4]
            for g in range(3):
                ops.append(lambda g=g: T.group(pe, [(lambda c4=c4: nc.tensor.matmul(
                    PS[:, A_b[g], c4 * 128:(c4 + 1) * 128],
                    lhsT=krT[i][:, (g * 4 + c4) * 128:(g * 4 + c4 + 1) * 128],
                    rhs=qpl[i][:, (g * 4 + c4) * 128:(g * 4 + c4 + 1) * 128], start=True, stop=True))
                    for c4 in range(4)], reads=[B_krT[i][g], B_qpl[i][g]], writes=[psb[A_b[g]]]))
            for g in range(3):
                ops.append(lambda g=g: T.op(dve, lambda: nc.vector.tensor_tensor(
                    out=innT[g], in0=PS[:, A_b[g], :].rearrange("p (c j) -> p c j", c=4),
                    in1=DC[:, h:h + 1, :].to_broadcast([128, 4, 128]), op=ALU.mult),
                    reads=[psb[A_b[g]], B_dec], writes=[B_inn[g]]))
            for g in range(3):
                def og(g=g):
                    fns = []
                    for c4 in range(4):
                        s_ = g * 4 + c4
                        fns.append(lambda c4=c4, s_=s_: nc.tensor.matmul(
                            PS[:, O_b[g], c4 * 128:(c4 + 1) * 128], lhsT=vr[i][:, s_, :], rhs=innT[g][:, c4, :],
                            start=True, stop=False))
                        fns.append(lambda c4=c4, s_=s_: nc.tensor.matmul(
                            PS[:, O_b[g], c4 * 128:(c4 + 1) * 128], lhsT=stB[:, s_, :],
                            rhs=qdT[i][:, s_ * 128:(s_ + 1) * 128], start=False, stop=True))
                    T.group(pe, fns, reads=[B_vr[i][g], B_inn[g], B_stb, B_qd[i][g]], writes=[psb[O_b[g]]])
                ops.append(og)
            for g in range(3):
                ops.append(lambda g=g: T.op(act, lambda: nc.scalar.activation(out=ob[g], in_=PS[:, O_b[g], :],
                                                                              func=AF.Copy),
                                            reads=[psb[O_b[g]]], writes=[B_ob[g]]))
            for g in range(3):
                ops.append(lambda g=g: T.op(pe, lambda: nc.tensor.matmul(PS[:, C_b[g], :], lhsT=cmat, rhs=ob[g],
                                                                         start=True, stop=True),
                                            reads=[B_ob[g], B_c2], writes=[psb[C_b[g]]]))
            for g in range(3):
                ops.append(lambda g=g: T.op(act, lambda: nc.scalar.activation(out=sqb[g], in_=PS[:, C_b[g], :],
                                                                              func=AF.Square),
                                            reads=[psb[C_b[g]]], writes=[B_sqb[g]]))
            for g in range(3):
                ops.append(lambda g=g: T.op(pe, lambda: nc.tensor.matmul(PS[:, V_b[g], :], lhsT=jmat, rhs=sqb[g],
                                                                         start=True, stop=True),
                                            reads=[B_sqb[g], B_c2], writes=[psb[V_b[g]]]))
                ops.append(lambda g=g: T.op(dve, lambda: nc.vector.tensor_scalar(
                    out=rstd[g], in0=PS[:, V_b[g], :], scalar1=LN_EPS, scalar2=None, op0=ALU.add),
                    reads=[psb[V_b[g]]], writes=[B_rstd[g]]))
            for g in range(3):
                ops.append(lambda g=g: T.op(act, lambda: nc.scalar.activation(out=rstd[g], in_=rstd[g], func=AF.Ln),
                                            reads=[B_rstd[g]], writes=[B_rstd[g]]))
            for g in range(3):
                ops.append(lambda g=g: T.op(act, lambda: nc.scalar.activation(out=rstd[g], in_=rstd[g], func=AF.Exp,
                                                                              scale=-0.5),
                                            reads=[B_rstd[g]], writes=[B_rstd[g]]))
            for g in range(3):
                ops.append(lambda g=g: T.op(pool, lambda: nc.gpsimd.tensor_tensor(
                    out=rstd[g], in0=rstd[g], in1=tzr[i][:, g * 512:(g + 1) * 512], op=ALU.mult),
                    reads=[B_rstd[g], B_tzr[i][g]], writes=[B_rstd[g]]))
            for g in range(3):
                ops.append(lambda g=g: T.op(dve, lambda: nc.vector.scalar_tensor_tensor(
                    out=orgT[:, h, g * 512:(g + 1) * 512], in0=PS[:, C_b[g], :], scalar=gngT[:, h:h + 1],
                    in1=rstd[g], op0=ALU.mult, op1=ALU.mult),
                    reads=[psb[C_b[g]], B_rstd[g], B_const], writes=[B_org[h][g]]))
            return ops, n_early

        ph4pre = Arena(BIG + 88 * KB, BIG + 88 * KB + 2 * 9216)
        WG = [sb(f"WG{i}", [128, 8, 512], BF16, ph4pre) for i in range(2)]
        B_WG = [T.b("WG", i) for i in range(2)]

        def load_WG(j, extra=()):
            v = wg_d[j].rearrange("(kt p) c -> p kt c", p=128)
            T.dma(pool, [(WG[j % 2], v)], writes=[B_WG[j % 2]] + list(extra))

        for c_ in mk_chunks3(0):
            c_()
        for h in range(8):
            ret, n_early = mk_ret(h)
            if h + 2 < 8:
                load_WR(h + 2)
            if h == 7:
                load_WG(0, extra=[B_WR[0], B_WR[1]])
                load_WG(1, extra=[B_WR[0], B_WR[1]])
            chs = mk_chunks3(h + 1) if h + 1 < 8 else []
            nr, ncs = len(ret), len(chs)
            ri = 0
            n_e_ch = 2
            for ci, c_ in enumerate(chs):
                c_()
                if ci < n_e_ch:
                    tgt = (ci + 1) * n_early // n_e_ch
                else:
                    tgt = n_early + (ci + 1 - n_e_ch) * (nr - n_early) // (ncs - n_e_ch)
                while ri < tgt:
                    ret[ri]()
                    ri += 1
            while ri < nr:
                ret[ri]()
                ri += 1
        T.barrier()
        if STOP == 3:
            T.finish()
            return nc

    if True:
        ph = Arena(BIG + 112 * KB, TOP)
        ph4b = Arena(BIG, BIG + 88 * KB)
        WO = sb("WO", [128, 8, D], BF16, ph)
        mT = sb("mT", [128, 8, NQ_TOK], BF16, ph)
        lnrow = sb("lnrow", [128, 2, D], F32, ph)
        ta = [sb(f"ta{i}", [128, 512], F32, ph) for i in range(2)]
        tr_ = [sb(f"tr{i}", [128, 512], F32, ph) for i in range(2)]
        XT = [sb(f"XT4_{i}", [128, D], F32, ph4b) for i in range(4)]
        Z = [sb(f"Z{i}", [128, D], F32, ph4b) for i in range(4)]
        Y = [sb(f"Y{i}", [128, D], F32, ph4b) for i in range(4)]
        st6 = [sb(f"st6b_{i}", [128, 2, 6], F32, ph) for i in range(4)]
        mv = [sb(f"mvb{i}", [128, 2], F32, ph) for i in range(4)]
        rs = [sb(f"rsb{i}", [128, 1], F32, ph) for i in range(4)]
        B_WO = T.b("WO")
        B_mT = [[T.b("mT", j, t) for t in range(3)] for j in range(8)]
        B_ta = [T.b("ta", i) for i in range(2)]
        B_tr = [T.b("tr", i) for i in range(2)]
        B_x4 = [T.b("XT4", i) for i in range(4)]
        B_z = [T.b("Z", i) for i in range(4)]
        B_y = [T.b("Y", i) for i in range(4)]
        B_s4 = [T.b("s4", i) for i in range(4)]

        wov = wout_d.rearrange("(kt p) c -> p kt c", p=128)
        T.dma(pool, [(WO, wov)], writes=[B_WO])
        B_ln = T.b("lnrow")
        T.dma(sp, [(lnrow, lnrow_d)], writes=[B_ln])
        rr = [0]
        for j in range(8):
            i = j % 2
            W = WG[i]
            if j >= 1 and j + 1 < 8:
                load_WG(j + 1)
            for t in range(3):
                def grp(c0, rhs_fn, rbufs):
                    k = next_bank()
                    T.group(pe, [(lambda kt=kt, k=k: nc.tensor.matmul(
                        PS[:, k, :], lhsT=W[:, kt, c0:c0 + 128], rhs=rhs_fn(kt), start=(kt == 0), stop=(kt == 7)))
                        for kt in range(8)], reads=[B_WG[i]] + rbufs, writes=[psb[k]])
                    return k
                hsl = lambda kt, t=t: hT[:, kt, t * 512:(t + 1) * 512]
                kga = grp(0, hsl, [B_hT[t]])
                kgr = grp(128, hsl, [B_hT[t]])
                kpa = grp(256, lambda kt, t=t: oagT[:, kt, t * 512:(t + 1) * 512], [B_oag[hh][t] for hh in range(8)])
                kpr = grp(384, lambda kt, t=t: orgT[:, kt, t * 512:(t + 1) * 512], [B_org[hh][t] for hh in range(8)])
                rr[0] ^= 1
                u = rr[0]
                T.op(act, lambda u=u, kga=kga, j=j: nc.scalar.activation(
                    out=ta[u], in_=PS[:, kga, :], func=AF.Tanh, scale=0.5, bias=bgTh[:, j:j + 1]),
                    reads=[psb[kga], B_lam], writes=[B_ta[u]])
                T.op(act, lambda u=u, kgr=kgr, j=j: nc.scalar.activation(
                    out=tr_[u], in_=PS[:, kgr, :], func=AF.Tanh, scale=0.5, bias=bgTh[:, 8 + j:9 + j]),
                    reads=[psb[kgr], B_lam], writes=[B_tr[u]])
                T.op(dve, lambda u=u, kpa=kpa: nc.vector.scalar_tensor_tensor(
                    out=ta[u], in0=ta[u], scalar=1.0, in1=PS[:, kpa, :], op0=ALU.add, op1=ALU.mult),
                    reads=[B_ta[u], psb[kpa]], writes=[B_ta[u]])
                T.op(dve, lambda u=u, kpr=kpr: nc.vector.scalar_tensor_tensor(
                    out=tr_[u], in0=tr_[u], scalar=1.0, in1=PS[:, kpr, :], op0=ALU.add, op1=ALU.mult),
                    reads=[B_tr[u], psb[kpr]], writes=[B_tr[u]])
                T.op(pool, lambda u=u, j=j, t=t: nc.gpsimd.tensor_tensor(
                    out=mT[:, j, t * 512:(t + 1) * 512], in0=ta[u], in1=tr_[u], op=ALU.add),
                    reads=[B_ta[u], B_tr[u]], writes=[B_mT[j][t]])
        T.barrier()
        def f_A(tt):
            u = tt % 4
            ci = 0 if tt < 4 else 1
            t = tt // 4
            kk = []
            for hf in range(2):
                k = next_bank()
                kk.append(k)
                T.group(pe, [(lambda kt=kt, k=k, hf=hf: nc.tensor.matmul(
                    PS[:, k, :], lhsT=mT[:, kt, tt * 128:(tt + 1) * 128], rhs=WO[:, kt, hf * 512:(hf + 1) * 512],
                    start=(kt == 0), stop=(kt == 7))) for kt in range(8)],
                    reads=[B_WO] + [B_mT[jj][t] for jj in range(8)], writes=[psb[k]])
            for hf in range(2):
                T.op(dve, lambda hf=hf, k=kk[hf]: nc.vector.tensor_tensor(
                    out=Z[u][:, hf * 512:(hf + 1) * 512], in0=PS[:, k, :],
                    in1=gate_bc[:, ci, hf * 512:(hf + 1) * 512], op=ALU.mult),
                    reads=[psb[kk[hf]], B_gate], writes=[B_z[u]])
            T.op(dve, lambda: nc.vector.scalar_tensor_tensor(
                out=Z[u], in0=XT[u], scalar=ALPHA, in1=Z[u], op0=ALU.mult, op1=ALU.add),
                reads=[B_x4[u], B_z[u]], writes=[B_z[u]])
            for c2 in range(2):
                T.op(dve, lambda c2=c2: nc.vector.bn_stats(out=st6[u][:, c2, :],
                                                           in_=Z[u][:, c2 * 512:(c2 + 1) * 512]),
                     reads=[B_z[u]], writes=[B_s4[u]])
            T.op(dve, lambda: nc.vector.bn_aggr(out=mv[u], in_=st6[u]), reads=[B_s4[u]], writes=[B_s4[u]])
            T.op(dve, lambda: nc.vector.tensor_scalar(out=rs[u], in0=mv[u][:, 1:2], scalar1=LN_EPS,
                                                      scalar2=None, op0=ALU.add),
                 reads=[B_s4[u]], writes=[B_s4[u]])
            T.op(act, lambda: nc.scalar.activation(out=rs[u], in_=rs[u], func=AF.Sqrt),
                 reads=[B_s4[u]], writes=[B_s4[u]])

        def f_B(tt):
            u = tt % 4
            T.op(dve, lambda: nc.vector.reciprocal(out=rs[u], in_=rs[u]), reads=[B_s4[u]], writes=[B_s4[u]])
            T.op(dve, lambda: nc.vector.scalar_tensor_tensor(
                out=mv[u][:, 1:2], in0=mv[u][:, 0:1], scalar=-1.0, in1=rs[u], op0=ALU.mult, op1=ALU.mult),
                reads=[B_s4[u]], writes=[B_s4[u]])
            T.op(act, lambda: nc.scalar.activation(out=Y[u], in_=Z[u], func=AF.Identity, scale=rs[u],
                                                   bias=mv[u][:, 1:2]),
                 reads=[B_z[u], B_s4[u]], writes=[B_y[u]])
            T.op(pool, lambda: nc.gpsimd.tensor_tensor(out=Y[u], in0=Y[u], in1=lnrow[:, 0, :], op=ALU.mult),
                 reads=[B_y[u], B_ln], writes=[B_y[u]])

        def f_C(tt):
            u = tt % 4
            T.op(dve, lambda: nc.vector.tensor_tensor(out=Y[u], in0=Y[u], in1=lnrow[:, 1, :], op=ALU.add),
                 reads=[B_y[u], B_ln], writes=[B_y[u]])
            if tt + 4 < 12:
                T.dma(sp, [(XT[u], xs[(tt + 4) * 128:(tt + 5) * 128, :])], writes=[B_x4[u]])
            T.dma(sp, [(y_d[tt * 128:(tt + 1) * 128, :], Y[u])], reads=[B_y[u]])

        for t_ in range(4):
            T.dma(sp, [(XT[t_], xs[t_ * 128:(t_ + 1) * 128, :])], writes=[B_x4[t_]])
        for i_ in range(12 + 2):
            if i_ < 12:
                f_A(i_)
            if 0 <= i_ - 1 < 12:
                f_B(i_ - 1)
            if 0 <= i_ - 2 < 12:
                f_C(i_ - 2)
        T.finish()
    es.close()
    return nc


def _consts():
    p = np.arange(128)
    i = np.arange(128)
    c = {}
    c["ident"] = np.eye(128, dtype=np.float32)
    c["cmat"] = (np.eye(128) - 1.0 / 128.0).astype(np.float32)
    iq = np.zeros((128, 128), np.float32)
    iq[0:64, :] = (i + 1)[None, :]
    iq[64:128, :] = (128 - i)[None, :]
    c["iq"] = iq
    c["jp"] = np.stack([127 - p, p], axis=1).astype(np.float32)
    diff = (i[None, :] - p[:, None]).astype(np.float32)
    c["diff"] = diff
    c["mx8"] = (0.125 * (diff >= 0)).astype(np.float32)
    c["my8"] = (0.125 * (diff <= 0)).astype(np.float32)
    sel2 = np.zeros((2, 2, 128), np.float32)
    sel2[0, 0, :] = 1.0
    sel2[1, 1, :] = 1.0
    c["sel2"] = sel2
    return c


def _rope_tables(pos):
    pos = np.asarray(pos)
    r = (pos // 64).astype(np.float32)
    col = (pos % 64).astype(np.float32)
    inv = (np.float32(10000.0) ** (-np.arange(16, dtype=np.float32) / np.float32(16))).astype(np.float32)
    ang = np.concatenate([r[:, None] * inv[None, :], col[:, None] * inv[None, :]], axis=1).astype(np.float32)
    cos = np.cos(ang).astype(np.float32)
    sin = np.sin(ang).astype(np.float32)
    pidx = (np.arange(128) % 64) // 2
    sgn = np.where(np.arange(128) % 2 == 1, 1.0, -1.0).astype(np.float32)
    C = cos[:, pidx].T.copy()
    S = (sin[:, pidx] * sgn[None, :]).T.copy()
    return np.ascontiguousarray(C, np.float32), np.ascontiguousarray(S, np.float32)


_PROGRAM = None


def kernel(x_prompt, x_sample, cache_attn_k, cache_attn_v, state_ret_fwd, state_ret_bwd,
           c, c_ctx, w_mod, b_mod, w_in, lam_params, subln_g, ret_decay, ret_gn_g,
           w_pa, w_pr, w_gate, b_gate, w_out, ln_g, ln_b):
    global _PROGRAM
    f = lambda a: np.ascontiguousarray(np.asarray(a), dtype=np.float32)
    x_prompt, x_sample = f(x_prompt), f(x_sample)
    cache_attn_k, cache_attn_v = f(cache_attn_k), f(cache_attn_v)
    state_ret_fwd, state_ret_bwd = f(state_ret_fwd), f(state_ret_bwd)
    c, c_ctx = f(c), f(c_ctx)
    w_in0 = f(w_in)[0]
    w_gate0, w_pa0, w_pr0 = f(w_gate)[0], f(w_pa)[0], f(w_pr)[0]
    ret_decay0 = f(ret_decay)[0]

    sw = np.arange(128) ^ 1
    wa = np.empty((8, D, WA_COLS), np.float32)
    wr = np.empty((8, D, WR_COLS), np.float32)
    wg = np.empty((8, D, 512), np.float32)
    for h in range(8):
        qa = w_in0[:, h * 128:(h + 1) * 128]
        ka = w_in0[:, 1024 + h * 128:1024 + (h + 1) * 128]
        va = w_in0[:, 2048 + h * 128:2048 + (h + 1) * 128]
        za = w_in0[:, 3072 + h * 128:3072 + (h + 1) * 128]
        qr = w_in0[:, 4096 + h * 64:4096 + (h + 1) * 64]
        kr = w_in0[:, 4608 + h * 64:4608 + (h + 1) * 64]
        vr = w_in0[:, 5120 + h * 128:5120 + (h + 1) * 128]
        zr = w_in0[:, 6144 + h * 128:6144 + (h + 1) * 128]
        wa[h] = np.concatenate([qa, qa[:, sw], ka[:, sw], za, ka, va], axis=1)
        wr[h] = np.concatenate([qr, qr, zr, kr, kr, kr, vr], axis=1)
        wg[h] = np.concatenate([w_gate0[:, h * 128:(h + 1) * 128], w_gate0[:, 1024 + h * 128:1024 + (h + 1) * 128],
                                w_pa0[:, h * 128:(h + 1) * 128], w_pr0[:, h * 128:(h + 1) * 128]], axis=1)
    consts = _consts()
    shared = {
        "w_mod": f(w_mod)[0], "bmod2": np.ascontiguousarray(np.broadcast_to(f(b_mod)[0][None, :], (2, 3 * D))),
        "wa": wa, "wr": wr, "wg": wg, "w_out": f(w_out)[0],
        "bgT": np.ascontiguousarray(f(b_gate)[0].reshape(16, 128).T),
        "lnrow": np.ascontiguousarray(np.broadcast_to(np.stack([f(ln_g)[0], f(ln_b)[0]])[None], (128, 2, D))),
        "lamp": np.ascontiguousarray(np.broadcast_to(f(lam_params)[0].reshape(1, 256), (128, 256))),
        "sublnT": np.ascontiguousarray(f(subln_g)[0].reshape(128, 1)),
        "gngT": np.ascontiguousarray(f(ret_gn_g)[0].reshape(8, 128).T),
    }
    shared.update(consts)

    in_maps = []
    for core in range(NCORES):
        b, half = core // 2, core % 2
        rev = half == 1
        o = (lambda a: a[::-1]) if rev else (lambda a: a)
        p0, p1 = o(x_prompt[2 * core]), o(x_prompt[2 * core + 1])
        mine = o(x_sample[b, half * 1024:(half + 1) * 1024])
        other = o(x_sample[b, (1 - half) * 1024:(2 - half) * 1024])
        pos = np.concatenate([o(np.arange(half * 1024, (half + 1) * 1024)),
                              o(np.arange((1 - half) * 1024, (2 - half) * 1024))])
        rc, rs_ = _rope_tables(pos)
        sX = state_ret_fwd[b, 0] if not rev else state_ret_bwd[b, 0]
        sY = state_ret_bwd[b, 0] if not rev else state_ret_fwd[b, 0]
        dX = ret_decay0[0] if not rev else ret_decay0[1]
        dY = ret_decay0[1] if not rev else ret_decay0[0]
        m = dict(shared)
        m["xs"] = np.ascontiguousarray(np.concatenate([p0, p1, mine, other], axis=0))
        m["ck"] = np.ascontiguousarray(cache_attn_k[b, 0].reshape(256, D))
        m["cv"] = np.ascontiguousarray(cache_attn_v[b, 0].reshape(256, D))
        m["s0"] = np.ascontiguousarray(np.concatenate([sX.transpose(1, 0, 2), sY.transpose(1, 0, 2)], axis=0))
        m["condT"] = np.ascontiguousarray(np.stack([c_ctx.reshape(8, 128).T, c[b].reshape(8, 128).T], axis=2))
        m["rdsel"] = np.ascontiguousarray(np.concatenate([np.broadcast_to(dX[None], (64, 8)),
                                                           np.broadcast_to(dY[None], (64, 8))], axis=0))
        m["rdrow"] = np.ascontiguousarray(np.broadcast_to(np.concatenate([dX, dY])[None], (128, 16)))
        m["ropeC"] = rc
        m["ropeS"] = rs_
        in_maps.append(m)

    if _PROGRAM is None:
        import os as _os
        _PROGRAM = build_program(int(_os.environ.get('MK_STOP', '99')))
    res = run_bass_kernel_spmd(_PROGRAM, in_maps, core_ids=list(range(NCORES)))

    y_prompt = np.empty((16, 256, D), np.float32)
    y_sample = np.empty((4, 2048, D), np.float32)
    new_k = np.empty((16, 1, 256, 8, 128), np.float32)
    new_v = np.empty((16, 1, 256, 8, 128), np.float32)
    new_f = np.empty((16, 1, 8, 64, 128), np.float32)
    new_b = np.empty((16, 1, 8, 64, 128), np.float32)
    for core in range(NCORES):
        r = res.results[core]
        b, half = core // 2, core % 2
        rev = half == 1
        o = (lambda a: a[::-1]) if rev else (lambda a: a)
        y = r["y"]
        for pb in range(2):
            y_prompt[2 * core + pb] = o(y[pb * 256:(pb + 1) * 256])
            new_k[2 * core + pb, 0] = o(r["nk"][pb * 256:(pb + 1) * 256]).reshape(256, 8, 128)
            new_v[2 * core + pb, 0] = o(r["nv"][pb * 256:(pb + 1) * 256]).reshape(256, 8, 128)
            X = r["nst"][pb][:, 0:64, :]
            Yv = r["nst"][pb][:, 64:128, :]
            new_f[2 * core + pb, 0] = Yv if rev else X
            new_b[2 * core + pb, 0] = X if rev else Yv
        y_sample[b, half * 1024:(half + 1) * 1024] = o(y[512:1536])
    return (y_prompt, y_sample, new_k, new_v, new_f, new_b)
```

```python
import math
from contextlib import ExitStack

import numpy as np
import concourse.bass as bass
import concourse.mybir as mybir
from concourse.bass_utils import run_bass_kernel_spmd

F32 = mybir.dt.float32
BF16 = mybir.dt.bfloat16
AF = mybir.ActivationFunctionType
ALU = mybir.AluOpType
AX = mybir.AxisListType

NCORES = 8
D = 1024
NP_TOK = 512
NM_TOK = 1024
NO_TOK = 1024
NTOK = NP_TOK + NM_TOK + NO_TOK
NQ_TOK = NP_TOK + NM_TOK
WA_COLS = 768
WR_COLS = 576
LAM_INIT = 0.8 - 0.6 * math.exp(-0.3 * 0)
ALPHA = (2.0 * 1) ** 0.25
MOD_EPS = 1e-6
LN_EPS = 1e-5


class _Sem:
    _n = 0

    def __init__(self, nc, name):
        _Sem._n += 1
        self.uid = _Sem._n
        self.h = nc.alloc_semaphore(f"{name}_{self.uid}")


class _Eng:
    def __init__(self, nc, name, handle):
        self.name = name
        self.h = handle
        self.sem = _Sem(nc, "e" + name)
        self.n = 0
        self.seen = {}


class _Buf:
    __slots__ = ("name", "w", "rs", "dsem", "dcnt", "excl")

    def __init__(self, name):
        self.name = name
        self.excl = False
        self.w = None
        self.rs = {}
        self.dsem = None
        self.dcnt = 0


class Tracker:
    def __init__(self, nc):
        self.nc = nc
        self.pe = _Eng(nc, "pe", nc.tensor)
        self.act = _Eng(nc, "act", nc.scalar)
        self.dve = _Eng(nc, "dve", nc.vector)
        self.pool = _Eng(nc, "pool", nc.gpsimd)
        self.sp = _Eng(nc, "sp", nc.sync)
        self.engs = [self.pe, self.act, self.dve, self.pool, self.sp]
        self.bufs = {}
        self.dsems = []

    def b(self, *key):
        bb = self.bufs.get(key)
        if bb is None:
            bb = _Buf(key)
            self.bufs[key] = bb
        return bb

    def _wait(self, eng, tok):
        if tok is None:
            return
        sem, val = tok
        if eng is self.pe and sem is self.pe.sem:
            return
        if eng.seen.get(sem.uid, 0) >= val:
            return
        eng.h.wait_ge(sem.h, val)
        eng.seen[sem.uid] = val

    def _deps(self, eng, reads, writes):
        for r in reads:
            self._wait(eng, r.w)
            if r.excl:
                for t in r.rs.values():
                    if t[0] is not eng.sem:
                        self._wait(eng, t)
        for w in writes:
            self._wait(eng, w.w)
            for t in w.rs.values():
                if t[0] is eng.sem and eng is not self.pool:
                    continue
                self._wait(eng, t)

    def _mark(self, tok, reads, writes):
        for w in writes:
            w.w = tok
            w.rs = {}
        for r in reads:
            r.rs[tok[0].uid] = tok

    def group(self, eng, fns, reads=(), writes=()):
        self._deps(eng, reads, writes)
        ins = None
        for f in fns:
            ins = f()
        eng.n += 1
        ins.then_inc(eng.sem.h, 1)
        self._mark((eng.sem, eng.n), reads, writes)

    def op(self, eng, fn, reads=(), writes=()):
        self.group(eng, [fn], reads, writes)

    def dma(self, eng, pairs, reads=(), writes=()):
        owner = (list(writes) + list(reads))[0]
        kind = "sw" if eng is self.pool else "hw"
        if owner.dsem is None:
            owner.dsem = {}
        if kind not in owner.dsem:
            owner.dsem[kind] = [_Sem(self.nc, "d" + kind), 0]
            self.dsems.append(owner.dsem[kind])
        ent = owner.dsem[kind]
        self._deps(eng, reads, writes)
        for (o, i) in pairs:
            eng.h.dma_start(out=o, in_=i).then_inc(ent[0].h, 16)
            ent[1] += 16
        self._mark((ent[0], ent[1]), reads, writes)

    def barrier(self, skip=()):
        skip_ids = set()
        for b_ in skip:
            for ent in (b_.dsem or {}).values():
                skip_ids.add(ent[0].uid)
        for e in self.engs:
            for f in self.engs:
                if f is not e and f.n > 0:
                    self._wait(e, (f.sem, f.n))
            for d in self.dsems:
                if d[0].uid not in skip_ids:
                    self._wait(e, (d[0], d[1]))

    def finish(self):
        for d in self.dsems:
            self._wait(self.sp, (d[0], d[1]))
        for f in self.engs:
            if f is not self.sp and f.n > 0:
                self._wait(self.sp, (f.sem, f.n))


class _StopBuild(Exception):
    pass


def build_program(STOP=99):
    import os as _os
    tag_stop = _os.environ.get("MK_TAG", "")
    holder = {}

    def chk(tag):
        if tag_stop and tag == tag_stop:
            raise _StopBuild()

    try:
        return _build_program(STOP, chk, holder)
    except _StopBuild:
        holder["T"].finish()
        return holder["nc"]


def _build_program(STOP, chk, holder):
    nc = bass.Bass("TRN2", target_bir_lowering=False)
    T = Tracker(nc)
    holder["T"] = T
    holder["nc"] = nc
    pe, act, dve, pool, sp = T.pe, T.act, T.dve, T.pool, T.sp

    def din(name, shape):
        return nc.dram_tensor(name, list(shape), F32, kind="ExternalInput").ap()

    def dout(name, shape):
        return nc.dram_tensor(name, list(shape), F32, kind="ExternalOutput").ap()

    xs = din("xs", [NTOK, D])
    ck = din("ck", [256, D])
    cv = din("cv", [256, D])
    s0_d = din("s0", [128, 8, 128])
    condT_d = din("condT", [128, 8, 2])
    wmod_d = din("w_mod", [D, 3 * D])
    bmod_d = din("bmod2", [2, 3 * D])
    wa_d = din("wa", [8, D, WA_COLS])
    wr_d = din("wr", [8, D, WR_COLS])
    wg_d = din("wg", [8, D, 512])
    wout_d = din("w_out", [D, D])
    bgT_d = din("bgT", [128, 16])
    lnrow_d = din("lnrow", [128, 2, D])
    lamp_d = din("lamp", [128, 256])
    sublnT_d = din("sublnT", [128, 1])
    gngT_d = din("gngT", [128, 8])
    rdsel_d = din("rdsel", [128, 8])
    rdrow_d = din("rdrow", [128, 16])
    ident_d = din("ident", [128, 128])
    cmat_d = din("cmat", [128, 128])
    iq_d = din("iq", [128, 128])
    jp_d = din("jp", [128, 2])
    diff_d = din("diff", [128, 128])
    mx8_d = din("mx8", [128, 128])
    my8_d = din("my8", [128, 128])
    ropeC_d = din("ropeC", [128, 2048])
    ropeS_d = din("ropeS", [128, 2048])
    sel2_d = din("sel2", [2, 2, 128])

    y_d = dout("y", [NQ_TOK, D])
    nk_d = dout("nk", [NP_TOK, D])
    nv_d = dout("nv", [NP_TOK, D])
    nst_d = dout("nst", [2, 8, 128, 128])

    es = ExitStack()
    KB = 1024
    BASE = 16640
    TOP = 228992
    BIG = BASE + 30 * KB
    _esz = {F32: 4, BF16: 2}

    class Arena:
        def __init__(self, start, end):
            self.start, self.end, self.cur = start, end, start

        def alloc(self, name, shape, dt):
            n = 1
            for d_ in shape[1:]:
                n *= d_
            nbytes = (n * _esz[dt] + 63) // 64 * 64
            off = self.cur
            assert off + nbytes <= self.end, (name, off, nbytes, self.end)
            self.cur = off + nbytes
            Arena.uid += 1
            return nc.alloc_sbuf_tensor_at(f"s_{name}_{Arena.uid}", list(shape), dt, offset=off).ap()

    Arena.uid = 0
    A_const = Arena(BASE, BIG)
    A_pers = Arena(BIG, BIG + 112 * KB)

    def sb(name, shape, dt=F32, arena=None):
        return (arena or A_const).alloc(name, shape, dt)

    PS = es.enter_context(nc.psum_tensor("PS", [128, 8, 512], F32)).ap()
    psb = [T.b("ps", k) for k in range(8)]
    for _b in psb:
        _b.excl = True
    bank_rr = [0]

    def next_bank():
        k = bank_rr[0]
        bank_rr[0] = (k + 1) % 8
        return k

    hT = sb("hT", [128, 8, NTOK], BF16, A_pers)
    oagT = sb("oagT", [128, 8, NQ_TOK], BF16, A_pers)
    orgT = sb("orgT", [128, 8, NQ_TOK], BF16, A_pers)
    ident = sb("ident", [128, 128])
    cmat_f = sb("cmat_f", [128, 128])
    cmat = sb("cmat", [128, 128], BF16)
    jmat = sb("jmat", [128, 128], BF16)
    ones_b = sb("ones_b", [128, 128], BF16)
    neghalf = sb("neghalf", [128, 512])
    one_col = sb("one_col", [128, 1])
    QD = sb("QD", [128, 8, 128])
    kdcol = sb("kdcol", [128, 16])
    DC = sb("DC", [128, 8, 128])
    CDv = sb("CDv", [128, 8])
    s0 = sb("s0", [128, 8, 128])
    shiftT = sb("shiftT", [128, 8, 2])
    scale1T = sb("scale1T", [128, 8, 2])
    gate_bc = sb("gate_bc", [128, 2, D])
    bgT = sb("bgT", [128, 16])
    bgTh = sb("bgTh", [128, 16])
    neglam = sb("neglam", [128, 1])
    gsc = sb("gsc", [128, 1])
    gngT = sb("gngT", [128, 8])
    gngTh = sb("gngTh", [128, 8])
    sublnT = sb("sublnT", [128, 1])

    B_const = T.b("const")
    B_hT = [T.b("hT", st) for st in range(5)]

    if True:
        ph = Arena(TOP - 32 * KB, TOP)
        condT = sb("condT", [128, 8, 2], F32, ph)
        silT = sb("silT", [128, 8, 2], BF16, ph)
        ctmp = sb("ctmp", [128, 8, 2], F32, ph)
        lamp = sb("lamp", [128, 256], F32, ph)
        ltmp = sb("ltmp", [128, 128], F32, ph)
        lsum = sb("lsum", [128, 2], F32, ph)
        rdsel = sb("rdsel", [128, 8], F32)
        rdrow = sb("rdrow", [128, 16], F32)
        lgsel = sb("lgsel", [128, 8], F32)
        lgrow = sb("lgrow", [128, 16], F32)
        nlgrow = sb("nlgrow", [128, 16], F32)
        lg128 = sb("lg128", [128, 8], F32, ph)
        iq = sb("iq", [128, 128], F32)
        jp = sb("jp", [128, 2], F32)
        diff = sb("diff", [128, 128], F32)
        mx8 = sb("mx8", [128, 1, 128], F32)
        my8 = sb("my8", [128, 1, 128], F32)
        sel2 = sb("sel2", [2, 2, 128], F32, ph)
        modrow = sb("modrow", [2, 3 * D], F32, ph)
        wm = [sb(f"wm{i}", [128, 8, 512], BF16, ph) for i in range(2)]
        B_wm = [T.b("wm", i) for i in range(2)]

        T.dma(sp, [(ident, ident_d), (cmat_f, cmat_d), (s0, s0_d), (condT, condT_d), (bgT, bgT_d),
                   (lamp, lamp_d), (sublnT, sublnT_d), (gngT, gngT_d),
                   (rdsel, rdsel_d), (rdrow, rdrow_d), (iq, iq_d), (jp, jp_d), (diff, diff_d),
                   (mx8, mx8_d.rearrange("p (o i) -> p o i", o=1)), (my8, my8_d.rearrange("p (o i) -> p o i", o=1)), (sel2, sel2_d), (modrow, bmod_d)],
              writes=[B_const])
        B_c2 = T.b("const2")
        T.op(pool, lambda: nc.gpsimd.memset(jmat, 1.0 / 128.0), writes=[B_c2])
        T.op(pool, lambda: nc.gpsimd.memset(ones_b, 1.0), writes=[B_c2])
        T.op(pool, lambda: nc.gpsimd.memset(neghalf, -0.5), writes=[B_c2])
        T.op(pool, lambda: nc.gpsimd.memset(one_col, 1.0), writes=[B_c2])
        T.op(dve, lambda: nc.vector.tensor_copy(out=cmat, in_=cmat_f), reads=[B_const], writes=[B_c2])

        chk('a')
        B_sil = T.b("sil")
        T.op(act, lambda: nc.scalar.activation(out=ctmp, in_=condT, func=AF.Tanh, scale=0.5),
             reads=[B_const], writes=[B_sil])
        T.op(dve, lambda: nc.vector.scalar_tensor_tensor(out=ctmp, in0=ctmp, scalar=1.0, in1=condT,
                                                         op0=ALU.add, op1=ALU.mult),
             reads=[B_sil, B_const], writes=[B_sil])
        T.op(dve, lambda: nc.vector.tensor_scalar(out=silT, in0=ctmp, scalar1=0.5, scalar2=None, op0=ALU.mult),
             reads=[B_sil], writes=[B_sil])

        chk('b')
        B_mod = T.b("modrow")
        wmod_v = wmod_d.rearrange("(kt p) c -> p kt c", p=128)
        for blk in range(6):
            slot = blk % 2
            T.dma(pool, [(wm[slot], wmod_v[:, :, blk * 512:(blk + 1) * 512])], writes=[B_wm[slot]])
            k = next_bank()
            T.group(pe, [(lambda kt=kt, k=k, slot=slot: nc.tensor.matmul(
                PS[0:2, k, :], lhsT=silT[:, kt, :], rhs=wm[slot][:, kt, :], start=(kt == 0), stop=(kt == 7)))
                for kt in range(8)], reads=[B_sil, B_wm[slot]], writes=[psb[k]])
            T.op(dve, lambda k=k, blk=blk: nc.vector.tensor_tensor(
                out=modrow[:, blk * 512:(blk + 1) * 512], in0=PS[0:2, k, :],
                in1=modrow[:, blk * 512:(blk + 1) * 512], op=ALU.add),
                reads=[psb[k], B_const], writes=[B_mod])
        chk('c')
        k = next_bank()
        T.group(pe, [(lambda j=j, k=k: nc.tensor.transpose(
            PS[:, k, 2 * j:2 * j + 2], modrow[0:2, j * 128:(j + 1) * 128], ident[0:2, 0:2]))
            for j in range(16)], reads=[B_mod, B_const], writes=[psb[k]])
        B_modT = T.b("modT")
        T.op(dve, lambda k=k: nc.vector.tensor_copy(
            out=shiftT, in_=PS[:, k, 0:16].rearrange("p (j c) -> p j c", c=2)),
            reads=[psb[k]], writes=[B_modT])
        T.op(dve, lambda k=k: nc.vector.tensor_scalar(
            out=scale1T, in0=PS[:, k, 16:32].rearrange("p (j c) -> p j c", c=2), scalar1=1.0, scalar2=None,
            op0=ALU.add), reads=[psb[k]], writes=[B_modT])
        chk('d')
        B_gate = T.b("gate_bc")
        for ci in range(2):
            for hf in range(2):
                k = next_bank()
                T.op(pe, lambda k=k, ci=ci, hf=hf: nc.tensor.matmul(
                    PS[:, k, :], lhsT=sel2[:, ci, :], rhs=modrow[0:2, 2048 + hf * 512:2048 + (hf + 1) * 512],
                    start=True, stop=True), reads=[B_mod, B_const], writes=[psb[k]])
                T.op(act, lambda k=k, ci=ci, hf=hf: nc.scalar.mul(
                    out=gate_bc[:, ci, hf * 512:(hf + 1) * 512], in_=PS[:, k, :], mul=0.5),
                    reads=[psb[k]], writes=[B_gate])

        chk('e')
        B_lam = T.b("lam")
        T.op(dve, lambda: nc.vector.tensor_tensor(
            out=ltmp.rearrange("p (a b) -> p a b", a=2),
            in0=lamp.rearrange("p (a r b) -> p a r b", a=2, r=2)[:, :, 0, :],
            in1=lamp.rearrange("p (a r b) -> p a r b", a=2, r=2)[:, :, 1, :], op=ALU.mult),
            reads=[B_const], writes=[B_lam])
        T.op(dve, lambda: nc.vector.reduce_sum(out=lsum, in_=ltmp.rearrange("p (a b) -> p a b", a=2), axis=AX.X),
             reads=[B_lam], writes=[B_lam])
        T.op(act, lambda: nc.scalar.activation(out=lsum, in_=lsum, func=AF.Exp), reads=[B_lam], writes=[B_lam])
        T.op(dve, lambda: nc.vector.tensor_tensor(out=neglam, in0=lsum[:, 1:2], in1=lsum[:, 0:1], op=ALU.subtract),
             reads=[B_lam], writes=[B_lam])
        T.op(dve, lambda: nc.vector.tensor_scalar(out=neglam, in0=neglam, scalar1=-LAM_INIT, scalar2=None,
                                                  op0=ALU.add), reads=[B_lam], writes=[B_lam])
        T.op(dve, lambda: nc.vector.tensor_scalar(out=gsc, in0=sublnT, scalar1=(1.0 - LAM_INIT),
                                                  scalar2=None, op0=ALU.mult), reads=[B_const], writes=[B_lam])
        T.op(dve, lambda: nc.vector.tensor_scalar(out=gngTh, in0=gngT, scalar1=0.5, scalar2=None, op0=ALU.mult),
             reads=[B_const], writes=[B_lam])
        T.op(dve, lambda: nc.vector.tensor_scalar(out=bgTh, in0=bgT, scalar1=0.5, scalar2=None, op0=ALU.mult),
             reads=[B_const], writes=[B_lam])

        B_dec = T.b("dec")
        ph0 = ph

        def emit_decay_tables():
            chk('f')
            B_dec = T.b("dec")

            def log1p_neg(dst, src, n):
                T.op(act, lambda: nc.scalar.activation(out=src, in_=src, func=AF.Exp, scale=math.log(2.0)),
                     reads=[B_const, B_dec], writes=[B_dec])
                T.op(dve, lambda: nc.vector.tensor_scalar(out=dst, in0=src, scalar1=1.0 / 8.0, scalar2=None,
                                                          op0=ALU.mult), reads=[B_dec], writes=[B_dec])
                for c in (1.0 / 7.0, 1.0 / 6.0, 1.0 / 5.0, 0.25, 1.0 / 3.0, 0.5, 1.0):
                    T.op(dve, lambda c=c: nc.vector.scalar_tensor_tensor(
                        out=dst, in0=dst, scalar=c, in1=src, op0=ALU.add, op1=ALU.mult),
                        reads=[B_dec], writes=[B_dec])
                T.op(dve, lambda: nc.vector.tensor_scalar(out=dst, in0=dst, scalar1=-1.0, scalar2=None, op0=ALU.mult),
                     reads=[B_dec], writes=[B_dec])

            chk('g')
            log1p_neg(lgsel, rdsel, 8)
            log1p_neg(lgrow, rdrow, 16)
            T.op(dve, lambda: nc.vector.tensor_scalar(out=nlgrow, in0=lgrow, scalar1=-1.0, scalar2=None, op0=ALU.mult),
                 reads=[B_dec], writes=[B_dec])
            T.op(act, lambda: nc.scalar.activation(out=CDv, in_=lgsel, func=AF.Exp, scale=128.0),
                 reads=[B_dec], writes=[B_dec])
            chk('h')
            ln8 = math.log(0.125)
            lnb = sb("lnb", [128, 1], F32)
            T.op(pool, lambda: nc.gpsimd.memset(lnb, ln8), writes=[B_dec])
            T.op(dve, lambda: nc.vector.tensor_scalar(out=kdcol[:, 0:8], in0=lgrow[:, 0:8], scalar1=jp[:, 0:1], scalar2=None,
                                                      op0=ALU.mult), reads=[B_dec, B_const], writes=[B_dec])
            T.op(dve, lambda: nc.vector.tensor_scalar(out=kdcol[:, 8:16], in0=lgrow[:, 8:16], scalar1=jp[:, 1:2],
                                                      scalar2=None, op0=ALU.mult), reads=[B_dec, B_const], writes=[B_dec])
            T.op(act, lambda: nc.scalar.activation(out=kdcol, in_=kdcol, func=AF.Exp, bias=lnb),
                 reads=[B_dec], writes=[B_dec])
            chk('i')
            for h in range(8):
                T.op(act, lambda h=h: nc.scalar.activation(out=DC[:, h, :], in_=diff, func=AF.Exp,
                                                           scale=lgrow[:, h:h + 1]),
                     reads=[B_dec, B_const], writes=[B_dec])
                T.op(act, lambda h=h: nc.scalar.activation(out=QD[:, h, :], in_=diff, func=AF.Exp,
                                                           scale=nlgrow[:, 8 + h:9 + h]),
                     reads=[B_dec, B_const], writes=[B_dec])
            T.op(dve, lambda: nc.vector.tensor_tensor(out=DC, in0=DC, in1=mx8.to_broadcast([128, 8, 128]),
                                                      op=ALU.mult), reads=[B_dec, B_const], writes=[B_dec])
            T.op(dve, lambda: nc.vector.tensor_tensor(out=QD, in0=QD, in1=my8.to_broadcast([128, 8, 128]),
                                                      op=ALU.mult), reads=[B_dec, B_const], writes=[B_dec])
            T.op(dve, lambda: nc.vector.tensor_tensor(out=DC, in0=DC, in1=QD, op=ALU.add),
                 reads=[B_dec], writes=[B_dec])
            for h in range(8):
                T.op(act, lambda h=h: nc.scalar.activation(out=QD[:, h, :], in_=iq, func=AF.Exp,
                                                           scale=lgsel[:, h:h + 1]),
                     reads=[B_dec, B_const], writes=[B_dec])

        if STOP == 0:
            T.finish()
            return nc

    if True:
        ph2 = Arena(BIG + 64 * KB, TOP - 65 * KB)
        ropeC = sb("ropeC", [128, 2048], F32, ph2)
        ropeS = sb("ropeS", [128, 2048], F32, ph2)
        ckT = sb("ckT", [128, 8, 256], BF16, ph2)
        cvb = sb("cvb", [128, 2, D], BF16, ph2)
        WA0 = sb("WA0", [128, 8, WA_COLS], BF16, ph2)
        _wa1_off = ph2.cur
        cst = sb("cst", [128, D], F32, ph2)
        ph2.cur = _wa1_off
        WA1 = sb("WA1", [128, 8, WA_COLS], BF16, ph2)
        WA = [WA0, WA1]
        B_rope = T.b("rope")
        B_WA = [T.b("WA", i) for i in range(2)]
        B_ctx = T.b("ctx")

        def load_WA(h):
            v = wa_d[h].rearrange("(kt p) c -> p kt c", p=128)
            T.dma(pool, [(WA[h % 2], v)], writes=[B_WA[h % 2]])

        load_WA(0)

    if True:
        ph = Arena(TOP - 65 * KB, TOP - 32 * KB)
        NB1 = 4
        XT = [sb(f"XT{i}", [128, D], F32, ph) for i in range(NB1)]
        XN = [sb(f"XN{i}", [128, D], F32, ph) for i in range(NB1)]
        st6 = [sb(f"st6_{i}", [128, 2, 6], F32, ph) for i in range(NB1)]
        mv = [sb(f"mv{i}", [128, 2], F32, ph) for i in range(NB1)]
        rs = [sb(f"rs{i}", [128, 1], F32, ph) for i in range(NB1)]
        B_xt = [T.b("XT", i) for i in range(NB1)]
        B_xn = [T.b("XN", i) for i in range(NB1)]
        B_stt = [T.b("stt", i) for i in range(NB1)]

        def p1_A(g):
            s = g % NB1
            T.dma(sp, [(XT[s], xs[g * 128:(g + 1) * 128, :])], writes=[B_xt[s]])
            for c2 in range(2):
                T.op(dve, lambda c2=c2: nc.vector.bn_stats(out=st6[s][:, c2, :],
                                                           in_=XT[s][:, c2 * 512:(c2 + 1) * 512]),
                     reads=[B_xt[s]], writes=[B_stt[s]])
            T.op(dve, lambda: nc.vector.bn_aggr(out=mv[s], in_=st6[s]), reads=[B_stt[s]], writes=[B_stt[s]])
            T.op(dve, lambda: nc.vector.tensor_scalar(out=rs[s], in0=mv[s][:, 1:2], scalar1=MOD_EPS,
                                                      scalar2=None, op0=ALU.add),
                 reads=[B_stt[s]], writes=[B_stt[s]])
            T.op(act, lambda: nc.scalar.activation(out=rs[s], in_=rs[s], func=AF.Sqrt),
                 reads=[B_stt[s]], writes=[B_stt[s]])

        def p1_B(g):
            s = g % NB1
            T.op(dve, lambda: nc.vector.reciprocal(out=rs[s], in_=rs[s]), reads=[B_stt[s]], writes=[B_stt[s]])
            T.op(dve, lambda: nc.vector.scalar_tensor_tensor(
                out=mv[s][:, 1:2], in0=mv[s][:, 0:1], scalar=-1.0, in1=rs[s], op0=ALU.mult, op1=ALU.mult),
                reads=[B_stt[s]], writes=[B_stt[s]])
            T.op(act, lambda: nc.scalar.activation(out=XN[s], in_=XT[s], func=AF.Identity,
                                                   scale=rs[s], bias=mv[s][:, 1:2]),
                 reads=[B_xt[s], B_stt[s]], writes=[B_xn[s]])

        def p1_C(g):
            s = g % NB1
            st, tt = g // 4, g % 4
            ci = 0 if st == 0 else 1
            for kt in range(8):
                T.op(pe, lambda kt=kt: nc.tensor.transpose(
                    PS[:, kt, tt * 128:(tt + 1) * 128], XN[s][:, kt * 128:(kt + 1) * 128], ident),
                    reads=[B_xn[s], B_const], writes=[psb[kt]])
            if tt == 3:
                for kt in range(8):
                    if kt % 8 == 0:
                        T.op(dve, lambda kt=kt: nc.vector.tensor_scalar(
                            out=hT[:, kt, st * 512:(st + 1) * 512], in0=PS[:, kt, :],
                            scalar1=scale1T[:, kt, ci:ci + 1], scalar2=shiftT[:, kt, ci:ci + 1],
                            op0=ALU.mult, op1=ALU.add), reads=[psb[kt], B_modT], writes=[B_hT[st]])
                    else:
                        T.op(act, lambda kt=kt: nc.scalar.activation(
                            out=hT[:, kt, st * 512:(st + 1) * 512], in_=PS[:, kt, :], func=AF.Identity,
                            scale=scale1T[:, kt, ci:ci + 1], bias=shiftT[:, kt, ci:ci + 1]),
                            reads=[psb[kt], B_modT], writes=[B_hT[st]])

        for i_ in range(20 + 2):
            if i_ < 20:
                p1_A(i_)
            if 0 <= i_ - 1 < 20:
                p1_B(i_ - 1)
            if 0 <= i_ - 2 < 20:
                p1_C(i_ - 2)
        T.dma(sp, [(ropeC, ropeC_d), (ropeS, ropeS_d)], writes=[B_rope])
        for t in range(2):
            T.dma(sp, [(cst, ck[t * 128:(t + 1) * 128, :])], writes=[B_WA[1]])
            for hq in range(2):
                k = next_bank()
                T.group(pe, [(lambda h=h, k=k: nc.tensor.transpose(
                    PS[:, k, (h % 4) * 128:(h % 4 + 1) * 128], cst[:, h * 128:(h + 1) * 128], ident))
                    for h in range(hq * 4, hq * 4 + 4)], reads=[B_WA[1], B_const], writes=[psb[k]])
                T.op(act, lambda k=k, hq=hq, t=t: nc.scalar.activation(
                    out=ckT[:, hq * 4:(hq + 1) * 4, t * 128:(t + 1) * 128],
                    in_=PS[:, k, :].rearrange("p (h j) -> p h j", h=4), func=AF.Copy),
                    reads=[psb[k]], writes=[B_ctx])
        for t in range(2):
            T.dma(sp, [(cst, cv[t * 128:(t + 1) * 128, :])], writes=[B_WA[1]])
            T.op(dve, lambda t=t: nc.vector.tensor_copy(out=cvb[:, t, :], in_=cst), reads=[B_WA[1]], writes=[B_ctx])
        load_WA(1)

        T.barrier(skip=[B_rope, B_WA[1]])
        if STOP == 1:
            T.finish()
            return nc

    if True:
        ph = ph2
        ph.end = TOP
        qT = [sb(f"qT{i}", [128, NQ_TOK], BF16, ph) for i in range(2)]
        kT = [sb(f"kT{i}", [128, NTOK], BF16, ph) for i in range(2)]
        vS = [sb(f"vS{i}", [128, 20, 128], BF16, ph) for i in range(2)]
        tza = [sb(f"tza{i}", [128, NQ_TOK], BF16, ph) for i in range(2)]
        E = [sb(f"E{i}", [128, 2, 256], BF16, ph) for i in range(3)]
        r1 = [sb(f"r1_{i}", [128, 512], F32, ph) for i in range(2)]
        r2 = [sb(f"r2_{i}", [128, 512], F32, ph) for i in range(2)]
        tmpz = [sb(f"tmpz{i}", [128, 512], F32, ph) for i in range(2)]
        stage = [sb(f"stage{i}", [128, 256], F32, ph) for i in range(2)]
        dn = sb("dn", [128, 2, 256], F32, ph)
        o2 = sb("o2", [128, 256], F32, ph)
        osq_ = [sb(f"osq{i}", [128, 256], BF16, ph) for i in range(2)]
        dd_ = [sb(f"dd{i}", [128, 256], F32, ph) for i in range(2)]
        o_all_ = [sb(f"o_all{i}", [128, 512], F32, ph) for i in range(2)]
        v_all_ = [sb(f"v_all{i}", [128, 512], F32, ph) for i in range(2)]

        B_qT = [[T.b("qT", i, g) for g in range(3)] for i in range(2)]
        B_kT = [[T.b("kT", i, g) for g in range(5)] for i in range(2)]
        B_vS = [[T.b("vS", i, g) for g in range(5)] for i in range(2)]
        B_tza = [[T.b("tza", i, g) for g in range(3)] for i in range(2)]
        B_E = [T.b("E", i) for i in range(3)]
        B_r1 = [T.b("r1", i) for i in range(2)]
        B_r2 = [T.b("r2", i) for i in range(2)]
        B_tz = [T.b("tmpz", i) for i in range(2)]
        B_stage = [T.b("stage", i) for i in range(2)]
        B_dn = T.b("dn")
        B_o2 = T.b("o2")
        B_osq_ = [T.b("osq", i) for i in range(2)]
        B_dd_ = [T.b("dd", i) for i in range(2)]
        B_oall_ = [[T.b("o_all", bp, i) for i in range(2)] for bp in range(2)]
        B_vall_ = [[T.b("v_all", bp, i) for i in range(2)] for bp in range(2)]
        gcount = [0]
        bcount = [0]
        B_oag = [[T.b("oag", h, g) for g in range(3)] for h in range(8)]

        chk('p2a')
        rr = [0]

        def rot2():
            rr[0] ^= 1
            return rr[0]

        pbank = [0]
        PB = [5, 6, 7]

        def pool_bank():
            pbank[0] = (pbank[0] + 1) % 3
            return PB[pbank[0]]

        Q2 = [sb(f"Q2_{i}", [128, 512], BF16, ph) for i in range(2)]
        B_Q2 = [T.b("Q2", i) for i in range(2)]
        for q_ in range(2):
            T.op(pool, lambda q_=q_: nc.gpsimd.memset(Q2[q_], 0.0), writes=[B_Q2[q_]])
        q2c = [0]
        q2carry = {}

        def build_Q2(i, qc0, g):
            q2c[0] ^= 1
            u = q2c[0]
            T.op(pool, lambda: nc.gpsimd.tensor_copy(out=Q2[u][0:64, 0:256], in_=qT[i][0:64, qc0:qc0 + 256]),
                 reads=[B_qT[i][g]], writes=[B_Q2[u]])
            T.op(pool, lambda: nc.gpsimd.tensor_copy(out=Q2[u][64:128, 256:512], in_=qT[i][64:128, qc0:qc0 + 256]),
                 reads=[B_qT[i][g]], writes=[B_Q2[u]])
            return u

        def mk_chunks(h):
            i = h % 2
            W = WA[i]
            ch = []

            def fgroup(c0, st, ncols=128):
                k = pool_bank()
                T.group(pe, [(lambda kt=kt, k=k: nc.tensor.matmul(
                    PS[0:ncols, k, :], lhsT=W[:, kt, c0:c0 + ncols], rhs=hT[:, kt, st * 512:(st + 1) * 512],
                    start=(kt == 0), stop=(kt == 7))) for kt in range(8)],
                    reads=[B_WA[i], B_hT[st]], writes=[psb[k]])
                return k

            def plain(c0, dst, dbuf):
                k = fgroup(c0, 0)
                T.op(dve, lambda k=k: nc.vector.tensor_copy(out=dst, in_=PS[:, k, :]),
                     reads=[psb[k]], writes=[dbuf])

            def rope_evac(c_plain, c_sw, st, dst, dbuf):
                ka_ = fgroup(c_plain, st)
                kb_ = fgroup(c_sw, st)
                j = rot2()
                rc = (st - 1) * 512
                T.op(dve, lambda: nc.vector.tensor_tensor(out=r1[j], in0=PS[:, ka_, :],
                                                          in1=ropeC[:, rc:rc + 512], op=ALU.mult),
                     reads=[psb[ka_], B_rope], writes=[B_r1[j]])
                T.op(dve, lambda: nc.vector.tensor_tensor(out=r2[j], in0=PS[:, kb_, :],
                                                          in1=ropeS[:, rc:rc + 512], op=ALU.mult),
                     reads=[psb[kb_], B_rope], writes=[B_r2[j]])
                T.op(pool, lambda: nc.gpsimd.tensor_tensor(out=dst, in0=r1[j], in1=r2[j], op=ALU.add),
                     reads=[B_r1[j], B_r2[j]], writes=[dbuf])

            def zgate(st):
                k = fgroup(384, st)
                j = rot2()
                T.op(act, lambda: nc.scalar.activation(out=tmpz[j], in_=PS[:, k, :], func=AF.Exp, scale=-1.0),
                     reads=[psb[k]], writes=[B_tz[j]])
                T.op(dve, lambda: nc.vector.tensor_copy(out=r2[j], in_=PS[:, k, :]),
                     reads=[psb[k]], writes=[B_r2[j]])
                T.op(act, lambda: nc.scalar.activation(out=tmpz[j], in_=tmpz[j], func=AF.Ln, bias=one_col),
                     reads=[B_tz[j], B_c2], writes=[B_tz[j]])
                T.op(act, lambda: nc.scalar.activation(out=tmpz[j], in_=tmpz[j], func=AF.Exp, scale=-1.0),
                     reads=[B_tz[j]], writes=[B_tz[j]])
                T.op(pool, lambda: nc.gpsimd.tensor_tensor(
                    out=tza[i][:, st * 512:(st + 1) * 512], in0=r2[j], in1=tmpz[j], op=ALU.mult),
                    reads=[B_tz[j], B_r2[j]], writes=[B_tza[i][st]])

            def ptile(tt):
                k = pool_bank()
                T.group(pe, [(lambda kt=kt: nc.tensor.matmul(
                    PS[:, k, 0:256], lhsT=hT[:, kt, tt * 128:(tt + 1) * 128], rhs=W[:, kt, 512:768],
                    start=(kt == 0), stop=(kt == 7))) for kt in range(8)],
                    reads=[B_WA[i], B_hT[0]], writes=[psb[k]])
                j = rot2()
                T.op(dve, lambda: nc.vector.tensor_copy(out=stage[j], in_=PS[:, k, 0:256]),
                     reads=[psb[k]], writes=[B_stage[j]])
                T.op(dve, lambda: nc.vector.tensor_copy(out=vS[i][:, tt, :], in_=PS[:, k, 128:256]),
                     reads=[psb[k]], writes=[B_vS[i][0]])
                T.dma(sp, [(nk_d[tt * 128:(tt + 1) * 128, h * 128:(h + 1) * 128], stage[j][:, 0:128]),
                           (nv_d[tt * 128:(tt + 1) * 128, h * 128:(h + 1) * 128], stage[j][:, 128:256])],
                      reads=[B_stage[j]])

            def vtiles(st):
                k = pool_bank()
                fns = []
                for t4 in range(4):
                    tt = st * 4 + t4
                    for kt in range(8):
                        fns.append(lambda kt=kt, tt=tt, t4=t4: nc.tensor.matmul(
                            PS[:, k, t4 * 128:(t4 + 1) * 128], lhsT=hT[:, kt, tt * 128:(tt + 1) * 128],
                            rhs=W[:, kt, 640:768], start=(kt == 0), stop=(kt == 7)))
                T.group(pe, fns, reads=[B_WA[i], B_hT[st]], writes=[psb[k]])
                T.op(dve, lambda: nc.vector.tensor_copy(
                    out=vS[i][:, st * 4:(st + 1) * 4, :], in_=PS[:, k, :].rearrange("p (t d) -> p t d", t=4)),
                    reads=[psb[k]], writes=[B_vS[i][st]])

            def seq(*fs):
                return lambda: [f() for f in fs]

            qrope = lambda st: (lambda: rope_evac(0, 128, st, qT[i][:, st * 512:(st + 1) * 512], B_qT[i][st]))
            krope = lambda st: (lambda: rope_evac(512, 256, st, kT[i][:, st * 512:(st + 1) * 512], B_kT[i][st]))
            ch.append(seq(lambda: plain(0, qT[i][:, 0:512], B_qT[i][0]), lambda: plain(512, kT[i][:, 0:512], B_kT[i][0])))
            ch.append(seq(lambda: ptile(0), lambda: ptile(1)))
            ch.append(seq(lambda: ptile(2), lambda: ptile(3)))
            ch.append(seq(lambda: zgate(0), qrope(1)))
            ch.append(qrope(2))
            ch.append(seq(lambda: zgate(1), krope(1)))
            ch.append(krope(2))
            ch.append(seq(lambda: zgate(2), krope(3)))
            ch.append(krope(4))
            ch.append(seq(lambda: vtiles(1), lambda: vtiles(2)))
            ch.append(seq(lambda: vtiles(3), lambda: vtiles(4)))
            return ch

        deferred = []
        act_heavy = []
        stepc = [0]

        def run_deferred(force=False):
            while deferred and (force or deferred[0][0] <= stepc[0]):
                deferred.pop(0)[1]()

        def attention_head(h, pend):
            i = h % 2
            groups = []
            for pb in range(2):
                keys = [(kT[i][:, pb * 256 + t * 128:pb * 256 + (t + 1) * 128], vS[i][:, pb * 2 + t, :],
                         (B_kT[i][0], B_vS[i][0])) for t in range(2)]
                groups.append((pb * 256, keys, 0, pb))
            skeys = [(ckT[:, h, t * 128:(t + 1) * 128], cvb[:, t, h * 128:(h + 1) * 128], (B_ctx,))
                     for t in range(2)]
            for t in range(16):
                st = 1 + t // 4
                skeys.append((kT[i][:, 512 + t * 128:512 + (t + 1) * 128], vS[i][:, 4 + t, :],
                              (B_kT[i][st], B_vS[i][st])))
            for qt in range(4):
                groups.append((512 + qt * 256, skeys, 1 + qt // 2, qt % 2))
            nsteps = sum(len(g_[1]) for g_ in groups)
            done = 0
            emitted = 0
            nq = 256
            nextq = q2carry.pop('q') if 'q' in q2carry else build_Q2(i, groups[0][0], groups[0][2])
            for gix, (qc0, keys, g, half) in enumerate(groups):
                n = len(keys)
                qu = nextq

                def S(kt):
                    kap, vap, kb = keys[kt]
                    T.op(pe, lambda: nc.tensor.matmul(PS[:, kt % 3, :], lhsT=kap, rhs=Q2[qu], start=True, stop=True),
                         reads=list(kb) + [B_Q2[qu]], writes=[psb[kt % 3]])

                S(0)
                if n > 1:
                    S(1)
                if gix + 1 < len(groups):
                    nextq = build_Q2(i, groups[gix + 1][0], groups[gix + 1][2])
                elif h + 1 < 8:
                    q2carry['q'] = build_Q2((h + 1) % 2, 0, 0)
                for kt in range(n):
                    if kt + 2 < n:
                        S(kt + 2)
                    a = kt % 3
                    e = kt % 3
                    T.op(act, lambda: nc.scalar.activation(
                        out=E[e], in_=PS[:, a, :].rearrange("p (m q) -> p m q", m=2), func=AF.Exp, scale=0.125),
                        reads=[psb[a]], writes=[B_E[e]])
                    kap, vap, kb = keys[kt]
                    fns = []
                    for m in range(2):
                        fns.append(lambda m=m: nc.tensor.matmul(
                            PS[:, 3, m * nq:(m + 1) * nq], lhsT=vap, rhs=E[e][:, m, :], start=(kt == 0 and m == 0),
                            stop=(kt == n - 1), skip_group_check=True))
                        fns.append(lambda m=m: nc.tensor.matmul(
                            PS[:, 4, m * nq:(m + 1) * nq], lhsT=ones_b, rhs=E[e][:, m, :], start=(kt == 0 and m == 0),
                            stop=(kt == n - 1), skip_group_check=True))
                    T.group(pe, fns, reads=[B_E[e]] + list(kb) + [B_c2], writes=[psb[3], psb[4]])
                    done += 1
                    stepc[0] += 1
                    run_deferred()
                    while emitted < len(pend) and emitted * nsteps < done * len(pend):
                        while act_heavy and act_heavy[0][0] <= stepc[0] and not any(d_[2] <= act_heavy[0][2] for d_ in deferred):
                            act_heavy.pop(0)[1]()
                        pend[emitted]()
                        emitted += 1
                    if not pend:
                        while act_heavy and act_heavy[0][0] <= stepc[0] and not any(d_[2] <= act_heavy[0][2] for d_ in deferred):
                            act_heavy.pop(0)[1]()
                gi = gcount[0]
                gcount[0] += 1
                gp = gi % 2
                bp = bcount[0] % 2
                if half == 1:
                    bcount[0] += 1
                while deferred and deferred[0][2] <= gi - 2:
                    deferred.pop(0)[1]()
                while act_heavy and act_heavy[0][2] <= gi - 2:
                    act_heavy.pop(0)[1]()
                while deferred and deferred[0][2] <= gi - 2:
                    deferred.pop(0)[1]()
                for d_ in [d_ for d_ in deferred if d_[2] == gi - 1 and len(d_) > 3 and d_[3] == "st1"]:
                    deferred.remove(d_)
                    d_[1]()
                o_all, v_all = o_all_[bp], v_all_[bp]
                B_oall, B_vall = B_oall_[bp], B_vall_[bp]
                osq, dd, B_osq, B_dd = osq_[gp], dd_[gp], B_osq_[gp], B_dd_[gp]
                oa = o_all[:, half * nq:(half + 1) * nq]
                va = v_all[:, half * nq:(half + 1) * nq]
                T.op(dve, lambda: nc.vector.tensor_scalar(
                    out=dn, in0=PS[:, 4, :].rearrange("p (m q) -> p m q", m=2), scalar1=2.0 ** -12, scalar2=None,
                    op0=ALU.mult), reads=[psb[4]], writes=[B_dn])
                T.op(dve, lambda: nc.vector.tensor_tensor(out=oa, in0=PS[:, 3, 0:nq], in1=dn[:, 1, :], op=ALU.mult),
                     reads=[psb[3], B_dn], writes=[B_oall[half]])
                T.op(dve, lambda: nc.vector.tensor_tensor(out=o2, in0=PS[:, 3, nq:2 * nq], in1=dn[:, 0, :],
                                                          op=ALU.mult), reads=[psb[3], B_dn], writes=[B_o2])
                T.op(pool, lambda: nc.gpsimd.tensor_tensor(out=dd, in0=dn[:, 0, :], in1=dn[:, 1, :], op=ALU.mult),
                     reads=[B_dn], writes=[B_dd])
                if emitted < len(pend):
                    pend[emitted]()
                    emitted += 1

                def st1(oa=oa, half=half, B_oall=B_oall, dd=dd, B_dd=B_dd):
                    T.op(dve, lambda: nc.vector.scalar_tensor_tensor(
                        out=oa, in0=o2, scalar=neglam, in1=oa, op0=ALU.mult, op1=ALU.add),
                        reads=[B_o2, B_oall[half], B_lam], writes=[B_oall[half]])
                    T.op(pool, lambda: nc.gpsimd.tensor_tensor(out=dd, in0=dd, in1=dd, op=ALU.mult),
                         reads=[B_dd], writes=[B_dd])

                def st2(oa=oa, half=half, B_oall=B_oall, osq=osq, B_osq=B_osq):
                    T.op(dve, lambda: nc.vector.tensor_tensor(out=osq, in0=oa, in1=oa, op=ALU.mult),
                         reads=[B_oall[half]], writes=[B_osq])

                hold = {}

                def st3(osq=osq, B_osq=B_osq, hold=hold):
                    k = pool_bank()
                    hold["k"] = k
                    T.op(pe, lambda: nc.tensor.matmul(PS[:, k, 0:nq], lhsT=jmat, rhs=osq, start=True, stop=True),
                         reads=[B_osq, B_c2], writes=[psb[k]])

                def st4(va=va, half=half, B_vall=B_vall, dd=dd, B_dd=B_dd, hold=hold):
                    k = hold["k"]
                    T.op(dve, lambda: nc.vector.scalar_tensor_tensor(
                        out=va, in0=dd, scalar=LN_EPS * (2.0 ** 24), in1=PS[:, k, 0:nq], op0=ALU.mult, op1=ALU.add),
                        reads=[B_dd, psb[k]], writes=[B_vall[half]])

                deferred.append((stepc[0] + 1, st1, gi, "st1"))
                deferred.append((stepc[0] + 2, st2, gi))
                deferred.append((stepc[0] + 3, st3, gi))
                deferred.append((stepc[0] + 5, st4, gi))
                if half == 1:
                    def b1(v_all=v_all, B_vall=B_vall):
                        T.op(act, lambda: nc.scalar.activation(out=v_all, in_=v_all, func=AF.Ln),
                             reads=B_vall, writes=B_vall)
                        T.op(act, lambda: nc.scalar.activation(out=v_all, in_=v_all, func=AF.Exp, scale=-0.5),
                             reads=B_vall, writes=B_vall)

                    def b2(v_all=v_all, B_vall=B_vall, g=g, i=i):
                        c0 = g * 512
                        T.op(pool, lambda: nc.gpsimd.tensor_tensor(out=v_all, in0=v_all, in1=tza[i][:, c0:c0 + 512],
                                                                   op=ALU.mult),
                             reads=B_vall + [B_tza[i][g]], writes=B_vall)

                    def b3(v_all=v_all, B_vall=B_vall, o_all=o_all, B_oall=B_oall, g=g, h=h):
                        c0 = g * 512
                        T.op(dve, lambda: nc.vector.scalar_tensor_tensor(
                            out=oagT[:, h, c0:c0 + 512], in0=o_all, scalar=gsc, in1=v_all, op0=ALU.mult,
                            op1=ALU.mult), reads=B_oall + B_vall + [B_lam], writes=[B_oag[h][g]])

                    def b1x(b1=b1, b2=b2, b3=b3, gi=gi):
                        b1()
                        deferred.append((stepc[0] + 3, b2, gi))
                        deferred.append((stepc[0] + 5, b3, gi))
                    act_heavy.append((stepc[0] + 7, b1x, gi))
            while emitted < len(pend):
                pend[emitted]()
                emitted += 1

        ph3pre = Arena(BIG + 88 * KB, BIG + 88 * KB + 2 * 9216)
        WR = [sb(f"WR{i}", [128, 8, WR_COLS], BF16, ph3pre) for i in range(2)]
        B_WR = [T.b("WR", i) for i in range(2)]

        def load_WR(h, extra=()):
            v = wr_d[h].rearrange("(kt p) c -> p kt c", p=128)
            T.dma(pool, [(WR[h % 2], v)], writes=[B_WR[h % 2]] + list(extra))

        for c_ in mk_chunks(0):
            c_()
        emit_decay_tables()
        for h in range(8):
            if h == 7:
                load_WR(0, extra=[B_WA[0], B_WA[1]])
                load_WR(1, extra=[B_WA[0], B_WA[1]])
            pend = mk_chunks(h + 1) if h + 1 < 8 else []
            if h + 2 < 8:
                load_WA(h + 2)
            attention_head(h, pend)
        run_deferred(force=True)
        while act_heavy:
            act_heavy.pop(0)[1]()
        run_deferred(force=True)
        T.barrier()
        if STOP == 2:
            T.finish()
            return nc

    if True:
        ph = Arena(BIG + 88 * KB + 2 * 9216, TOP)
        qpl = [sb(f"qpl{i}", [64, NQ_TOK], BF16, ph) for i in range(2)]
        qdT = [sb(f"qdT{i}", [128, NQ_TOK], BF16, ph) for i in range(2)]
        krT = [sb(f"krT{i}", [64, NQ_TOK], BF16, ph) for i in range(2)]
        kd = [sb(f"kd{i}", [128, 20, 128], BF16, ph) for i in range(2)]
        vr = [sb(f"vr{i}", [128, 20, 128], BF16, ph) for i in range(2)]
        tzr = [sb(f"tzr{i}", [128, NQ_TOK], BF16, ph) for i in range(2)]
        tmpz = sb("tmpzr", [128, 512], F32, ph)
        KDh = [sb(f"KDh{i}", [128, 1, 128], F32, ph) for i in range(2)]
        B_KDh = [T.b("KDh", i) for i in range(2)]
        zc = sb("zcr", [128, 512], F32, ph)
        S_ = sb("stS", [128, 12, 128], F32, ph)
        stB = sb("stB", [128, 12, 128], BF16, ph)
        yc = sb("ycar", [128, 2, 128], F32, ph)
        fin = [sb(f"fin{i}", [128, 2, 128], F32, ph) for i in range(2)]
        innT = [sb(f"innT{i}", [128, 4, 128], BF16, ph) for i in range(3)]
        ob = [t_.rearrange("p c j -> p (c j)") for t_ in innT]
        sqb = ob
        rstd = [sb(f"rstd3_{i}", [128, 512], F32, ph) for i in range(3)]

        B_qpl = [[T.b("qpl", i, g) for g in range(3)] for i in range(2)]
        B_qd = [[T.b("qd", i, g) for g in range(3)] for i in range(2)]
        B_krT = [[T.b("krT", i, g) for g in range(3)] for i in range(2)]
        B_kd = [[T.b("kd", i, g) for g in range(5)] for i in range(2)]
        B_vr = [[T.b("vr", i, g) for g in range(5)] for i in range(2)]
        B_tzr = [[T.b("tzr", i, g) for g in range(3)] for i in range(2)]
        B_tz = T.b("tmpzr")
        B_zc = T.b("zcr")
        B_st = T.b("stS")
        B_stb = T.b("stB")
        B_fin = [T.b("fin", i) for i in range(2)]
        B_inn = [T.b("innT", i) for i in range(3)]
        B_ob = B_inn
        B_sqb = B_inn
        B_rstd = [T.b("rstd3", i) for i in range(3)]
        B_org = [[T.b("org", h, g) for g in range(3)] for h in range(8)]

        pb3 = [0]
        PB3 = [5, 6, 7]

        def pool_bank3():
            pb3[0] = (pb3[0] + 1) % 3
            return PB3[pb3[0]]

        def mk_chunks3(h):
            i = h % 2
            W = WR[i]

            def fgroup(c0, st, ncols=128):
                k = pool_bank3()
                T.group(pe, [(lambda kt=kt: nc.tensor.matmul(
                    PS[0:ncols, k, :], lhsT=W[:, kt, c0:c0 + ncols], rhs=hT[:, kt, st * 512:(st + 1) * 512],
                    start=(kt == 0), stop=(kt == 7))) for kt in range(8)],
                    reads=[B_WR[i], B_hT[st]], writes=[psb[k]])
                return k

            def qq(st):
                k = fgroup(0, st)
                T.op(dve, lambda: nc.vector.tensor_copy(out=qpl[i][:, st * 512:(st + 1) * 512], in_=PS[0:64, k, :]),
                     reads=[psb[k]], writes=[B_qpl[i][st]])
                T.op(dve, lambda: nc.vector.tensor_tensor(
                    out=qdT[i][:, st * 512:(st + 1) * 512].rearrange("p (c j) -> p c j", c=4),
                    in0=PS[:, k, :].rearrange("p (c j) -> p c j", c=4),
                    in1=QD[:, h:h + 1, :].to_broadcast([128, 4, 128]), op=ALU.mult),
                    reads=[psb[k], B_dec], writes=[B_qd[i][st]])

            def kr(st):
                k = fgroup(256, st, 64)
                T.op(act, lambda: nc.scalar.activation(out=krT[i][:, st * 512:(st + 1) * 512], in_=PS[0:64, k, :],
                                                       func=AF.Copy), reads=[psb[k]], writes=[B_krT[i][st]])

            def zr(st):
                k = fgroup(128, st)
                T.op(act, lambda: nc.scalar.activation(out=tmpz, in_=PS[:, k, :], func=AF.Exp, scale=-1.0),
                     reads=[psb[k]], writes=[B_tz])
                T.op(dve, lambda: nc.vector.tensor_copy(out=zc, in_=PS[:, k, :]), reads=[psb[k]], writes=[B_zc])
                T.op(act, lambda: nc.scalar.activation(out=tmpz, in_=tmpz, func=AF.Ln, bias=one_col),
                     reads=[B_tz, B_c2], writes=[B_tz])
                T.op(act, lambda: nc.scalar.activation(out=tmpz, in_=tmpz, func=AF.Exp, scale=-1.0),
                     reads=[B_tz], writes=[B_tz])
                T.op(pool, lambda: nc.gpsimd.tensor_tensor(out=tzr[i][:, st * 512:(st + 1) * 512], in0=zc, in1=tmpz,
                                                           op=ALU.mult),
                     reads=[B_tz, B_zc], writes=[B_tzr[i][st]])

            def tpair(tp):
                k = pool_bank3()
                st = tp // 2
                fns = []
                for t2 in range(2):
                    tt = tp * 2 + t2
                    for kt in range(8):
                        fns.append(lambda kt=kt, tt=tt, t2=t2: nc.tensor.matmul(
                            PS[:, k, t2 * 256:(t2 + 1) * 256], lhsT=hT[:, kt, tt * 128:(tt + 1) * 128],
                            rhs=W[:, kt, 320:576], start=(kt == 0), stop=(kt == 7)))
                T.group(pe, fns, reads=[B_WR[i], B_hT[st]], writes=[psb[k]])
                pv = PS[:, k, :].rearrange("p (t c) -> p t c", t=2)
                T.op(dve, lambda: nc.vector.tensor_tensor(
                    out=kd[i][:, tp * 2:tp * 2 + 2, :], in0=pv[:, :, 0:128],
                    in1=KDh[i].to_broadcast([128, 2, 128]), op=ALU.mult),
                    reads=[psb[k], B_KDh[i]], writes=[B_kd[i][st]])
                T.op(act, lambda: nc.scalar.activation(out=vr[i][:, tp * 2:tp * 2 + 2, :], in_=pv[:, :, 128:256],
                                                       func=AF.Copy), reads=[psb[k]], writes=[B_vr[i][st]])

            def seq(*fs):
                return lambda: [f() for f in fs]

            def mk_kdh():
                T.op(pool, lambda: nc.gpsimd.tensor_copy(out=KDh[i][:, 0, 0:64],
                                                         in_=kdcol[:, h:h + 1].to_broadcast([128, 64])),
                     reads=[B_dec], writes=[B_KDh[i]])
                T.op(pool, lambda: nc.gpsimd.tensor_copy(out=KDh[i][:, 0, 64:128],
                                                         in_=kdcol[:, 8 + h:9 + h].to_broadcast([128, 64])),
                     reads=[B_dec], writes=[B_KDh[i]])

            mk_kdh()
            ch = []
            fch = [seq(lambda st=st: zr(st), lambda st=st: qq(st), lambda st=st: kr(st)) for st in range(3)]
            ch.append(seq(lambda: tpair(9), lambda: tpair(8)))
            ch.append(fch[0])
            ch.append(seq(lambda: tpair(7), lambda: tpair(6)))
            ch.append(seq(lambda: tpair(0), lambda: tpair(1)))
            ch.append(fch[1])
            ch.append(seq(lambda: tpair(2), lambda: tpair(3)))
            ch.append(fch[2])
            ch.append(seq(lambda: tpair(4), lambda: tpair(5)))
            return ch

        def mk_ret(h):
            i = h % 2
            ops = []
            cdX = CDv[0:64, h:h + 1]
            cdY = CDv[64:128, h:h + 1]
            ubank = {}

            def Ugrp(g4, bank):
                def f():
                    T.group(pe, [(lambda t4=t4: nc.tensor.matmul(
                        PS[:, bank, t4 * 128:(t4 + 1) * 128], lhsT=kd[i][:, g4 * 4 + t4, :],
                        rhs=vr[i][:, g4 * 4 + t4, :], start=True, stop=True)) for t4 in range(4)],
                        reads=[B_kd[i][g4], B_vr[i][g4]], writes=[psb[bank]])
                for t4 in range(4):
                    ubank[g4 * 4 + t4] = (bank, t4)
                return f

            def U(tile, lo, hi):
                k, t4 = ubank[tile]
                return PS[lo:hi, k, t4 * 128:(t4 + 1) * 128], psb[k]

            def chain(dst, src, cd, tile, lo, hi, wbuf):
                def f():
                    u, ub = U(tile, lo, hi)
                    T.op(dve, lambda: nc.vector.scalar_tensor_tensor(out=dst, in0=src, scalar=cd, in1=u,
                                                                     op0=ALU.mult, op1=ALU.add),
                         reads=[ub, B_st, B_dec, B_const], writes=[wbuf])
                return f

            def ucopy(dst, tile, lo, hi):
                def f():
                    u, ub = U(tile, lo, hi)
                    T.op(dve, lambda: nc.vector.tensor_copy(out=dst, in_=u), reads=[ub], writes=[B_st])
                return f

            ops.append(Ugrp(4, 0))
            ops.append(Ugrp(3, 1))
            prev = s0[64:128, h, :]
            for n_, tile in enumerate(range(19, 11, -1)):
                dst = yc[64:128, n_ % 2, :] if tile > 12 else S_[64:128, 11, :]
                ops.append(chain(dst, prev, cdY, tile, 64, 128, B_st))
                prev = dst
            ops.append(Ugrp(0, 2))
            ops.append(Ugrp(1, 3))
            ops.append(Ugrp(2, 4))
            ops.append(lambda: T.op(pool, lambda: nc.gpsimd.memset(S_[0:64, 0:4:2, :], 0.0), writes=[B_st]))
            ops.append(lambda: T.op(pool, lambda: nc.gpsimd.memset(S_[64:128, 1:4:2, :], 0.0), writes=[B_st]))
            for pb in range(2):
                c0 = pb * 2
                ops.append(ucopy(S_[0:64, c0 + 1, :], c0, 0, 64))
                ops.append(ucopy(S_[64:128, c0, :], c0 + 1, 64, 128))
                ops.append(chain(fin[i][0:64, pb, :], S_[0:64, c0 + 1, :], cdX, c0 + 1, 0, 64, B_fin[i]))
                ops.append(chain(fin[i][64:128, pb, :], S_[64:128, c0, :], cdY, c0, 64, 128, B_fin[i]))
            ops.append(lambda: T.dma(sp, [(nst_d[pb, h], fin[i][:, pb, :]) for pb in range(2)], reads=[B_fin[i]]))
            ops.append(lambda: T.op(dve, lambda: nc.vector.tensor_copy(out=S_[0:64, 4, :], in_=s0[0:64, h, :]),
                                    reads=[B_const], writes=[B_st]))
            for c in range(7):
                ops.append(chain(S_[0:64, 5 + c, :], S_[0:64, 4 + c, :], cdX, 4 + c, 0, 64, B_st))
                cy = 7 - c
                ops.append(chain(S_[64:128, 4 + cy - 1, :], S_[64:128, 4 + cy, :], cdY, 4 + cy, 64, 128, B_st))
            ops.append(lambda: T.op(act, lambda: nc.scalar.activation(out=stB, in_=S_, func=AF.Copy),
                                    reads=[B_st], writes=[B_stb]))
            n_early = len(ops)
            A_b = [0, 1, 2]
            O_b = [3, 4, 0]
            C_b = [1, 2, 3]
            V_b = [4, 0, 4]
            for g in range(3):
                ops.append(lambda g=g: T.group(pe, [(lambda c4=c4: nc.tensor.matmul(
                    PS[:, A_b[g], c4 * 128:(c4 + 1) * 128],
                    lhsT=krT[i][:, (g * 4 + c4) * 128:(g * 4 + c4 + 1) * 128],
                    rhs=qpl[i][:, (g * 4 + c4) * 128:(g * 4 + c4 + 1) * 128], start=True, stop=True))
                    for c4 in range(4)], reads=[B_krT[i][g], B_qpl[i][g]], writes=[psb[A_b[g]]]))
            for g in range(3):
                ops.append(lambda g=g: T.op(dve, lambda: nc.vector.tensor_tensor(
                    out=innT[g], in0=PS[:, A_b[g], :].rearrange("p (c j) -> p c j", c=4),
                    in1=DC[:, h:h + 1, :].to_broadcast([128, 4, 128]), op=ALU.mult),
                    reads=[psb[A_b[g]], B_dec], writes=[B_inn[g]]))
            for g in range(3):
                def og(g=g):
                    fns = []
                    for c4 in range(4):
                        s_ = g * 4 + c4
                        fns.append(lambda c4=c4, s_=s_: nc.tensor.matmul(
                            PS[:, O_b[g], c4 * 128:(c4 + 1) * 128], lhsT=vr[i][:, s_, :], rhs=innT[g][:, c4, :],
                            start=True, stop=False))
                        fns.append(lambda c4=c4, s_=s_: nc.tensor.matmul(
                            PS[:, O_b[g], c4 * 128:(c4 + 1) * 128], lhsT=stB[:, s_, :],
                            rhs=qdT[i][:, s_ * 128:(s_ + 1) * 128], start=False, stop=True))
                    T.group(pe, fns, reads=[B_vr[i][g], B_inn[g], B_stb, B_qd[i][g]], writes=[psb[O_b[g]]])
                ops.append(og)
            for g in range(3):
                ops.append(lambda g=g: T.op(act, lambda: nc.scalar.activation(out=ob[g], in_=PS[:, O_b[g], :],
                                                                              func=AF.Copy),
                                            reads=[psb[O_b[g]]], writes=[B_ob[g]]))
            for g in range(3):
                ops.append(lambda g=g: T.op(pe, lambda: nc.tensor.matmul(PS[:, C_b[g], :], lhsT=cmat, rhs=ob[g],
                                                                         start=True, stop=True),
                                            reads=[B_ob[g], B_c2], writes=[psb[C_b[g]]]))
            for g in range(3):
                ops.append(lambda g=g: T.op(act, lambda: nc.scalar.activation(out=sqb[g], in_=PS[:, C_b[g], :],
                                                                              func=AF.Square),
                                            reads=[psb[C_b[g]]], writes=[B_sqb[g]]))
            for g in range(3):
                ops.append(lambda g=g: T.op(pe, lambda: nc.tensor.matmul(PS[:, V_b[g], :], lhsT=jmat, rhs=sqb[g],
                                                                         start=True, stop=True),
                                            reads=[B_sqb[g], B_c2], writes=[psb[V_b[g]]]))
                ops.append(lambda g=g: T.op(dve, lambda: nc.vector.tensor_scalar(
                    out=rstd[g], in0=PS[:, V_b[g], :], scalar1=LN_EPS, scalar2=None, op0=ALU.add),
                    reads=[psb[V_b[g]]], writes=[B_rstd[g]]))
            for g in range(3):
                ops.append(lambda g=g: T.op(act, lambda: nc.scalar.activation(out=rstd[g], in_=rstd[g], func=AF.Ln),
                                            reads=[B_rstd[g]], writes=[B_rstd[g]]))
            for g in range(3):
                ops.append(lambda g=g: T.op(act, lambda: nc.scalar.activation(out=rstd[g], in_=rstd[g], func=AF.Exp,
                                                                              scale=-0.5),
                                            reads=[B_rstd[g]], writes=[B_rstd[g]]))
            for g in range(3):
                ops.append(lambda g=g: T.op(pool, lambda: nc.gpsimd.tensor_tensor(
                    out=rstd[g], in0=rstd[g], in1=tzr[i][:, g * 512:(g + 1) * 512], op=ALU.mult),
                    reads=[B_rstd[g], B_tzr[i][g]], writes=[B_rstd[g]]))
            for g in range(3):
                ops.append(lambda g=g: T.op(dve, lambda: nc.vector.scalar_tensor_tensor(
                    out=orgT[:, h, g * 512:(g + 1) * 512], in0=PS[:, C_b[g], :], scalar=gngT[:, h:h + 1],
                    in1=rstd[g], op0=ALU.mult, op1=ALU.mult),
                    reads=[psb[C_b[g]], B_rstd[g], B_const], writes=[B_org[h][g]]))
            return ops, n_early

        ph4pre = Arena(BIG + 88 * KB, BIG + 88 * KB + 2 * 9216)
        WG = [sb(f"WG{i}", [128, 8, 512], BF16, ph4pre) for i in range(2)]
        B_WG = [T.b("WG", i) for i in range(2)]

        def load_WG(j, extra=()):
            v = wg_d[j].rearrange("(kt p) c -> p kt c", p=128)
            T.dma(pool, [(WG[j % 2], v)], writes=[B_WG[j % 2]] + list(extra))

        for c_ in mk_chunks3(0):
            c_()
        for h in range(8):
            ret, n_early = mk_ret(h)
            if h + 2 < 8:
                load_WR(h + 2)
            if h == 7:
                load_WG(0, extra=[B_WR[0], B_WR[1]])
                load_WG(1, extra=[B_WR[0], B_WR[1]])
            chs = mk_chunks3(h + 1) if h + 1 < 8 else []
            nr, ncs = len(ret), len(chs)
            ri = 0
            n_e_ch = 2
            for ci, c_ in enumerate(chs):
                c_()
                if ci < n_e_ch:
                    tgt = (ci + 1) * n_early // n_e_ch
                else:
                    tgt = n_early + (ci + 1 - n_e_ch) * (nr - n_early) // (ncs - n_e_ch)
                while ri < tgt:
                    ret[ri]()
                    ri += 1
            while ri < nr:
                ret[ri]()
                ri += 1
        T.barrier()
        if STOP == 3:
            T.finish()
            return nc

    if True:
        ph = Arena(BIG + 112 * KB, TOP)
        ph4b = Arena(BIG, BIG + 88 * KB)
        WO = sb("WO", [128, 8, D], BF16, ph)
        mT = sb("mT", [128, 8, NQ_TOK], BF16, ph)
        lnrow = sb("lnrow", [128, 2, D], F32, ph)
        ta = [sb(f"ta{i}", [128, 512], F32, ph) for i in range(2)]
        tr_ = [sb(f"tr{i}", [128, 512], F32, ph) for i in range(2)]
        XT = [sb(f"XT4_{i}", [128, D], F32, ph4b) for i in range(4)]
        Z = [sb(f"Z{i}", [128, D], F32, ph4b) for i in range(4)]
        Y = [sb(f"Y{i}", [128, D], F32, ph4b) for i in range(4)]
        st6 = [sb(f"st6b_{i}", [128, 2, 6], F32, ph) for i in range(4)]
        mv = [sb(f"mvb{i}", [128, 2], F32, ph) for i in range(4)]
        rs = [sb(f"rsb{i}", [128, 1], F32, ph) for i in range(4)]
        B_WO = T.b("WO")
        B_mT = [[T.b("mT", j, t) for t in range(3)] for j in range(8)]
        B_ta = [T.b("ta", i) for i in range(2)]
        B_tr = [T.b("tr", i) for i in range(2)]
        B_x4 = [T.b("XT4", i) for i in range(4)]
        B_z = [T.b("Z", i) for i in range(4)]
        B_y = [T.b("Y", i) for i in range(4)]
        B_s4 = [T.b("s4", i) for i in range(4)]

        wov = wout_d.rearrange("(kt p) c -> p kt c", p=128)
        T.dma(pool, [(WO, wov)], writes=[B_WO])
        B_ln = T.b("lnrow")
        T.dma(sp, [(lnrow, lnrow_d)], writes=[B_ln])
        rr = [0]
        for j in range(8):
            i = j % 2
            W = WG[i]
            if j >= 1 and j + 1 < 8:
                load_WG(j + 1)
            for t in range(3):
                def grp(c0, rhs_fn, rbufs):
                    k = next_bank()
                    T.group(pe, [(lambda kt=kt, k=k: nc.tensor.matmul(
                        PS[:, k, :], lhsT=W[:, kt, c0:c0 + 128], rhs=rhs_fn(kt), start=(kt == 0), stop=(kt == 7)))
                        for kt in range(8)], reads=[B_WG[i]] + rbufs, writes=[psb[k]])
                    return k
                hsl = lambda kt, t=t: hT[:, kt, t * 512:(t + 1) * 512]
                kga = grp(0, hsl, [B_hT[t]])
                kgr = grp(128, hsl, [B_hT[t]])
                kpa = grp(256, lambda kt, t=t: oagT[:, kt, t * 512:(t + 1) * 512], [B_oag[hh][t] for hh in range(8)])
                kpr = grp(384, lambda kt, t=t: orgT[:, kt, t * 512:(t + 1) * 512], [B_org[hh][t] for hh in range(8)])
                rr[0] ^= 1
                u = rr[0]
                T.op(act, lambda u=u, kga=kga, j=j: nc.scalar.activation(
                    out=ta[u], in_=PS[:, kga, :], func=AF.Tanh, scale=0.5, bias=bgTh[:, j:j + 1]),
                    reads=[psb[kga], B_lam], writes=[B_ta[u]])
                T.op(act, lambda u=u, kgr=kgr, j=j: nc.scalar.activation(
                    out=tr_[u], in_=PS[:, kgr, :], func=AF.Tanh, scale=0.5, bias=bgTh[:, 8 + j:9 + j]),
                    reads=[psb[kgr], B_lam], writes=[B_tr[u]])
                T.op(dve, lambda u=u, kpa=kpa: nc.vector.scalar_tensor_tensor(
                    out=ta[u], in0=ta[u], scalar=1.0, in1=PS[:, kpa, :], op0=ALU.add, op1=ALU.mult),
                    reads=[B_ta[u], psb[kpa]], writes=[B_ta[u]])
                T.op(dve, lambda u=u, kpr=kpr: nc.vector.scalar_tensor_tensor(
                    out=tr_[u], in0=tr_[u], scalar=1.0, in1=PS[:, kpr, :], op0=ALU.add, op1=ALU.mult),
                    reads=[B_tr[u], psb[kpr]], writes=[B_tr[u]])
                T.op(pool, lambda u=u, j=j, t=t: nc.gpsimd.tensor_tensor(
                    out=mT[:, j, t * 512:(t + 1) * 512], in0=ta[u], in1=tr_[u], op=ALU.add),
                    reads=[B_ta[u], B_tr[u]], writes=[B_mT[j][t]])
        T.barrier()
        def f_A(tt):
            u = tt % 4
            ci = 0 if tt < 4 else 1
            t = tt // 4
            kk = []
            for hf in range(2):
                k = next_bank()
                kk.append(k)
                T.group(pe, [(lambda kt=kt, k=k, hf=hf: nc.tensor.matmul(
                    PS[:, k, :], lhsT=mT[:, kt, tt * 128:(tt + 1) * 128], rhs=WO[:, kt, hf * 512:(hf + 1) * 512],
                    start=(kt == 0), stop=(kt == 7))) for kt in range(8)],
                    reads=[B_WO] + [B_mT[jj][t] for jj in range(8)], writes=[psb[k]])
            for hf in range(2):
                T.op(dve, lambda hf=hf, k=kk[hf]: nc.vector.tensor_tensor(
                    out=Z[u][:, hf * 512:(hf + 1) * 512], in0=PS[:, k, :],
                    in1=gate_bc[:, ci, hf * 512:(hf + 1) * 512], op=ALU.mult),
                    reads=[psb[kk[hf]], B_gate], writes=[B_z[u]])
            T.op(dve, lambda: nc.vector.scalar_tensor_tensor(
                out=Z[u], in0=XT[u], scalar=ALPHA, in1=Z[u], op0=ALU.mult, op1=ALU.add),
                reads=[B_x4[u], B_z[u]], writes=[B_z[u]])
            for c2 in range(2):
                T.op(dve, lambda c2=c2: nc.vector.bn_stats(out=st6[u][:, c2, :],
                                                           in_=Z[u][:, c2 * 512:(c2 + 1) * 512]),
                     reads=[B_z[u]], writes=[B_s4[u]])
            T.op(dve, lambda: nc.vector.bn_aggr(out=mv[u], in_=st6[u]), reads=[B_s4[u]], writes=[B_s4[u]])
            T.op(dve, lambda: nc.vector.tensor_scalar(out=rs[u], in0=mv[u][:, 1:2], scalar1=LN_EPS,
                                                      scalar2=None, op0=ALU.add),
                 reads=[B_s4[u]], writes=[B_s4[u]])
            T.op(act, lambda: nc.scalar.activation(out=rs[u], in_=rs[u], func=AF.Sqrt),
                 reads=[B_s4[u]], writes=[B_s4[u]])

        def f_B(tt):
            u = tt % 4
            T.op(dve, lambda: nc.vector.reciprocal(out=rs[u], in_=rs[u]), reads=[B_s4[u]], writes=[B_s4[u]])
            T.op(dve, lambda: nc.vector.scalar_tensor_tensor(
                out=mv[u][:, 1:2], in0=mv[u][:, 0:1], scalar=-1.0, in1=rs[u], op0=ALU.mult, op1=ALU.mult),
                reads=[B_s4[u]], writes=[B_s4[u]])
            T.op(act, lambda: nc.scalar.activation(out=Y[u], in_=Z[u], func=AF.Identity, scale=rs[u],
                                                   bias=mv[u][:, 1:2]),
                 reads=[B_z[u], B_s4[u]], writes=[B_y[u]])
            T.op(pool, lambda: nc.gpsimd.tensor_tensor(out=Y[u], in0=Y[u], in1=lnrow[:, 0, :], op=ALU.mult),
                 reads=[B_y[u], B_ln], writes=[B_y[u]])

        def f_C(tt):
            u = tt % 4
            T.op(dve, lambda: nc.vector.tensor_tensor(out=Y[u], in0=Y[u], in1=lnrow[:, 1, :], op=ALU.add),
                 reads=[B_y[u], B_ln], writes=[B_y[u]])
            if tt + 4 < 12:
                T.dma(sp, [(XT[u], xs[(tt + 4) * 128:(tt + 5) * 128, :])], writes=[B_x4[u]])
            T.dma(sp, [(y_d[tt * 128:(tt + 1) * 128, :], Y[u])], reads=[B_y[u]])

        for t_ in range(4):
            T.dma(sp, [(XT[t_], xs[t_ * 128:(t_ + 1) * 128, :])], writes=[B_x4[t_]])
        for i_ in range(12 + 2):
            if i_ < 12:
                f_A(i_)
            if 0 <= i_ - 1 < 12:
                f_B(i_ - 1)
            if 0 <= i_ - 2 < 12:
                f_C(i_ - 2)
        T.finish()
    es.close()
    return nc


def _consts():
    p = np.arange(128)
    i = np.arange(128)
    c = {}
    c["ident"] = np.eye(128, dtype=np.float32)
    c["cmat"] = (np.eye(128) - 1.0 / 128.0).astype(np.float32)
    iq = np.zeros((128, 128), np.float32)
    iq[0:64, :] = (i + 1)[None, :]
    iq[64:128, :] = (128 - i)[None, :]
    c["iq"] = iq
    c["jp"] = np.stack([127 - p, p], axis=1).astype(np.float32)
    diff = (i[None, :] - p[:, None]).astype(np.float32)
    c["diff"] = diff
    c["mx8"] = (0.125 * (diff >= 0)).astype(np.float32)
    c["my8"] = (0.125 * (diff <= 0)).astype(np.float32)
    sel2 = np.zeros((2, 2, 128), np.float32)
    sel2[0, 0, :] = 1.0
    sel2[1, 1, :] = 1.0
    c["sel2"] = sel2
    return c


def _rope_tables(pos):
    pos = np.asarray(pos)
    r = (pos // 64).astype(np.float32)
    col = (pos % 64).astype(np.float32)
    inv = (np.float32(10000.0) ** (-np.arange(16, dtype=np.float32) / np.float32(16))).astype(np.float32)
    ang = np.concatenate([r[:, None] * inv[None, :], col[:, None] * inv[None, :]], axis=1).astype(np.float32)
    cos = np.cos(ang).astype(np.float32)
    sin = np.sin(ang).astype(np.float32)
    pidx = (np.arange(128) % 64) // 2
    sgn = np.where(np.arange(128) % 2 == 1, 1.0, -1.0).astype(np.float32)
    C = cos[:, pidx].T.copy()
    S = (sin[:, pidx] * sgn[None, :]).T.copy()
    return np.ascontiguousarray(C, np.float32), np.ascontiguousarray(S, np.float32)


_PROGRAM = None


def kernel(x_prompt, x_sample, cache_attn_k, cache_attn_v, state_ret_fwd, state_ret_bwd,
           c, c_ctx, w_mod, b_mod, w_in, lam_params, subln_g, ret_decay, ret_gn_g,
           w_pa, w_pr, w_gate, b_gate, w_out, ln_g, ln_b):
    global _PROGRAM
    f = lambda a: np.ascontiguousarray(np.asarray(a), dtype=np.float32)
    x_prompt, x_sample = f(x_prompt), f(x_sample)
    cache_attn_k, cache_attn_v = f(cache_attn_k), f(cache_attn_v)
    state_ret_fwd, state_ret_bwd = f(state_ret_fwd), f(state_ret_bwd)
    c, c_ctx = f(c), f(c_ctx)
    w_in0 = f(w_in)[0]
    w_gate0, w_pa0, w_pr0 = f(w_gate)[0], f(w_pa)[0], f(w_pr)[0]
    ret_decay0 = f(ret_decay)[0]

    sw = np.arange(128) ^ 1
    wa = np.empty((8, D, WA_COLS), np.float32)
    wr = np.empty((8, D, WR_COLS), np.float32)
    wg = np.empty((8, D, 512), np.float32)
    for h in range(8):
        qa = w_in0[:, h * 128:(h + 1) * 128]
        ka = w_in0[:, 1024 + h * 128:1024 + (h + 1) * 128]
        va = w_in0[:, 2048 + h * 128:2048 + (h + 1) * 128]
        za = w_in0[:, 3072 + h * 128:3072 + (h + 1) * 128]
        qr = w_in0[:, 4096 + h * 64:4096 + (h + 1) * 64]
        kr = w_in0[:, 4608 + h * 64:4608 + (h + 1) * 64]
        vr = w_in0[:, 5120 + h * 128:5120 + (h + 1) * 128]
        zr = w_in0[:, 6144 + h * 128:6144 + (h + 1) * 128]
        wa[h] = np.concatenate([qa, qa[:, sw], ka[:, sw], za, ka, va], axis=1)
        wr[h] = np.concatenate([qr, qr, zr, kr, kr, kr, vr], axis=1)
        wg[h] = np.concatenate([w_gate0[:, h * 128:(h + 1) * 128], w_gate0[:, 1024 + h * 128:1024 + (h + 1) * 128],
                                w_pa0[:, h * 128:(h + 1) * 128], w_pr0[:, h * 128:(h + 1) * 128]], axis=1)
    consts = _consts()
    shared = {
        "w_mod": f(w_mod)[0], "bmod2": np.ascontiguousarray(np.broadcast_to(f(b_mod)[0][None, :], (2, 3 * D))),
        "wa": wa, "wr": wr, "wg": wg, "w_out": f(w_out)[0],
        "bgT": np.ascontiguousarray(f(b_gate)[0].reshape(16, 128).T),
        "lnrow": np.ascontiguousarray(np.broadcast_to(np.stack([f(ln_g)[0], f(ln_b)[0]])[None], (128, 2, D))),
        "lamp": np.ascontiguousarray(np.broadcast_to(f(lam_params)[0].reshape(1, 256), (128, 256))),
        "sublnT": np.ascontiguousarray(f(subln_g)[0].reshape(128, 1)),
        "gngT": np.ascontiguousarray(f(ret_gn_g)[0].reshape(8, 128).T),
    }
    shared.update(consts)

    in_maps = []
    for core in range(NCORES):
        b, half = core // 2, core % 2
        rev = half == 1
        o = (lambda a: a[::-1]) if rev else (lambda a: a)
        p0, p1 = o(x_prompt[2 * core]), o(x_prompt[2 * core + 1])
        mine = o(x_sample[b, half * 1024:(half + 1) * 1024])
        other = o(x_sample[b, (1 - half) * 1024:(2 - half) * 1024])
        pos = np.concatenate([o(np.arange(half * 1024, (half + 1) * 1024)),
                              o(np.arange((1 - half) * 1024, (2 - half) * 1024))])
        rc, rs_ = _rope_tables(pos)
        sX = state_ret_fwd[b, 0] if not rev else state_ret_bwd[b, 0]
        sY = state_ret_bwd[b, 0] if not rev else state_ret_fwd[b, 0]
        dX = ret_decay0[0] if not rev else ret_decay0[1]
        dY = ret_decay0[1] if not rev else ret_decay0[0]
        m = dict(shared)
        m["xs"] = np.ascontiguousarray(np.concatenate([p0, p1, mine, other], axis=0))
        m["ck"] = np.ascontiguousarray(cache_attn_k[b, 0].reshape(256, D))
        m["cv"] = np.ascontiguousarray(cache_attn_v[b, 0].reshape(256, D))
        m["s0"] = np.ascontiguousarray(np.concatenate([sX.transpose(1, 0, 2), sY.transpose(1, 0, 2)], axis=0))
        m["condT"] = np.ascontiguousarray(np.stack([c_ctx.reshape(8, 128).T, c[b].reshape(8, 128).T], axis=2))
        m["rdsel"] = np.ascontiguousarray(np.concatenate([np.broadcast_to(dX[None], (64, 8)),
                                                           np.broadcast_to(dY[None], (64, 8))], axis=0))
        m["rdrow"] = np.ascontiguousarray(np.broadcast_to(np.concatenate([dX, dY])[None], (128, 16)))
        m["ropeC"] = rc
        m["ropeS"] = rs_
        in_maps.append(m)

    if _PROGRAM is None:
        import os as _os
        _PROGRAM = build_program(int(_os.environ.get('MK_STOP', '99')))
    res = run_bass_kernel_spmd(_PROGRAM, in_maps, core_ids=list(range(NCORES)))

    y_prompt = np.empty((16, 256, D), np.float32)
    y_sample = np.empty((4, 2048, D), np.float32)
    new_k = np.empty((16, 1, 256, 8, 128), np.float32)
    new_v = np.empty((16, 1, 256, 8, 128), np.float32)
    new_f = np.empty((16, 1, 8, 64, 128), np.float32)
    new_b = np.empty((16, 1, 8, 64, 128), np.float32)
    for core in range(NCORES):
        r = res.results[core]
        b, half = core // 2, core % 2
        rev = half == 1
        o = (lambda a: a[::-1]) if rev else (lambda a: a)
        y = r["y"]
        for pb in range(2):
            y_prompt[2 * core + pb] = o(y[pb * 256:(pb + 1) * 256])
            new_k[2 * core + pb, 0] = o(r["nk"][pb * 256:(pb + 1) * 256]).reshape(256, 8, 128)
            new_v[2 * core + pb, 0] = o(r["nv"][pb * 256:(pb + 1) * 256]).reshape(256, 8, 128)
            X = r["nst"][pb][:, 0:64, :]
            Yv = r["nst"][pb][:, 64:128, :]
            new_f[2 * core + pb, 0] = Yv if rev else X
            new_b[2 * core + pb, 0] = X if rev else Yv
        y_sample[b, half * 1024:(half + 1) * 1024] = o(y[512:1536])
    return (y_prompt, y_sample, new_k, new_v, new_f, new_b)
```
